# Optimizing a Trainium2 kernel written in Bass

```python
import math
import jax, jax.numpy as jnp
from jax import lax
import numpy as np

D_MODEL = 1024
BATCH = 16
SEQ = 2048
DEPTH = 1

CHUNK = 64
MIX_WIDTH = D_MODEL
SB_WIDTH = MIX_WIDTH // 2
SB_HEAD_DIM = 64
SB_HEADS = SB_WIDTH // SB_HEAD_DIM
SSM_WIDTH = MIX_WIDTH - SB_WIDTH
SSM_GROUP = 16
SSM_GROUPS = SSM_WIDTH // SSM_GROUP
SSM_STATE = 64
D_FF = 4 * D_MODEL
QBLOCK = 128
EPS = 1e-6
DT_MIN = 1e-3
DT_MAX = 1e-1

kernel_name = "hybrid_stickbreaking_s5_block"


def rmsnorm(x, g):
    xf = x.astype(jnp.float32)
    y = xf * lax.rsqrt(jnp.mean(xf * xf, axis=-1, keepdims=True) + EPS)
    return y * g.astype(jnp.float32)


def stick_breaking_attention(q, k, v):
    L = q.shape[2]
    scale = 1.0 / math.sqrt(q.shape[-1])
    outs = []
    for i in range(L // QBLOCK):
        q0 = i * QBLOCK
        kend = q0 + QBLOCK
        qb = q[:, :, q0:kend]
        kb = k[:, :, :kend]
        vb = v[:, :, :kend]
        z = jnp.einsum('bhqd,bhkd->bhqk', qb, kb) * scale
        t_idx = q0 + jnp.arange(QBLOCK)[:, None]
        s_idx = jnp.arange(kend)[None, :]
        mask = s_idx < t_idx
        log_one_minus = jnp.where(mask, -jax.nn.softplus(z), 0.0)
        tail = lax.cumsum(log_one_minus, axis=3, reverse=True) - log_one_minus
        log_a = jax.nn.log_sigmoid(z) + tail
        a = jnp.where(mask, jnp.exp(log_a), 0.0)
        outs.append(jnp.einsum('bhqk,bhkd->bhqd', a, vb))
    return jnp.concatenate(outs, axis=2)


def s5_glu(u, lam_re, lam_im, log_dt, b_re, b_im, c_re, c_im, d_skip, w_glu, b_glu):
    Bsz, L, _ = u.shape
    ug = u.reshape(Bsz, L, SSM_GROUPS, SSM_GROUP)
    lam = lax.complex(lam_re.astype(jnp.float32), lam_im.astype(jnp.float32))
    dt = jnp.exp(log_dt.astype(jnp.float32))[:, None]
    lam_bar = jnp.exp(lam * dt)
    b_mat = lax.complex(b_re.astype(jnp.float32), b_im.astype(jnp.float32))
    c_mat = lax.complex(c_re.astype(jnp.float32), c_im.astype(jnp.float32))
    b_bar = ((lam_bar - 1.0) / lam)[:, :, None] * b_mat
    bu = jnp.einsum('blgh,gph->blgp', ug.astype(jnp.complex64), b_bar)
    lam_seq = jnp.broadcast_to(lam_bar[None, None], (1, L, SSM_GROUPS, SSM_STATE))

    def combine(e_i, e_j):
        a_i, s_i = e_i
        a_j, s_j = e_j
        return a_j * a_i, a_j * s_i + s_j

    _, states = lax.associative_scan(combine, (lam_seq, bu), axis=1)
    y = jnp.einsum('blgp,ghp->blgh', states, c_mat).real + d_skip.astype(jnp.float32)[None, None] * ug
    y = jax.nn.gelu(y.reshape(Bsz, L, SSM_WIDTH))
    gate = jax.nn.sigmoid(y @ w_glu.astype(jnp.float32) + b_glu.astype(jnp.float32))
    return y * gate


def setup_inputs(seed: int = 0) -> dict:
    key = jax.random.key(seed)
    ks = jax.random.split(key, 24)
    f32 = jnp.float32
    G, P, H = SSM_GROUPS, SSM_STATE, SSM_GROUP
    x = jax.random.normal(ks[0], (BATCH, SEQ, D_MODEL), f32)
    norm1_g = 1.0 + 0.02 * jax.random.normal(ks[1], (D_MODEL,), f32)
    w_in = jax.random.normal(ks[2], (D_MODEL, 3 * SB_WIDTH + SSM_WIDTH), f32) * D_MODEL ** -0.5
    q_norm_g = 1.0 + 0.02 * jax.random.normal(ks[3], (SB_HEAD_DIM,), f32)
    k_norm_g = 1.0 + 0.02 * jax.random.normal(ks[4], (SB_HEAD_DIM,), f32)
    ssm_lambda_re = -0.5 + 0.01 * jax.random.normal(ks[5], (G, P), f32)
    ssm_lambda_im = math.pi * jnp.broadcast_to(jnp.arange(P, dtype=f32)[None], (G, P)) \
        + 0.01 * jax.random.normal(ks[6], (G, P), f32)
    ssm_log_dt = jax.random.uniform(ks[7], (G,), f32, math.log(DT_MIN), math.log(DT_MAX))
    ssm_b_re = jax.random.normal(ks[8], (G, P, H), f32) * (2.0 * H) ** -0.5
    ssm_b_im = jax.random.normal(ks[9], (G, P, H), f32) * (2.0 * H) ** -0.5
    ssm_c_re = jax.random.normal(ks[10], (G, H, P), f32) * (2.0 * P) ** -0.5
    ssm_c_im = jax.random.normal(ks[11], (G, H, P), f32) * (2.0 * P) ** -0.5
    ssm_d = jax.random.normal(ks[12], (G, H), f32)
    w_glu = jax.random.normal(ks[13], (SSM_WIDTH, SSM_WIDTH), f32) * SSM_WIDTH ** -0.5
    b_glu = 0.01 * jax.random.normal(ks[14], (SSM_WIDTH,), f32)
    attn_out_g = 1.0 + 0.02 * jax.random.normal(ks[15], (SB_WIDTH,), f32)
    ssm_out_g = 1.0 + 0.02 * jax.random.normal(ks[16], (SSM_WIDTH,), f32)
    w_out = jax.random.normal(ks[17], (MIX_WIDTH, D_MODEL), f32) * MIX_WIDTH ** -0.5
    norm2_g = 1.0 + 0.02 * jax.random.normal(ks[18], (D_MODEL,), f32)
    w_mlp_in = jax.random.normal(ks[19], (D_MODEL, D_FF), f32) * D_MODEL ** -0.5
    w_mlp_out = jax.random.normal(ks[20], (D_FF, D_MODEL), f32) * D_FF ** -0.5
    return {"x": x, "norm1_g": norm1_g, "w_in": w_in, "q_norm_g": q_norm_g, "k_norm_g": k_norm_g,
            "ssm_lambda_re": ssm_lambda_re, "ssm_lambda_im": ssm_lambda_im, "ssm_log_dt": ssm_log_dt,
            "ssm_b_re": ssm_b_re, "ssm_b_im": ssm_b_im, "ssm_c_re": ssm_c_re, "ssm_c_im": ssm_c_im,
            "ssm_d": ssm_d, "w_glu": w_glu, "b_glu": b_glu, "attn_out_g": attn_out_g,
            "ssm_out_g": ssm_out_g, "w_out": w_out, "norm2_g": norm2_g,
            "w_mlp_in": w_mlp_in, "w_mlp_out": w_mlp_out}


def reference(x, norm1_g, w_in, q_norm_g, k_norm_g, ssm_lambda_re, ssm_lambda_im, ssm_log_dt,
              ssm_b_re, ssm_b_im, ssm_c_re, ssm_c_im, ssm_d, w_glu, b_glu, attn_out_g,
              ssm_out_g, w_out, norm2_g, w_mlp_in, w_mlp_out):
    Bsz, L, _ = x.shape
    h = x.astype(jnp.float32)
    for _layer in range(DEPTH):
        xn = rmsnorm(h, norm1_g)
        proj = xn @ w_in.astype(jnp.float32)
        q, k, v, u = jnp.split(proj, [SB_WIDTH, 2 * SB_WIDTH, 3 * SB_WIDTH], axis=-1)

        def heads(t):
            return t.reshape(Bsz, L, SB_HEADS, SB_HEAD_DIM).transpose(0, 2, 1, 3)

        qh = rmsnorm(heads(q), q_norm_g)
        kh = rmsnorm(heads(k), k_norm_g)
        vh = heads(v)
        sb = stick_breaking_attention(qh, kh, vh)
        sb = sb.transpose(0, 2, 1, 3).reshape(Bsz, L, SB_WIDTH)

        ssm = s5_glu(u, ssm_lambda_re, ssm_lambda_im, ssm_log_dt, ssm_b_re, ssm_b_im,
                     ssm_c_re, ssm_c_im, ssm_d, w_glu, b_glu)

        mixed = jnp.concatenate([rmsnorm(sb, attn_out_g), rmsnorm(ssm, ssm_out_g)], axis=-1)
        h = h + mixed @ w_out.astype(jnp.float32)

        hn = rmsnorm(h, norm2_g)
        a = jnp.square(jax.nn.relu(hn @ w_mlp_in.astype(jnp.float32)))
        h = h + a @ w_mlp_out.astype(jnp.float32)
    return h.astype(x.dtype)
```

```python
import math
import numpy as np
from contextlib import ExitStack
import concourse.bass as bass
import concourse.mybir as mybir
from concourse.bass_utils import run_bass_kernel_spmd

F32 = mybir.dt.float32
BF16 = mybir.dt.bfloat16
I32 = mybir.dt.int32
AF = mybir.ActivationFunctionType
ALU = mybir.AluOpType
EPS = 1e-6
TWO_PI = 2.0 * math.pi


class Buf:
    __slots__ = ("name", "w", "r", "dsem", "dcnt")

    def __init__(self, name):
        self.name = name
        self.w = None
        self.r = []
        self.dsem = None
        self.dcnt = 0


class Eng:
    def __init__(self, name, sem, same_sync=True):
        self.name = name
        self.sem = sem
        self.cnt = 0
        self.seen = {}
        self.same_sync = same_sync


class FW:
    def __init__(self, nc, stack):
        self.H = {"pe": nc.tensor, "act": nc.scalar, "dve": nc.vector, "pool": nc.gpsimd, "sp": nc.sync}
        self.nc = nc
        self.stack = stack
        self.E = {}
        for n, ss in (("pe", False), ("act", True), ("dve", True), ("pool", True), ("sp", True)):
            sem = stack.enter_context(nc.semaphore("s_" + n))
            self.E[n] = Eng(n, sem, ss)
        self.nbuf = 0
        self.dma_last = {}

    def buf(self, name=None):
        self.nbuf += 1
        return Buf(name or f"b{self.nbuf}")

    def _waits(self, e, reads, writes):
        need = {}

        def add(ev):
            if ev is None:
                return
            sem, val = ev
            if (not e.same_sync) and sem is e.sem:
                return
            k = id(sem)
            if e.seen.get(k, 0) >= val:
                return
            if k not in need or need[k][1] < val:
                need[k] = (sem, val)

        for b in reads:
            add(b.w)
        for b in writes:
            add(b.w)
            for ev in b.r:
                add(ev)
        out = list(need.values())
        for sem, val in out:
            e.seen[id(sem)] = val
        return out

    def _do(self, e, waits, fn, inc):
        h = self.H[e.name]
        for sem, val in waits:
            h.wait_ge(sem, val)
        if fn is not None:
            fn(h).then_inc(inc[0], inc[1])

    def op(self, eng, fn, reads=(), writes=()):
        e = self.E[eng]
        waits = self._waits(e, reads, writes)
        e.cnt += 1
        ev = (e.sem, e.cnt)
        self._do(e, waits, fn, (e.sem, 1))
        for b in reads:
            b.r.append(ev)
        for b in writes:
            b.w = ev
            b.r = []
        return ev

    def dma(self, eng, fn, reads=(), writes=(), track=None):
        e = self.E[eng]
        waits = self._waits(e, reads, writes)
        tb = track or (writes[0] if writes else reads[0])
        if tb.dsem is None:
            tb.dsem = self.stack.enter_context(self.nc.semaphore("d_" + tb.name))
        tb.dcnt += 16
        ev = (tb.dsem, tb.dcnt)
        self.dma_last[id(tb.dsem)] = ev
        self._do(e, waits, fn, (tb.dsem, 16))
        for b in reads:
            b.r.append(ev)
        for b in writes:
            b.w = ev
            b.r = []
        return ev

    def barrier(self):
        evs = [(x.sem, x.cnt) for x in self.E.values() if x.cnt > 0] + list(self.dma_last.values())
        for e in self.E.values():
            ws = []
            for sem, val in evs:
                if sem is e.sem and not e.same_sync:
                    continue
                if e.seen.get(id(sem), 0) < val:
                    ws.append((sem, val))
                    e.seen[id(sem)] = val
            if ws:
                self._do(e, ws, None, None)


def build_nc(L, NB, dbg=False):
    NT = L // 128
    C = L // 8
    CS = min(128, C)
    NSP = C // CS
    SPT = CS * 8
    NBK = max(1, L // 512)
    BW = min(512, L)
    nc = bass.Bass("TRN2", target_bir_lowering=False)

    def din(name, shape):
        return nc.dram_tensor(name, list(shape), F32, kind="ExternalInput").ap()

    x_d = din("x", [NB, L, 1024])
    out_d = nc.dram_tensor("out", [NB, L, 1024], F32, kind="ExternalOutput").ap()
    norm1_d = din("norm1_g", [1024]); w_in_d = din("w_in", [1024, 2048])
    qg_d = din("q_norm_g", [64]); kg_d = din("k_norm_g", [64])
    lre_d = din("ssm_lambda_re", [32, 64]); lim_d = din("ssm_lambda_im", [32, 64]); ldt_d = din("ssm_log_dt", [32])
    bre_d = din("ssm_b_re", [32, 64, 16]); bim_d = din("ssm_b_im", [32, 64, 16])
    cre_d = din("ssm_c_re", [32, 16, 64]); cim_d = din("ssm_c_im", [32, 16, 64])
    sd_d = din("ssm_d", [32, 16]); wglu_d = din("w_glu", [512, 512]); bglu_d = din("b_glu", [512])
    gao_d = din("attn_out_g", [512]); gso_d = din("ssm_out_g", [512]); wout_d = din("w_out", [1024, 1024])
    norm2_d = din("norm2_g", [1024]); w1_d = din("w_mlp_in", [1024, 4096]); w2_d = din("w_mlp_out", [4096, 1024])
    cst_d = din("consts", [3, 128, 128])
    iota_d = din("iota", [128, 256])
    dbg_outs = {}

    with ExitStack() as gst:
        fw = FW(nc, gst)

        def V(fn, r=(), w=()): return fw.op("dve", fn, r, w)
        def A(fn, r=(), w=()): return fw.op("act", fn, r, w)
        def G(fn, r=(), w=()): return fw.op("pool", fn, r, w)
        def T(fn, r=(), w=()): return fw.op("pe", fn, r, w)
        def D(fn, r=(), w=(), track=None): return fw.dma("sp", fn, r, w, track)

        uniq = [0]

        def sbt(st, name, shape, dt=F32):
            uniq[0] += 1
            name = f"{name}_{uniq[0]}"
            return st.enter_context(nc.sbuf_tensor(name, list(shape), dt)), fw.buf(name)

        def dump(name, ap, b, shape):
            if not dbg:
                return
            d = nc.dram_tensor("dbg_" + name, list(shape), ap.dtype, kind="ExternalOutput").ap()
            bo = fw.buf("dbgo_" + name)
            D(lambda h: h.dma_start(out=d, in_=ap), [b], [bo])
            dbg_outs[name] = bo

        BoutBlk = [fw.buf(f"o{i}") for i in range(NB * NT)]
        PF = []
        for i in range(5):
            t = gst.enter_context(nc.psum_tensor(f"pf{i}", [128, 512], F32)); PF.append((t, fw.buf(f"pf{i}")))
        PH = []
        for i in range(3):
            t = gst.enter_context(nc.psum_tensor(f"ph{i}", [128, 1024], BF16)); PH.append((t, fw.buf(f"ph{i}")))

        cstf, Bcstf = sbt(gst, "cstf", [128, 3, 128])
        D(lambda h: h.dma_start(out=cstf[:], in_=cst_d.rearrange("k p f -> p k f")), [], [Bcstf])
        identf = cstf[:, 0, :]
        cstb, Bcstb = sbt(gst, "cstb", [128, 3, 128], BF16)
        V(lambda h: h.tensor_copy(cstb[:], cstf[:]), [Bcstf], [Bcstb])
        identb = cstb[:, 0, :]; maskb = cstb[:, 1, :]; bd64 = cstb[:, 2, :]
        ones_f, Bones = sbt(gst, "ones_f", [128, 1])
        G(lambda h: h.memset(ones_f[:], 1.0), [], [Bones])
        onecol, Bonecol = sbt(gst, "onecol", [128, 1], BF16)
        G(lambda h: h.memset(onecol[:], 1.0), [], [Bonecol])
        epsT, Beps = sbt(gst, "epsT", [128, 1])
        G(lambda h: h.memset(epsT[:], EPS), [], [Beps])
        vecs, Bvecs = sbt(gst, "vecs", [128, 32])
        with nc.allow_non_contiguous_dma("tiny param vectors"):
            D(lambda h: h.dma_start(out=vecs[:, 0:8], in_=norm1_d.rearrange("(c p) -> p c", p=128)), [], [Bvecs])
            D(lambda h: h.dma_start(out=vecs[:, 8:16], in_=norm2_d.rearrange("(c p) -> p c", p=128)), [], [Bvecs])
            D(lambda h: h.dma_start(out=vecs[:, 16:20], in_=gao_d.rearrange("(c p) -> p c", p=128)), [], [Bvecs])
            D(lambda h: h.dma_start(out=vecs[:, 20:24], in_=gso_d.rearrange("(c p) -> p c", p=128)), [], [Bvecs])
            D(lambda h: h.dma_start(out=vecs[:, 24:28], in_=bglu_d.rearrange("(c p) -> p c", p=128)), [], [Bvecs])
            for hh in range(2):
                D(lambda h, hh=hh: h.dma_start(out=vecs[hh * 64:(hh + 1) * 64, 28:29], in_=qg_d.rearrange("(p o) -> p o", o=1)), [], [Bvecs])
                D(lambda h, hh=hh: h.dma_start(out=vecs[hh * 64:(hh + 1) * 64, 29:30], in_=kg_d.rearrange("(p o) -> p o", o=1)), [], [Bvecs])

        gqk, Bgqk = sbt(gst, "gqk", [128, 2, 64])
        D(lambda h: h.dma_start(out=gqk[:, 0, :], in_=qg_d.partition_broadcast(128)), [], [Bgqk])
        D(lambda h: h.dma_start(out=gqk[:, 1, :], in_=kg_d.partition_broadcast(128)), [], [Bgqk])

        def rstd_from_ss(ss_ap, Bss, out_ap, Bout, inv_n, tmp_ap, Btmp):
            A(lambda h: h.activation(tmp_ap, ss_ap, AF.Sqrt, bias=epsT[:], scale=inv_n), [Bss, Beps], [Btmp])
            V(lambda h: h.reciprocal(out_ap, tmp_ap), [Btmp], [Bout])

        with ExitStack() as p1:
            rho, Brho = sbt(p1, "rho", [128, 16])
            CSPEC = [("toep", [128, 32 * 128], BF16), ("wbr", [128, 32 * 128], BF16), ("wbi", [128, 32 * 128], BF16),
                     ("ceir", [128, 16 * 128], BF16), ("ceii", [128, 16 * 128], BF16), ("cosT", [128, 16 * C], F32), ("sinT", [128, 16 * C], F32)]
            scr = {n: (nc.dram_tensor("scr_" + n, shp, dt_, kind="Internal").ap(), fw.buf("scr_" + n)) for n, shp, dt_ in CSPEC}

            def alloc_consts(st):
                t = {n: sbt(st, n, shp, dt_) for n, shp, dt_ in CSPEC}
                return t
            wglu, Bwglu = sbt(p1, "wglu", [128, 4, 512], BF16)

            with ExitStack() as bs:
                _ct = alloc_consts(bs)
                toep = _ct["toep"][0][:].rearrange("p (g f) -> p g f", g=32); Btoep = _ct["toep"][1]
                wbr = _ct["wbr"][0][:].rearrange("p (g f) -> p g f", g=32); Bwbr = _ct["wbr"][1]
                wbi = _ct["wbi"][0][:].rearrange("p (g f) -> p g f", g=32); Bwbi = _ct["wbi"][1]
                ceir = _ct["ceir"][0][:].rearrange("p (g j o) -> p g j o", g=16, j=8); Bceir = _ct["ceir"][1]
                ceii = _ct["ceii"][0][:].rearrange("p (g j o) -> p g j o", g=16, j=8); Bceii = _ct["ceii"][1]
                cosT = _ct["cosT"][0][:].rearrange("p (g c) -> p g c", g=16); BcosT = _ct["cosT"][1]
                sinT = _ct["sinT"][0][:].rearrange("p (g c) -> p g c", g=16); BsinT = _ct["sinT"][1]
                def t32(name, shape):
                    return sbt(bs, name, shape)
                LR, BLR = t32("LR", [128, 16]); LI, BLI = t32("LI", [128, 16]); LDT, BLDT = t32("LDT", [128, 16])
                BR, BBR = t32("BR", [128, 16, 16]); BI, BBI = t32("BI", [128, 16, 16])
                CR, BCR = t32("CR", [128, 16, 16]); CI, BCI = t32("CI", [128, 16, 16]); CIn, BCIn = t32("CIn", [128, 16, 16])
                DCOL, BDCOL = t32("DCOL", [128, 32])
                cpad, Bcpad = t32("cpad", [128, 2, 4, 128])
                wg32, Bwg32 = t32("wg32", [128, 4, 512])
                with nc.allow_non_contiguous_dma("small transposed parameter loads"):
                    for hf in range(2):
                        qs = slice(hf * 64, hf * 64 + 64); gs = slice(hf * 16, hf * 16 + 16)
                        D(lambda h, qs=qs, gs=gs: h.dma_start(out=LR[qs, :], in_=lre_d[gs, :].rearrange("g p -> p g")), [], [BLR])
                        D(lambda h, qs=qs, gs=gs: h.dma_start(out=LI[qs, :], in_=lim_d[gs, :].rearrange("g p -> p g")), [], [BLI])
                        D(lambda h, qs=qs, gs=gs: h.dma_start(out=LDT[qs, :], in_=ldt_d[gs].partition_broadcast(64)), [], [BLDT])
                        D(lambda h, qs=qs, gs=gs: h.dma_start(out=BR[qs, :, :], in_=bre_d[gs].rearrange("g p h -> p g h")), [], [BBR])
                        D(lambda h, qs=qs, gs=gs: h.dma_start(out=BI[qs, :, :], in_=bim_d[gs].rearrange("g p h -> p g h")), [], [BBI])
                    for i in range(8):
                        D(lambda h, i=i: h.dma_start(out=DCOL[i * 16:(i + 1) * 16, :], in_=sd_d.rearrange("g h -> h g")), [], [BDCOL])
                G(lambda h: h.memset(cpad[:], 0.0), [], [Bcpad])
                for ri, cd in enumerate((cre_d, cim_d)):
                    for tg in range(4):
                        hf = tg // 2
                        D(lambda h, ri=ri, tg=tg, hf=hf, cd=cd: h.dma_start(
                            out=cpad[:, ri, tg, hf * 64:(hf + 1) * 64],
                            in_=cd[tg * 8:(tg + 1) * 8].rearrange("g o p -> (g o) p")), [], [Bcpad])
                D(lambda h: h.dma_start(out=wg32[:], in_=wglu_d.rearrange("(c p) n -> p c n", p=128)), [], [Bwg32])
                V(lambda h: h.tensor_copy(wglu[:], wg32[:]), [Bwg32], [Bwglu])
                for ri, (dst, Bdst) in enumerate(((CR, BCR), (CI, BCI))):
                    for tg in range(4):
                        hf = tg // 2
                        pf, Bpf = PF[tg % 2]
                        T(lambda h, ri=ri, tg=tg, pf=pf: h.matmul(pf[:, 0:128], cpad[:, ri, tg, :], identf, start=True, stop=True), [Bcpad, Bcstf], [Bpf])
                        qs = slice(hf * 64, hf * 64 + 64)
                        g0 = (tg % 2) * 8
                        V(lambda h, dst=dst, pf=pf, qs=qs, g0=g0: h.tensor_copy(
                            dst[qs, g0:g0 + 8, :], pf[qs, 0:128].rearrange("p (g o) -> p g o", o=16)), [Bpf], [Bdst])
                V(lambda h: h.tensor_single_scalar(CIn[:], CI[:], -1.0, ALU.mult), [BCI], [BCIn])

                cnt = [0]

                def newt(shape):
                    cnt[0] += 1
                    return t32(f"bt{cnt[0]}", shape)

                def tt(o, Bo, a, Ba, b, Bb, op):
                    V(lambda h: h.tensor_tensor(o, a, b, op), [Ba, Bb], [Bo])

                NBIG = 4 * C
                ft, Bft = t32("ft", [128, NBIG]); fti, Bfti = sbt(bs, "fti", [128, NBIG], I32); ftf, Bftf = t32("ftf", [128, NBIG])
                cu = [t32(f"cu{i}", [128, 16, 16]) for i in range(5)]

                def frac(dst, Bdst, src, Bsrc, add, n):
                    t, Bt = ft[:, 0:n], Bft; ti, Bti = fti[:, 0:n], Bfti; tf, Btf = ftf[:, 0:n], Bftf
                    V(lambda h: h.tensor_single_scalar(t, src, add, ALU.add), [Bsrc], [Bt])
                    V(lambda h: h.tensor_copy(ti, t), [Bt], [Bti])
                    V(lambda h: h.tensor_copy(tf, ti), [Bti], [Btf])
                    V(lambda h: h.tensor_tensor(dst, t, tf, ALU.subtract), [Bt, Btf], [Bdst])

                dt, Bdt = newt([128, 16]); are, Bare = newt([128, 16]); turns, Bturns = newt([128, 16]); mag, Bmag = newt([128, 16])
                A(lambda h: h.activation(dt[:], LDT[:], AF.Exp), [BLDT], [Bdt])
                tt(are[:], Bare, LR[:], BLR, dt[:], Bdt, ALU.mult)
                tt(turns[:], Bturns, LI[:], BLI, dt[:], Bdt, ALU.mult)
                V(lambda h: h.tensor_single_scalar(turns[:], turns[:], 1.0 / TWO_PI, ALU.mult), [Bturns], [Bturns])
                A(lambda h: h.activation(mag[:], are[:], AF.Exp), [Bare], [Bmag])
                A(lambda h: h.activation(rho[:], are[:], AF.Exp, scale=8.0), [Bare], [Brho])
                fs, Bfs = newt([128, 16]); fcn, Bfcn = newt([128, 16]); sA, BsA = newt([128, 16]); cA, BcA = newt([128, 16])
                frac(fs[:], Bfs, turns[:], Bturns, 0.0, 16)
                frac(fcn[:], Bfcn, turns[:], Bturns, 0.25, 16)
                A(lambda h: h.activation(sA[:], fs[:], AF.Sin, scale=TWO_PI), [Bfs], [BsA])
                A(lambda h: h.activation(cA[:], fcn[:], AF.Sin, scale=TWO_PI), [Bfcn], [BcA])
                lbr, Blbr = newt([128, 16]); lbi, Blbi = newt([128, 16])
                tt(lbr[:], Blbr, mag[:], Bmag, cA[:], BcA, ALU.mult)
                tt(lbi[:], Blbi, mag[:], Bmag, sA[:], BsA, ALU.mult)
                n2, Bn2 = newt([128, 16]); t1, Bt1 = newt([128, 16]); t2, Bt2 = newt([128, 16]); inv, Binv = newt([128, 16])
                nr, Bnr = newt([128, 16]); kr, Bkr = newt([128, 16]); ki, Bki = newt([128, 16])
                tt(t1[:], Bt1, LR[:], BLR, LR[:], BLR, ALU.mult)
                tt(t2[:], Bt2, LI[:], BLI, LI[:], BLI, ALU.mult)
                tt(n2[:], Bn2, t1[:], Bt1, t2[:], Bt2, ALU.add)
                V(lambda h: h.reciprocal(inv[:], n2[:]), [Bn2], [Binv])
                V(lambda h: h.tensor_single_scalar(nr[:], lbr[:], -1.0, ALU.add), [Blbr], [Bnr])
                tt(t1[:], Bt1, nr[:], Bnr, LR[:], BLR, ALU.mult)
                tt(t2[:], Bt2, lbi[:], Blbi, LI[:], BLI, ALU.mult)
                tt(kr[:], Bkr, t1[:], Bt1, t2[:], Bt2, ALU.add)
                tt(kr[:], Bkr, kr[:], Bkr, inv[:], Binv, ALU.mult)
                tt(t1[:], Bt1, lbi[:], Blbi, LR[:], BLR, ALU.mult)
                tt(t2[:], Bt2, nr[:], Bnr, LI[:], BLI, ALU.mult)
                tt(ki[:], Bki, t1[:], Bt1, t2[:], Bt2, ALU.subtract)
                tt(ki[:], Bki, ki[:], Bki, inv[:], Binv, ALU.mult)

                def bc(ap2):
                    return ap2.unsqueeze(2).to_broadcast([128, 16, 16])

                def cmul_b(outr, outi, Bor, Boi, sr, si, Bsr, Bsi, xr, xi, Bxr, Bxi, negate_i=False):
                    (u1, Bu1), (u2, Bu2), (u3, Bu3), (u4, Bu4), (u5, Bu5) = cu
                    V(lambda h: h.tensor_tensor(u1[:], xr, bc(sr), ALU.mult), [Bxr, Bsr], [Bu1])
                    V(lambda h: h.tensor_tensor(u2[:], xi, bc(si), ALU.mult), [Bxi, Bsi], [Bu2])
                    V(lambda h: h.tensor_tensor(outr, u1[:], u2[:], ALU.subtract), [Bu1, Bu2], [Bor])
                    V(lambda h: h.tensor_tensor(u3[:], xi, bc(sr), ALU.mult), [Bxi, Bsr], [Bu3])
                    V(lambda h: h.tensor_tensor(u4[:], xr, bc(si), ALU.mult), [Bxr, Bsi], [Bu4])
                    if negate_i:
                        V(lambda h: h.tensor_tensor(u5[:], u3[:], u4[:], ALU.add), [Bu3, Bu4], [Bu5])
                        V(lambda h: h.tensor_single_scalar(outi, u5[:], -1.0, ALU.mult), [Bu5], [Boi])
                    else:
                        V(lambda h: h.tensor_tensor(outi, u3[:], u4[:], ALU.add), [Bu3, Bu4], [Boi])

                BBr, BBBr = newt([128, 16, 16]); BBi, BBBi = newt([128, 16, 16])
                cmul_b(BBr[:], BBi[:], BBBr, BBBi, kr[:], ki[:], Bkr, Bki, BR[:], BI[:], BBR, BBI)
                PWr, BPWr = newt([128, 9, 16]); PWi, BPWi = newt([128, 9, 16])
                V(lambda h: h.memset(PWr[:, 0, :], 1.0), [], [BPWr]); V(lambda h: h.memset(PWi[:, 0, :], 0.0), [], [BPWi])
                for m in range(1, 9):
                    a1, Ba1 = newt([128, 16]); a2, Ba2 = newt([128, 16])
                    V(lambda h, m=m, a1=a1: h.tensor_tensor(a1[:], PWr[:, m - 1, :], lbr[:], ALU.mult), [BPWr, Blbr], [Ba1])
                    V(lambda h, m=m, a2=a2: h.tensor_tensor(a2[:], PWi[:, m - 1, :], lbi[:], ALU.mult), [BPWi, Blbi], [Ba2])
                    V(lambda h, m=m, a1=a1, a2=a2: h.tensor_tensor(PWr[:, m, :], a1[:], a2[:], ALU.subtract), [Ba1, Ba2], [BPWr])
                    a3, Ba3 = newt([128, 16]); a4, Ba4 = newt([128, 16])
                    V(lambda h, m=m, a3=a3: h.tensor_tensor(a3[:], PWr[:, m - 1, :], lbi[:], ALU.mult), [BPWr, Blbi], [Ba3])
                    V(lambda h, m=m, a4=a4: h.tensor_tensor(a4[:], PWi[:, m - 1, :], lbr[:], ALU.mult), [BPWi, Blbr], [Ba4])
                    V(lambda h, m=m, a3=a3, a4=a4: h.tensor_tensor(PWi[:, m, :], a3[:], a4[:], ALU.add), [Ba3, Ba4], [BPWi])
                BEr, BBEr = newt([128, 16, 15, 16]); BEi, BBEi = newt([128, 16, 15, 16])
                G(lambda h: h.memset(BEr[:], 0.0), [], [BBEr]); G(lambda h: h.memset(BEi[:], 0.0), [], [BBEi])
                for i in range(8):
                    cmul_b(BEr[:, :, i, :], BEi[:, :, i, :], BBEr, BBEi, PWr[:, 7 - i, :], PWi[:, 7 - i, :], BPWr, BPWi,
                           BBr[:], BBi[:], BBBr, BBBi)
                for j in range(8):
                    cmul_b(ceir[:, :, j, :], ceii[:, :, j, :], Bceir, Bceii, PWr[:, j + 1, :], PWi[:, j + 1, :], BPWr, BPWi,
                           CR[:], CI[:], BCR, BCI, negate_i=True)
                ph8, Bph8 = newt([128, 16]); t8, Bt8 = newt([128, 16])
                V(lambda h: h.tensor_single_scalar(t8[:], turns[:], 8.0, ALU.mult), [Bturns], [Bt8])
                frac(ph8[:], Bph8, t8[:], Bt8, 0.0, 16)
                iot, Biot = newt([128, C])
                D(lambda h: h.dma_start(out=iot[:], in_=iota_d[:, 0:C]), [], [Biot])
                TT, BTT = newt([128, 4, C]); FR, BFR = newt([128, 4 * C])
                for g4 in range(4):
                    for gl in range(4):
                        V(lambda h, gl=gl, g4=g4: h.tensor_scalar_mul(TT[:, gl, :], iot[:], ph8[:, g4 * 4 + gl:g4 * 4 + gl + 1]), [Biot, Bph8], [BTT])
                    frac(FR[:], BFR, TT[:].rearrange("p g c -> p (g c)"), BTT, 0.0, 4 * C)
                    A(lambda h, g4=g4: h.activation(sinT[:, g4 * 4:(g4 + 1) * 4, :].rearrange("p g c -> p (g c)"), FR[:], AF.Sin, scale=TWO_PI), [BFR], [BsinT])
                    frac(FR[:], BFR, TT[:].rearrange("p g c -> p (g c)"), BTT, 0.25, 4 * C)
                    A(lambda h, g4=g4: h.activation(cosT[:, g4 * 4:(g4 + 1) * 4, :].rearrange("p g c -> p (g c)"), FR[:], AF.Sin, scale=TWO_PI), [BFR], [BcosT])
                G(lambda h: h.memset(wbr, 0.0), [], [Bwbr]); G(lambda h: h.memset(wbi, 0.0), [], [Bwbi])
                for g in range(32):
                    hf = g // 16; gl = g % 16
                    qs = slice(hf * 64, hf * 64 + 64)
                    pf, Bpf = PF[g % 2]
                    BEr_f = BEr[:].rearrange("p g b h -> p g (b h)"); BEi_f = BEi[:].rearrange("p g b h -> p g (b h)")
                    if g == 0:
                        BE16r, BBE16r = sbt(bs, "BE16r", [128, 16, 240], BF16); BE16i, BBE16i = sbt(bs, "BE16i", [128, 16, 240], BF16)
                        C16r, BC16r = sbt(bs, "C16r", [128, 16, 16], BF16); C16n, BC16n = sbt(bs, "C16n", [128, 16, 16], BF16)
                        V(lambda h: h.tensor_copy(BE16r[:], BEr_f), [BBEr], [BBE16r]); V(lambda h: h.tensor_copy(BE16i[:], BEi_f), [BBEi], [BBE16i])
                        V(lambda h: h.tensor_copy(C16r[:], CR[:]), [BCR], [BC16r]); V(lambda h: h.tensor_copy(C16n[:], CIn[:]), [BCIn], [BC16n])
                    for j in range(8):
                        off = (7 - j) * 16
                        T(lambda h, pf=pf, j=j, qs=qs, gl=gl, off=off: h.matmul(pf[:, j * 16:(j + 1) * 16], BE16r[qs, gl, off:off + 128], C16r[qs, gl, :], start=True, stop=False),
                          [BBE16r, BC16r], [Bpf])
                        T(lambda h, pf=pf, j=j, qs=qs, gl=gl, off=off: h.matmul(pf[:, j * 16:(j + 1) * 16], BE16i[qs, gl, off:off + 128], C16n[qs, gl, :], start=False, stop=True),
                          [BBE16i, BC16n], [Bpf])
                    V(lambda h, pf=pf, g=g: h.scalar_tensor_tensor(toep[:, g, :], identf, DCOL[:, g:g + 1], pf[:, 0:128], ALU.mult, ALU.add),
                      [Bpf, Bcstf, BDCOL], [Btoep])
                    pg, Bpg = PF[2 + g % 2]
                    T(lambda h, pg=pg, qs=qs, gl=gl: h.matmul(pg[:, 0:64], BEr_f[qs, gl, 0:128], cstf[qs, 0, hf * 64:hf * 64 + 64], start=True, stop=True), [BBEr, Bcstf], [Bpg])
                    T(lambda h, pg=pg, qs=qs, gl=gl: h.matmul(pg[:, 64:128], BEi_f[qs, gl, 0:128], cstf[qs, 0, hf * 64:hf * 64 + 64], start=True, stop=True), [BBEi, Bcstf], [Bpg])
                    A(lambda h, pg=pg, g=g, hf=hf: h.copy(wbr[:, g, hf * 64:hf * 64 + 64], pg[:, 0:64]), [Bpg], [Bwbr])
                    A(lambda h, pg=pg, g=g, hf=hf: h.copy(wbi[:, g, hf * 64:hf * 64 + 64], pg[:, 64:128]), [Bpg], [Bwbi])
                dump("toep", toep, Btoep, [128, 32, 128]); dump("wbr", wbr, Bwbr, [128, 32, 128])
                dump("ceir", ceir, Bceir, [128, 16, 8, 16]); dump("cosT", cosT, BcosT, [128, 16, C])
                for n, shp, dt_ in CSPEC:
                    D(lambda h, n=n: h.dma_start(out=scr[n][0], in_=_ct[n][0][:]), [_ct[n][1]], [scr[n][1]])
                fw.barrier()

            SW = max(4 * L, NSP * 4096)
            Q1, BQ1 = sbt(p1, "Q1", [128, SW], BF16)
            Q2, BQ2 = sbt(p1, "Q2", [128, SW], BF16)
            Q3, BQ3 = sbt(p1, "Q3", [128, SW], BF16)
            S1, BS1 = sbt(p1, "S1", [128, SW], BF16)
            XSa, BXSa = sbt(p1, "XSa", [128, 4 * L], BF16)
            XSb, BXSb = sbt(p1, "XSb", [128, max(4 * L, 8192)], BF16)
            sbT = XSa[:].rearrange("p (t l) -> p t l", t=4); BsbT = BXSa
            xsTa = XSa[:].rearrange("p (c l) -> p c l", c=4); xsTb = XSb[:, 0:4 * L].rearrange("p (c l) -> p c l", c=4)
            BxsT2 = [BXSa, BXSb]
            wo = XSb[:, 0:8192].rearrange("p (c n) -> p c n", c=8); Bwo = BXSb

            def xsT(dc, sl):
                return (xsTa if dc < 4 else xsTb)[:, dc % 4, sl]
            stat, Bstat = sbt(p1, "stat", [128, 4, NT])
            rs, Brs = sbt(p1, "rs", [128, 4, NT])
            junk, Bjunk = sbt(p1, "junk", [128, 1024], BF16)

            for b in range(NB):
                qT = Q1[:, 0:4 * L].rearrange("p (t l) -> p t l", t=4)
                kTr = Q2[:, 0:4 * L].rearrange("p (t l) -> p t l", t=4)
                vrev = Q3[:, 0:4 * L].rearrange("p (n f) -> p n f", f=512)
                vTr = S1[:, 0:4 * L].rearrange("p (t l) -> p t l", t=4)
                u_tm = S1[:, 0:NSP * 4096].rearrange("p (s g f) -> p s g f", g=32, f=128)
                u_tmw = S1[:, 0:NSP * 4096].rearrange("p (s g j h) -> p s g j h", g=32, j=8, h=16)
                U2 = Q1[:, 0:32 * C].rearrange("p (g c) -> p g c", g=32)
                SPr = Q2[:, 0:16 * C].rearrange("p (g c) -> p g c", g=16)
                SPi = Q2[:, 16 * C:32 * C].rearrange("p (g c) -> p g c", g=16)
                G2 = S1[:, 0:32 * C].rearrange("p (g c) -> p g c", g=32)
                y_tm = Q1[:, 0:NSP * 4096].rearrange("p (s j f) -> p s j f", j=8, f=512)
                yT = Q2[:, 0:4 * L].rearrange("p (t l) -> p t l", t=4)
                ssmT = Q3[:, 0:4 * L].rearrange("p (t l) -> p t l", t=4)

                with ExitStack() as sa:
                    xb = [sbt(sa, f"xb{i}", [128, 1024]) for i in range(4)]
                    wsts = [sbt(sa, f"wst{k}", [128, 8, 128]) for k in range(2)]
                    wgrps = [sbt(sa, f"wgrp{k}", [128, 8, 512], BF16) for k in range(4)]
                    xss = [sbt(sa, f"xs{i}", [128, 1024], BF16) for i in range(2)]
                    sqfs = [sbt(sa, f"sqf{k}", [128, 512]) for k in range(2)]
                    kns = [sbt(sa, f"kn{k}", [128, 512]) for k in range(2)]
                    kn16s = [sbt(sa, f"kn16_{k}", [128, 512], BF16) for k in range(3)]
                    qkst, _ = sbt(sa, "qkst", [128, 3, 2 * NT * 8])
                    Bqk = [[fw.buf(f"qkst{b}_{i}_{r}") for r in range(3)] for i in range(2 * NT)]
                    def load_wgrp(c0):
                        wgrp, Bwgrp = wgrps[c0 // 512]
                        for hh in range(4):
                            wst, Bwst = wsts[hh % 2]
                            fw.dma("pool", lambda h, hh=hh, wst=wst: h.dma_start(out=wst[:], in_=w_in_d[:, c0 + hh * 128:c0 + (hh + 1) * 128].rearrange("(c p) n -> p c n", p=128)), [], [Bwst])
                            G(lambda h, hh=hh, wst=wst: h.tensor_tensor(wgrp[:, :, hh * 128:(hh + 1) * 128], wst[:], vecs[:, 0:8].unsqueeze(2).to_broadcast([128, 8, 128]), ALU.mult),
                              [Bwst, Bvecs], [Bwgrp])

                    G(lambda h: h.memset(stat[:], 0.0), [], [Bstat])
                    st1, _ = sbt(sa, "st1", [128, 3, NT])
                    Bst1 = [fw.buf(f"st1_{b}_{i}") for i in range(NT)]
                    G(lambda h: h.memset(st1[:], 0.0), [], Bst1)
                    for c0 in (0, 512, 1024, 1536):
                        load_wgrp(c0)
                    def a1_s1(n):
                        xt, Bxt = xb[n % 4]
                        xs, Bxs = xss[n % 2]
                        D(lambda h: h.dma_start(out=xt[:], in_=x_d[b, n * 128:(n + 1) * 128, :]), [], [Bxt])
                        A(lambda h: h.activation(junk[:], xt[:], AF.Square, accum_out=st1[:, 0, n:n + 1]), [Bxt], [Bjunk, Bst1[n]])
                        rstd_from_ss(st1[:, 0, n:n + 1], Bst1[n], st1[:, 1, n:n + 1], Bst1[n], 1.0 / 1024, st1[:, 2, n:n + 1], Bst1[n])
                        V(lambda h: h.tensor_scalar_mul(xs[:], xt[:], st1[:, 1, n:n + 1]), [Bxt, Bst1[n]], [Bxs])

                    def a1_s2(n):
                        xs, Bxs = xss[n % 2]
                        ph, Bph = PH[n % 2]
                        for dc in range(8):
                            T(lambda h, dc=dc: h.transpose(ph[:, dc * 128:(dc + 1) * 128], xs[:, dc * 128:(dc + 1) * 128], identb), [Bxs, Bcstb], [Bph])
                        A(lambda h: h.copy(xsTa[:, :, n * 128:(n + 1) * 128], ph[:, 0:512].rearrange("p (c t) -> p c t", c=4)), [Bph], [BXSa])
                        A(lambda h: h.copy(xsTb[:, :, n * 128:(n + 1) * 128], ph[:, 512:1024].rearrange("p (c t) -> p c t", c=4)), [Bph], [BXSb])

                    for n in range(NT):
                        a1_s1(n)
                        if n >= 1:
                            a1_s2(n - 1)
                    a1_s2(NT - 1)

                    qk_units = [(which, n) for which in range(2) for n in range(NT)]

                    def qk_s1(u):
                        which, n = qk_units[u]
                        wgrp, Bwgrp = wgrps[which]
                        pf, Bpf = PF[u % 3]
                        sqf, Bsqf = sqfs[u % 2]; kn, Bkn = kns[u % 2]; kn16, Bkn16 = kn16s[u % 3]
                        for dc in range(8):
                            T(lambda h, dc=dc: h.matmul(pf[:], xsT(dc, slice(n * 128, (n + 1) * 128)), wgrp[:, dc, :], start=(dc == 0), stop=(dc == 7)),
                              BxsT2 + [Bwgrp], [Bpf])
                        A(lambda h: h.activation(sqf[:], pf[:], AF.Square), [Bpf], [Bsqf])
                        c0 = (which * NT + n) * 8
                        Bq = Bqk[which * NT + n]
                        V(lambda h: h.tensor_reduce(qkst[:, 0, c0:c0 + 8], sqf[:].rearrange("p (a d) -> p a d", d=64), mybir.AxisListType.X, ALU.add), [Bsqf], [Bq[0]])
                        A(lambda h: h.activation(qkst[:, 1, c0:c0 + 8], qkst[:, 0, c0:c0 + 8], AF.Sqrt, bias=epsT[:], scale=1.0 / 64), [Bq[0], Beps], [Bq[1]])
                        V(lambda h: h.reciprocal(qkst[:, 2, c0:c0 + 8], qkst[:, 1, c0:c0 + 8]), [Bq[1]], [Bq[2]])
                        V(lambda h: h.tensor_tensor(kn16[:].rearrange("p (a d) -> p a d", d=64), pf[:].rearrange("p (a d) -> p a d", d=64),
                                                    qkst[:, 2, c0:c0 + 8].unsqueeze(2).to_broadcast([128, 8, 64]), ALU.mult), [Bpf, Bq[2]], [Bkn16])

                    def qk_s2(u):
                        which, n = qk_units[u]
                        kn16, Bkn16 = kn16s[u % 3]
                        ph, Bph = PH[u % 2]
                        for hp in range(4):
                            T(lambda h, hp=hp: h.transpose(ph[:, hp * 128:(hp + 1) * 128], kn16[:, hp * 128:(hp + 1) * 128], identb), [Bkn16, Bcstb], [Bph])
                        if which == 0:
                            A(lambda h: h.activation(qT[:, :, n * 128:(n + 1) * 128], ph[:, 0:512].rearrange("p (c t) -> p c t", c=4), AF.Copy, scale=vecs[:, 28:29]), [Bph, Bvecs], [BQ1])
                        else:
                            hi = L - n * 128 - 1
                            lo = L - (n + 1) * 128 - 1
                            A(lambda h: h.activation(kTr[:, :, hi:(lo if lo >= 0 else None):-1], ph[:, 0:512].rearrange("p (c t) -> p c t", c=4), AF.Copy, scale=vecs[:, 29:30]), [Bph, Bvecs], [BQ2])

                    for u in range(len(qk_units)):
                        qk_s1(u)
                        if u >= 2:
                            qk_s2(u - 2)
                    qk_s2(len(qk_units) - 2)
                    qk_s2(len(qk_units) - 1)
                    wgrp, Bwgrp = wgrps[2]
                    for hp in range(4):
                        for bk in range(NBK):
                            pf, Bpf = PF[(hp * NBK + bk) % 2]
                            for dc in range(8):
                                T(lambda h, pf=pf, dc=dc, hp=hp, bk=bk: h.matmul(pf[:, 0:BW], wgrp[:, dc, hp * 128:(hp + 1) * 128], xsT(dc, slice(bk * BW, (bk + 1) * BW)),
                                                                              start=(dc == 0), stop=(dc == 7)), [Bwgrp] + BxsT2, [Bpf])
                            lo = L - (bk + 1) * BW
                            stop = lo - 1 if lo > 0 else None
                            A(lambda h, pf=pf, hp=hp, lo=lo, stop=stop: h.copy(vTr[:, hp, lo + BW - 1:stop:-1], pf[:, 0:BW]), [Bpf], [BS1])
                    for sbk in range(NT):
                        ph, Bph = PH[sbk % 2]
                        for hp in range(4):
                            T(lambda h, ph=ph, hp=hp, sbk=sbk: h.transpose(ph[:, hp * 128:(hp + 1) * 128], vTr[:, hp, sbk * 128:(sbk + 1) * 128], identb), [BS1, Bcstb], [Bph])
                        V(lambda h, ph=ph, sbk=sbk: h.tensor_copy(vrev[:, sbk, :], ph[:, 0:512]), [Bph], [BQ3])
                    wgrp, Bwgrp = wgrps[3]
                    for sp in range(NSP):
                        for j in range(8):
                            pf, Bpf = PF[(sp * 8 + j) % 2]
                            for dc in range(8):
                                T(lambda h, pf=pf, dc=dc, sp=sp, j=j: h.matmul(pf[0:CS, :], xsT(dc, slice(sp * SPT + j, (sp + 1) * SPT, 8)), wgrp[:, dc, :],
                                                                            start=(dc == 0), stop=(dc == 7)), BxsT2 + [Bwgrp], [Bpf])
                            A(lambda h, pf=pf, sp=sp, j=j: h.copy(u_tmw[0:CS, sp, :, j, :], pf[0:CS, :].rearrange("p (g h) -> p g h", h=16)), [Bpf], [BS1])
                    dump(f"qT{b}", qT, BQ1, [128, 4, L]); dump(f"kTr{b}", kTr, BQ2, [128, 4, L]); dump(f"vrev{b}", vrev, BQ3, [128, NT, 512])
                    dump(f"utm{b}", S1[0:CS, 0:NSP * 4096], BS1, [CS, NSP * 4096])
                    fw.barrier()

                with ExitStack() as sj:
                    gams = [sbt(sj, f"gam{k}", [128, L]) for k in range(2)]
                    Pcs = [sbt(sj, f"Pc{k}", [128, L + 1]) for k in range(2)]
                    a16s = [sbt(sj, f"a16_{k}", [128, L], BF16) for k in range(3)]
                    aTs = [sbt(sj, f"aT_{k}", [128, NT, 128], BF16) for k in range(2)]
                    sbs, Bsbs = sbt(sj, "sbs", [128, 512], BF16)
                    wst2s = [sbt(sj, f"wst2_{k}", [128, 8, 128]) for k in range(2)]
                    for q4 in range(8):
                        wst2, Bwst2 = wst2s[q4 % 2]
                        D(lambda h, q4=q4: h.dma_start(out=wst2[:], in_=wout_d[:, q4 * 128:(q4 + 1) * 128].rearrange("(c p) n -> p c n", p=128)), [], [Bwst2])
                        G(lambda h, q4=q4: h.tensor_tensor(wo[:, :, q4 * 128:(q4 + 1) * 128], wst2[:], vecs[:, 16:24].unsqueeze(2).to_broadcast([128, 8, 128]), ALU.mult),
                          [Bwst2, Bvecs], [Bwo])
                    for Pc_, BPc_ in Pcs:
                        V(lambda h, Pc_=Pc_: h.memset(Pc_[:, 0:1], 1.0), [], [BPc_])
                    osb, Bosb = PF[4]
                    units = [(i, hd) for i in range(NT) for hd in range(8)]

                    def stage1(n):
                        i, hd = units[n]
                        a16, Ba16 = a16s[n % 3]
                        gam, Bgam = gams[n % 2]
                        Pc, BPc = Pcs[n % 2]
                        S = (i + 1) * 128
                        k0 = L - S
                        hp = hd // 2; hs = slice((hd % 2) * 64, (hd % 2) * 64 + 64)
                        npc = (S + 511) // 512
                        for pc in range(npc):
                            w = min(512, S - pc * 512)
                            pf, Bpf = PF[pc % 4]
                            if pc == 0:
                                T(lambda h, pf=pf: h.matmul(pf[:, 0:128], identb, maskb, start=True, stop=False), [Bcstb], [Bpf])
                                T(lambda h, pf=pf: h.matmul(pf[:, 0:128], qT[hs, hp, i * 128:(i + 1) * 128], kTr[hs, hp, k0:k0 + 128], start=False, stop=True),
                                  [BQ1, BQ2], [Bpf])
                                if w > 128:
                                    T(lambda h, pf=pf, w=w: h.matmul(pf[:, 128:w], qT[hs, hp, i * 128:(i + 1) * 128], kTr[hs, hp, k0 + 128:k0 + w], start=True, stop=True),
                                      [BQ1, BQ2], [Bpf])
                            else:
                                T(lambda h, pf=pf, w=w, pc=pc: h.matmul(pf[:, 0:w], qT[hs, hp, i * 128:(i + 1) * 128], kTr[hs, hp, k0 + pc * 512:k0 + pc * 512 + w], start=True, stop=True),
                                  [BQ1, BQ2], [Bpf])
                            A(lambda h, pf=pf, pc=pc, w=w: h.activation(gam[:, pc * 512:pc * 512 + w], pf[:, 0:w], AF.Sigmoid, scale=-0.125), [Bpf], [Bgam])
                        V(lambda h: h.tensor_tensor_scan(Pc[:, 1:S + 1], gam[:, 0:S], gam[:, 0:S], 1.0, ALU.mult, ALU.min), [Bgam], [BPc])
                        V(lambda h: h.tensor_tensor(a16[:, 0:S], Pc[:, 0:S], Pc[:, 1:S + 1], ALU.subtract), [BPc], [Ba16])

                    def stage2(n):
                        i, hd = units[n]
                        a16, Ba16 = a16s[n % 3]
                        aT, BaT = aTs[n % 2]
                        for b4 in range(0, i + 1, 4):
                            nb4 = min(4, i + 1 - b4)
                            ph, Bph = PH[(b4 // 4) % 2]
                            for q in range(nb4):
                                T(lambda h, ph=ph, q=q, b4=b4: h.transpose(ph[:, q * 128:(q + 1) * 128], a16[:, (b4 + q) * 128:(b4 + q + 1) * 128], identb), [Ba16, Bcstb], [Bph])
                            A(lambda h, ph=ph, b4=b4, nb4=nb4: h.copy(aT[:, b4:b4 + nb4, :], ph[:, 0:nb4 * 128].rearrange("p (q t) -> p q t", t=128)), [Bph], [BaT])
                        for blk in range(i + 1):
                            T(lambda h, blk=blk: h.matmul(osb[:, hd * 64:(hd + 1) * 64], aT[:, blk, :], vrev[:, NT - 1 - i + blk, hd * 64:(hd + 1) * 64],
                                                          start=(blk == 0), stop=(blk == i)), [BaT, BQ3], [Bosb])
                        if hd == 7:
                            A(lambda h: h.copy(sbs[:], osb[:]), [Bosb], [Bsbs])
                            A(lambda h: h.activation(junk[:, 0:512], osb[:], AF.Square, accum_out=stat[:, 1, i:i + 1]), [Bosb], [Bjunk, Bstat])
                            ph, Bph = PH[2]
                            for hp in range(4):
                                T(lambda h, ph=ph, hp=hp: h.transpose(ph[:, hp * 128:(hp + 1) * 128], sbs[:, hp * 128:(hp + 1) * 128], identb), [Bsbs, Bcstb], [Bph])
                            V(lambda h, ph=ph: h.tensor_copy(sbT[:, :, i * 128:(i + 1) * 128], ph[:, 0:512].rearrange("p (c t) -> p c t", c=4)), [Bph], [BsbT])

                    for n in range(len(units)):
                        stage1(n)
                        if n >= 2:
                            stage2(n - 2)
                    stage2(len(units) - 2)
                    stage2(len(units) - 1)
                    dump(f"sbT{b}", sbT, BsbT, [128, 4, L])
                    fw.barrier()

                with ExitStack() as ss_:
                    _ct = alloc_consts(ss_)
                    for n, shp, dt_ in CSPEC:
                        D(lambda h, n=n, _ct=_ct: h.dma_start(out=_ct[n][0][:], in_=scr[n][0]), [scr[n][1]], [_ct[n][1]])
                    toep = _ct["toep"][0][:].rearrange("p (g f) -> p g f", g=32); Btoep = _ct["toep"][1]
                    wbr = _ct["wbr"][0][:].rearrange("p (g f) -> p g f", g=32); Bwbr = _ct["wbr"][1]
                    wbi = _ct["wbi"][0][:].rearrange("p (g f) -> p g f", g=32); Bwbi = _ct["wbi"][1]
                    ceir = _ct["ceir"][0][:].rearrange("p (g j o) -> p g j o", g=16, j=8); Bceir = _ct["ceir"][1]
                    ceii = _ct["ceii"][0][:].rearrange("p (g j o) -> p g j o", g=16, j=8); Bceii = _ct["ceii"][1]
                    cosT = _ct["cosT"][0][:].rearrange("p (g c) -> p g c", g=16); BcosT = _ct["cosT"][1]
                    sinT = _ct["sinT"][0][:].rearrange("p (g c) -> p g c", g=16); BsinT = _ct["sinT"][1]
                    BU2 = [fw.buf(f"U2_{g}") for g in range(32)]
                    BG2 = [fw.buf(f"G2_{g}") for g in range(32)]
                    BSPr = [fw.buf(f"SPr_{g}") for g in range(16)]; BSPi = [fw.buf(f"SPi_{g}") for g in range(16)]
                    tset = [[sbt(ss_, f"st{k}_{q}", [128, C]) for q in range(10)] for k in range(2)]
                    gset = [[sbt(ss_, f"gt{k}_{q}", [128, C]) for q in range(3)] for k in range(2)]
                    gT, BgT = sbt(ss_, "gT", [128, 512], BF16); sq4 = [sbt(ss_, f"sq4_{k}", [128, 4, 512], BF16) for k in range(2)]
                    for g0 in range(0, 32, 4):
                        ph, Bph = PH[(g0 // 4) % 2]
                        for gg in range(4):
                            g = g0 + gg
                            for sp in range(NSP):
                                T(lambda h, ph=ph, gg=gg, g=g, sp=sp: h.transpose(ph[:, gg * C + sp * CS:gg * C + (sp + 1) * CS], u_tm[0:CS, sp, g, :], cstb[0:CS, 0, 0:CS]),
                                  [BS1, Bcstb], [Bph])
                        A(lambda h, ph=ph, g0=g0: h.copy(U2[:, g0:g0 + 4, :], ph[:, 0:4 * C].rearrange("p (g c) -> p g c", g=4)), [Bph], BU2[g0:g0 + 4] + ([BQ1] if g0 == 0 else []))
                    V(lambda h: h.memset(SPr[:, :, 0:1], 0.0), [], [BQ2] + BSPr); V(lambda h: h.memset(SPi[:, :, 0:1], 0.0), [], [BQ2] + BSPi)
                    for gl in range(16):
                        pr, Bpr = PF[2 * (gl % 2)]; pi_, Bpi = PF[2 * (gl % 2) + 1]
                        (mr, Bmr), (mi, Bmi), (c1, Bc1), (c2, Bc2), (zr, Bzr), (zi, Bzi), (c3, Bc3), (c4, Bc4), (c5, Bc5), (c6, Bc6) = tset[gl % 2]
                        for (pp, Bpp, wb, Bwb) in ((pr, Bpr, wbr, Bwbr), (pi_, Bpi, wbi, Bwbi)):
                            T(lambda h, pp=pp, wb=wb, gl=gl: h.matmul(pp[:, 0:C], wb[:, gl, :], U2[:, gl, :], start=True, stop=False), [Bwb, BU2[gl]], [Bpp])
                            T(lambda h, pp=pp, wb=wb, gl=gl: h.matmul(pp[:, 0:C], wb[:, gl + 16, :], U2[:, gl + 16, :], start=False, stop=True), [Bwb, BU2[gl + 16]], [Bpp])
                        V(lambda h, gl=gl: h.tensor_tensor(c1[:], pr[:, 0:C], cosT[:, gl, :], ALU.mult), [Bpr, BcosT], [Bc1])
                        V(lambda h, gl=gl: h.tensor_tensor(c2[:], pi_[:, 0:C], sinT[:, gl, :], ALU.mult), [Bpi, BsinT], [Bc2])
                        G(lambda h: h.tensor_tensor(mr[:], c1[:], c2[:], ALU.add), [Bc1, Bc2], [Bmr])
                        V(lambda h, gl=gl, c3=c3, pi_=pi_: h.tensor_tensor(c3[:], pi_[:, 0:C], cosT[:, gl, :], ALU.mult), [Bpi, BcosT], [Bc3])
                        V(lambda h, gl=gl, c4=c4, pr=pr: h.tensor_tensor(c4[:], pr[:, 0:C], sinT[:, gl, :], ALU.mult), [Bpr, BsinT], [Bc4])
                        G(lambda h, mi=mi, c3=c3, c4=c4: h.tensor_tensor(mi[:], c3[:], c4[:], ALU.subtract), [Bc3, Bc4], [Bmi])
                        V(lambda h, gl=gl: h.tensor_tensor_scan(zr[:], rho[:, gl:gl + 1].to_broadcast([128, C]), mr[:], 0.0, ALU.mult, ALU.add), [Brho, Bmr], [Bzr])
                        V(lambda h, gl=gl: h.tensor_tensor_scan(zi[:], rho[:, gl:gl + 1].to_broadcast([128, C]), mi[:], 0.0, ALU.mult, ALU.add), [Brho, Bmi], [Bzi])
                        V(lambda h, gl=gl: h.tensor_tensor(c5[:, 0:C - 1], zr[:, 0:C - 1], cosT[:, gl, 0:C - 1], ALU.mult), [Bzr, BcosT], [Bc5])
                        V(lambda h, gl=gl: h.tensor_tensor(c6[:, 0:C - 1], zi[:, 0:C - 1], sinT[:, gl, 0:C - 1], ALU.mult), [Bzi, BsinT], [Bc6])
                        G(lambda h, gl=gl: h.tensor_tensor(SPr[:, gl, 1:C], c5[:, 0:C - 1], c6[:, 0:C - 1], ALU.subtract), [Bc5, Bc6], [BSPr[gl]])
                        V(lambda h, gl=gl, c3=c3, zr=zr: h.tensor_tensor(c3[:, 0:C - 1], zr[:, 0:C - 1], sinT[:, gl, 0:C - 1], ALU.mult), [Bzr, BsinT], [Bc3])
                        V(lambda h, gl=gl, c4=c4, zi=zi: h.tensor_tensor(c4[:, 0:C - 1], zi[:, 0:C - 1], cosT[:, gl, 0:C - 1], ALU.mult), [Bzi, BcosT], [Bc4])
                        G(lambda h, gl=gl, c3=c3, c4=c4: h.tensor_tensor(SPi[:, gl, 1:C], c3[:, 0:C - 1], c4[:, 0:C - 1], ALU.add), [Bc3, Bc4], [BSPi[gl]])
                    ceir_f = ceir.rearrange("p g j o -> p g (j o)"); ceii_f = ceii.rearrange("p g j o -> p g (j o)")
                    def s3_a(g):
                        hf = g // 16; gl = g % 16; qs = slice(hf * 64, hf * 64 + 64)
                        py, Bpy = PF[2 + g % 2]
                        (gsq, Bgsq), (gw, Bgw), (gs_, Bgs) = gset[g % 2]
                        T(lambda h: h.matmul(py[:, 0:C], toep[:, g, :], U2[:, g, :], start=True, stop=False), [Btoep, BU2[g]], [Bpy])
                        T(lambda h: h.matmul(py[:, 0:C], ceir_f[qs, gl, :], SPr[qs, gl, :], start=False, stop=False), [Bceir, BSPr[gl]], [Bpy])
                        T(lambda h: h.matmul(py[:, 0:C], ceii_f[qs, gl, :], SPi[qs, gl, :], start=False, stop=True), [Bceii, BSPi[gl]], [Bpy])
                        A(lambda h: h.activation(gsq[:], py[:, 0:C], AF.Square), [Bpy], [Bgsq])
                        V(lambda h: h.tensor_scalar(gw[:], gsq[:], 0.044715, 1.0, ALU.mult, ALU.add), [Bgsq], [Bgw])
                        V(lambda h: h.tensor_tensor(gw[:], gw[:], py[:, 0:C], ALU.mult), [Bgw, Bpy], [Bgw])

                    def s3_b(g):
                        py, Bpy = PF[2 + g % 2]
                        (gsq, Bgsq), (gw, Bgw), (gs_, Bgs) = gset[g % 2]
                        A(lambda h: h.activation(gs_[:], gw[:], AF.Sigmoid, scale=1.5957691216), [Bgw], [Bgs])
                        V(lambda h: h.tensor_tensor(G2[:, g, :], gs_[:], py[:, 0:C], ALU.mult), [Bgs, Bpy], [BG2[g]] + ([BS1] if g == 0 else []))

                    for g in range(32):
                        s3_a(g)
                        if g >= 1:
                            s3_b(g - 1)
                    s3_b(31)
                    for sp in range(NSP):
                        for g0 in range(0, 32, 8):
                            ph, Bph = PH[(g0 // 8) % 2]
                            for gg in range(8):
                                T(lambda h, ph=ph, gg=gg, g0=g0, sp=sp: h.transpose(ph[0:CS, gg * 128:(gg + 1) * 128], G2[:, g0 + gg, sp * CS:(sp + 1) * CS], identb), [BG2[g0 + gg], Bcstb], [Bph])
                            A(lambda h, ph=ph, g0=g0, sp=sp: h.copy(y_tm[0:CS, sp, :, g0 * 16:(g0 + 8) * 16].rearrange("p j (g o) -> p j g o", o=16),
                                                                    ph[0:CS, :].rearrange("p (g j o) -> p j g o", g=8, j=8)), [Bph], [BQ1] + (BU2 if (sp == 0 and g0 == 0) else []))
                    for sp in range(NSP):
                        for tg in range(4):
                            ph, Bph = PH[tg % 2]
                            for j in range(8):
                                T(lambda h, ph=ph, j=j, sp=sp, tg=tg: h.transpose(ph[:, j * CS:(j + 1) * CS], y_tm[0:CS, sp, j, tg * 128:(tg + 1) * 128], cstb[0:CS, 0, 0:CS]), [BQ1, Bcstb], [Bph])
                            V(lambda h, ph=ph, sp=sp, tg=tg: h.tensor_copy(yT[:, tg, sp * SPT:(sp + 1) * SPT].rearrange("p (c j) -> p c j", j=8),
                                                                           ph[:, 0:8 * CS].rearrange("p (j c) -> p c j", j=8)), [Bph], [BQ2] + ((BSPr + BSPi) if (sp == 0 and tg == 0) else []))
                    dump(f"yT{b}", yT, BQ2, [128, 4, L])
                    for bk in range(NBK):
                        nblk = BW // 128
                        pss, Bpss = PF[4]
                        for co in range(4):
                            pf, Bpf = PF[co % 2]
                            for tg in range(4):
                                T(lambda h, pf=pf, tg=tg, co=co, bk=bk: h.matmul(pf[:, 0:BW], wglu[:, tg, co * 128:(co + 1) * 128], yT[:, tg, bk * BW:(bk + 1) * BW], start=(tg == 0), stop=(tg == 3)),
                                  [Bwglu, BQ2], [Bpf])
                            A(lambda h, pf=pf, co=co: h.activation(gT[:, 0:BW], pf[:, 0:BW], AF.Sigmoid, bias=vecs[:, 24 + co:25 + co]), [Bpf, Bvecs], [BgT])
                            V(lambda h, co=co, bk=bk: h.tensor_tensor(ssmT[:, co, bk * BW:(bk + 1) * BW], yT[:, co, bk * BW:(bk + 1) * BW], gT[:, 0:BW], ALU.mult), [BQ2, BgT], [BQ3])
                            sqc, Bsqc = sq4[bk % 2]
                            A(lambda h, co=co, bk=bk, sqc=sqc: h.activation(sqc[:, co, 0:BW], ssmT[:, co, bk * BW:(bk + 1) * BW], AF.Square), [BQ3], [Bsqc])
                        for tb in range(nblk):
                            for co in range(4):
                                T(lambda h, tb=tb, co=co, sqc=sqc: h.matmul(pss[:, tb:tb + 1], sqc[:, co, tb * 128:(tb + 1) * 128], onecol[:], start=(co == 0), stop=(co == 3)), [Bsqc, Bonecol], [Bpss])
                        V(lambda h, bk=bk, nblk=nblk: h.tensor_copy(stat[:, 2, bk * nblk:(bk + 1) * nblk], pss[:, 0:nblk]), [Bpss], [Bstat])
                    dump(f"ssmT{b}", ssmT, BQ3, [128, 4, L])
                    fw.barrier()

                with ExitStack() as sk:
                    xb = [sbt(sk, f"xbk{i}", [128, 1024]) for i in range(4)]
                    hb = [sbt(sk, f"hb{i}", [128, 1024]) for i in range(2)]
                    rstd_from_ss(stat[:, 1, :], Bstat, rs[:, 1, :], Brs, 1.0 / 512, stat[:, 3, :], Bstat)
                    rstd_from_ss(stat[:, 2, :], Bstat, rs[:, 2, :], Brs, 1.0 / 512, stat[:, 3, :], Bstat)
                    for n in range(NT):
                        xt, Bxt = xb[n % 4]
                        D(lambda h, xt=xt, n=n: h.dma_start(out=xt[:], in_=x_d[b, n * 128:(n + 1) * 128, :]), [], [Bxt])
                        for hf in range(2):
                            pa, Bpa = PF[hf]; ps_, Bps = PF[2 + hf]
                            for ct in range(4):
                                T(lambda h, pa=pa, ct=ct, n=n, hf=hf: h.matmul(pa[:], sbT[:, ct, n * 128:(n + 1) * 128], wo[:, ct, hf * 512:(hf + 1) * 512], start=(ct == 0), stop=(ct == 3)),
                                  [BsbT, Bwo], [Bpa])
                            for ct in range(4):
                                T(lambda h, ps_=ps_, ct=ct, n=n, hf=hf: h.matmul(ps_[:], ssmT[:, ct, n * 128:(n + 1) * 128], wo[:, 4 + ct, hf * 512:(hf + 1) * 512], start=(ct == 0), stop=(ct == 3)),
                                  [BQ3, Bwo], [Bps])
                        ht, Bht = hb[n % 2]
                        for hf in range(2):
                            pa, Bpa = PF[hf]; ps_, Bps = PF[2 + hf]
                            V(lambda h, pa=pa, xt=xt, ht=ht, n=n, hf=hf: h.scalar_tensor_tensor(ht[:, hf * 512:(hf + 1) * 512], pa[:], rs[:, 1, n:n + 1], xt[:, hf * 512:(hf + 1) * 512], ALU.mult, ALU.add),
                              [Bpa, Brs, Bxt], [Bht])
                            V(lambda h, ps_=ps_, ht=ht, n=n, hf=hf: h.scalar_tensor_tensor(ht[:, hf * 512:(hf + 1) * 512], ps_[:], rs[:, 2, n:n + 1], ht[:, hf * 512:(hf + 1) * 512], ALU.mult, ALU.add),
                              [Bps, Brs, Bht], [Bht])
                        fw.dma("sp", lambda h, ht=ht, n=n: h.dma_start(out=out_d[b, n * 128:(n + 1) * 128, :], in_=ht[:]), [Bht], [BoutBlk[b * NT + n]], track=Bht)
                        dump(f"h{b}_{n}", ht[:], Bht, [128, 1024])
                    dump(f"rs{b}", rs[:], Brs, [128, 4, NT]); dump(f"stat{b}", stat[:], Bstat, [128, 4, NT])
                    fw.barrier()
            fw.barrier()

        with ExitStack() as p2:
            w1, Bw1 = sbt(p2, "w1", [128, 8, 4096], BF16)
            w2, Bw2 = sbt(p2, "w2", [128, 32, 1024], BF16)
            wscope = ExitStack()
            wsas = [sbt(wscope, f"wsa{k}", [128, 8, 256]) for k in range(2)]
            for q in range(16):
                wsa, Bwsa = wsas[q % 2]
                D(lambda h, q=q, wsa=wsa: h.dma_start(out=wsa[:], in_=w1_d[:, q * 256:(q + 1) * 256].rearrange("(c p) n -> p c n", p=128)), [], [Bwsa])
                (G if q % 2 == 0 else V)(lambda h, q=q, wsa=wsa: h.tensor_tensor(w1[:, :, q * 256:(q + 1) * 256], wsa[:], vecs[:, 8:16].unsqueeze(2).to_broadcast([128, 8, 256]), ALU.mult), [Bwsa, Bvecs], [Bw1])
            for q in range(16):
                wsa, Bwsa = wsas[q % 2]
                D(lambda h, q=q, wsa=wsa: h.dma_start(out=wsa[:].rearrange("p c n -> p (c n)").rearrange("p (c n) -> p c n", c=2), in_=w2_d[q * 256:(q + 1) * 256, :].rearrange("(c p) n -> p c n", p=128)), [], [Bwsa])
                (V if q % 2 == 0 else G)(lambda h, q=q, wsa=wsa: h.tensor_copy(w2[:, q * 2:(q + 1) * 2, :], wsa[:].rearrange("p c n -> p (c n)").rearrange("p (c n) -> p c n", c=2)), [Bwsa], [Bw2])
            fw.barrier()
            wscope.close()
            TBG = min(4, NT)
            NG = NB * NT // TBG
            hld = [sbt(p2, f"hld{i}", [128, 1024]) for i in range(2)]
            hrs = [sbt(p2, f"hrs{i}", [128, 1024]) for i in range(2)]
            hs, Bhs = sbt(p2, "hs", [128, 1024], BF16)
            hsTs = [sbt(p2, f"hsT{i}", [128, 8, TBG * 128], BF16) for i in range(2)]
            aT2, BaT2 = sbt(p2, "aT2", [128, 32, TBG * 128], BF16)
            rl, Brl = sbt(p2, "rl", [128, TBG * 128], BF16)
            ob, Bob = sbt(p2, "ob", [128, 1024])
            st2, Bst2_ = sbt(p2, "st2", [128, 3, NG * TBG])
            Bst2 = [fw.buf(f"st2_{i}") for i in range(NG * TBG)]
            junk2, Bjunk2 = sbt(p2, "junk2", [128, 1024], BF16)
            NW = TBG * 128
            G(lambda h: h.memset(st2[:], 0.0), [], Bst2)

            def mlp_prep(grp):
                hsT, BhsT = hsTs[grp % 2]
                for tb in range(TBG):
                    blk = grp * TBG + tb
                    bb, n = blk // NT, blk % NT
                    ht, Bht = hld[blk % 2]
                    Bs = Bst2[blk]
                    D(lambda h: h.dma_start(out=ht[:], in_=out_d[bb, n * 128:(n + 1) * 128, :]), [BoutBlk[blk]], [Bht])
                    A(lambda h: h.activation(junk2[:], ht[:], AF.Square, accum_out=st2[:, 0, blk:blk + 1]), [Bht], [Bjunk2, Bs])
                    rstd_from_ss(st2[:, 0, blk:blk + 1], Bs, st2[:, 1, blk:blk + 1], Bs, 1.0 / 1024, st2[:, 2, blk:blk + 1], Bs)
                    V(lambda h: h.tensor_scalar_mul(hs[:], ht[:], st2[:, 1, blk:blk + 1]), [Bht, Bs], [Bhs])
                    ph, Bph = PH[tb % 2]
                    for dc in range(8):
                        T(lambda h, dc=dc: h.transpose(ph[:, dc * 128:(dc + 1) * 128], hs[:, dc * 128:(dc + 1) * 128], identb), [Bhs, Bcstb], [Bph])
                    A(lambda h: h.copy(hsT[:, :, tb * 128:(tb + 1) * 128], ph[:].rearrange("p (c t) -> p c t", c=8)), [Bph], [BhsT])

            def mlp_main(grp):
                hsT, BhsT = hsTs[grp % 2]
                for ht_ in range(32):
                    pf, Bpf = PF[ht_ % 2]
                    for dc in range(8):
                        T(lambda h, dc=dc: h.matmul(pf[:, 0:NW], w1[:, dc, ht_ * 128:(ht_ + 1) * 128], hsT[:, dc, :], start=(dc == 0), stop=(dc == 7)), [Bw1, BhsT], [Bpf])
                    A(lambda h: h.activation(rl[:], pf[:, 0:NW], AF.Relu), [Bpf], [Brl])
                    if ht_ % 2 == 0:
                        V(lambda h: h.tensor_tensor(aT2[:, ht_, :], rl[:], rl[:], ALU.mult), [Brl], [BaT2])
                    else:
                        G(lambda h: h.tensor_tensor(aT2[:, ht_, :], rl[:], rl[:], ALU.mult), [Brl], [BaT2])
                for tb in range(TBG):
                    blk = grp * TBG + tb
                    bb, n = blk // NT, blk % NT
                    hr, Bhr = hrs[blk % 2]
                    D(lambda h: h.dma_start(out=hr[:], in_=out_d[bb, n * 128:(n + 1) * 128, :]), [BoutBlk[blk]], [Bhr])
                    for hf in range(2):
                        po, Bpo = PF[2 + hf]
                        for k in range(32):
                            T(lambda h, k=k: h.matmul(po[:], aT2[:, k, tb * 128:(tb + 1) * 128], w2[:, k, hf * 512:(hf + 1) * 512], start=(k == 0), stop=(k == 31)), [BaT2, Bw2], [Bpo])
                        V(lambda h: h.tensor_tensor(ob[:, hf * 512:(hf + 1) * 512], po[:], hr[:, hf * 512:(hf + 1) * 512], ALU.add), [Bpo, Bhr], [Bob])
                    fw.dma("sp", lambda h: h.dma_start(out=out_d[bb, n * 128:(n + 1) * 128, :], in_=ob[:]), [Bob, Bhr], [BoutBlk[blk]], track=Bob)

            mlp_prep(0)
            for grp in range(NG):
                if grp + 1 < NG:
                    mlp_prep(grp + 1)
                mlp_main(grp)
            e = fw.E["sp"]
            waits = fw._waits(e, [], BoutBlk + list(dbg_outs.values()))
            fw._do(e, waits, None, None)
            fw.barrier()
    return nc


def _consts():
    ident = np.eye(128, dtype=np.float32)
    t = np.arange(128)[:, None]
    s = np.arange(128)[None, :]
    maskb = np.where(s <= 127 - t, -1000.0, 0.0).astype(np.float32)
    bd = np.zeros((128, 128), np.float32)
    bd[:64, :64] = 1.0
    bd[64:, 64:] = 1.0
    iota = np.tile(np.arange(256, dtype=np.float32)[None, :], (128, 1))
    return np.stack([ident, maskb, bd]), iota


_NC_CACHE = {}


def run(inputs, L, NB, ncores, dbg=False):
    key = (L, NB, dbg)
    if key not in _NC_CACHE:
        _NC_CACHE[key] = build_nc(L, NB, dbg)
    nc = _NC_CACHE[key]
    consts, iota = _consts()
    x = np.ascontiguousarray(inputs["x"], dtype=np.float32)
    in_maps = []
    for c in range(ncores):
        m = {k: np.ascontiguousarray(v, dtype=np.float32) for k, v in inputs.items() if k != "x"}
        m["x"] = np.ascontiguousarray(x[c * NB:(c + 1) * NB])
        m["consts"] = consts
        m["iota"] = iota
        in_maps.append(m)
    res = run_bass_kernel_spmd(nc, in_maps, core_ids=list(range(ncores)))
    return res


def kernel(**inputs):
    res = run(inputs, 2048, 2, 8)
    out = np.concatenate([r["out"] for r in res.results], axis=0)
    return out.astype(np.float32)
```

```python
import math
import numpy as np
from contextlib import ExitStack
import concourse.bass as bass
import concourse.mybir as mybir
from concourse.bass_utils import run_bass_kernel_spmd

F32 = mybir.dt.float32
BF16 = mybir.dt.bfloat16
I32 = mybir.dt.int32
AF = mybir.ActivationFunctionType
ALU = mybir.AluOpType
EPS = 1e-6
TWO_PI = 2.0 * math.pi


class Buf:
    __slots__ = ("name", "w", "r", "dsem", "dcnt")

    def __init__(self, name):
        self.name = name
        self.w = None
        self.r = []
        self.dsem = None
        self.dcnt = 0


class Eng:
    def __init__(self, name, sem, same_sync=True):
        self.name = name
        self.sem = sem
        self.cnt = 0
        self.seen = {}
        self.same_sync = same_sync


class FW:
    def __init__(self, nc, stack):
        self.H = {"pe": nc.tensor, "act": nc.scalar, "dve": nc.vector, "pool": nc.gpsimd, "sp": nc.sync}
        self.nc = nc
        self.stack = stack
        self.E = {}
        for n, ss in (("pe", False), ("act", True), ("dve", True), ("pool", True), ("sp", True)):
            sem = stack.enter_context(nc.semaphore("s_" + n))
            self.E[n] = Eng(n, sem, ss)
        self.nbuf = 0
        self.dma_last = {}

    def buf(self, name=None):
        self.nbuf += 1
        return Buf(name or f"b{self.nbuf}")

    def _waits(self, e, reads, writes):
        need = {}

        def add(ev):
            if ev is None:
                return
            sem, val = ev
            if (not e.same_sync) and sem is e.sem:
                return
            k = id(sem)
            if e.seen.get(k, 0) >= val:
                return
            if k not in need or need[k][1] < val:
                need[k] = (sem, val)

        for b in reads:
            add(b.w)
        for b in writes:
            add(b.w)
            for ev in b.r:
                add(ev)
        out = list(need.values())
        for sem, val in out:
            e.seen[id(sem)] = val
        return out

    def _do(self, e, waits, fn, inc):
        h = self.H[e.name]
        for sem, val in waits:
            h.wait_ge(sem, val)
        if fn is not None:
            fn(h).then_inc(inc[0], inc[1])

    def op(self, eng, fn, reads=(), writes=()):
        e = self.E[eng]
        waits = self._waits(e, reads, writes)
        e.cnt += 1
        ev = (e.sem, e.cnt)
        self._do(e, waits, fn, (e.sem, 1))
        for b in reads:
            b.r.append(ev)
        for b in writes:
            b.w = ev
            b.r = []
        return ev

    def dma(self, eng, fn, reads=(), writes=(), track=None):
        e = self.E[eng]
        waits = self._waits(e, reads, writes)
        tb = track or (writes[0] if writes else reads[0])
        if tb.dsem is None:
            tb.dsem = self.stack.enter_context(self.nc.semaphore("d_" + tb.name))
        tb.dcnt += 16
        ev = (tb.dsem, tb.dcnt)
        self.dma_last[id(tb.dsem)] = ev
        self._do(e, waits, fn, (tb.dsem, 16))
        for b in reads:
            b.r.append(ev)
        for b in writes:
            b.w = ev
            b.r = []
        return ev

    def barrier(self):
        evs = [(x.sem, x.cnt) for x in self.E.values() if x.cnt > 0] + list(self.dma_last.values())
        for e in self.E.values():
            ws = []
            for sem, val in evs:
                if sem is e.sem and not e.same_sync:
                    continue
                if e.seen.get(id(sem), 0) < val:
                    ws.append((sem, val))
                    e.seen[id(sem)] = val
            if ws:
                self._do(e, ws, None, None)


def build_nc(L, NB, dbg=False):
    NT = L // 128
    C = L // 8
    CS = min(128, C)
    NSP = C // CS
    SPT = CS * 8
    NBK = max(1, L // 512)
    BW = min(512, L)
    nc = bass.Bass("TRN2", target_bir_lowering=False)

    def din(name, shape):
        return nc.dram_tensor(name, list(shape), F32, kind="ExternalInput").ap()

    x_d = din("x", [NB, L, 1024])
    out_d = nc.dram_tensor("out", [NB, L, 1024], F32, kind="ExternalOutput").ap()
    norm1_d = din("norm1_g", [1024]); w_in_d = din("w_in", [1024, 2048])
    qg_d = din("q_norm_g", [64]); kg_d = din("k_norm_g", [64])
    lre_d = din("ssm_lambda_re", [32, 64]); lim_d = din("ssm_lambda_im", [32, 64]); ldt_d = din("ssm_log_dt", [32])
    bre_d = din("ssm_b_re", [32, 64, 16]); bim_d = din("ssm_b_im", [32, 64, 16])
    cre_d = din("ssm_c_re", [32, 16, 64]); cim_d = din("ssm_c_im", [32, 16, 64])
    sd_d = din("ssm_d", [32, 16]); wglu_d = din("w_glu", [512, 512]); bglu_d = din("b_glu", [512])
    gao_d = din("attn_out_g", [512]); gso_d = din("ssm_out_g", [512]); wout_d = din("w_out", [1024, 1024])
    norm2_d = din("norm2_g", [1024]); w1_d = din("w_mlp_in", [1024, 4096]); w2_d = din("w_mlp_out", [4096, 1024])
    cst_d = din("consts", [3, 128, 128])
    iota_d = din("iota", [128, 256])
    dbg_outs = {}

    with ExitStack() as gst:
        fw = FW(nc, gst)

        def V(fn, r=(), w=()): return fw.op("dve", fn, r, w)
        def A(fn, r=(), w=()): return fw.op("act", fn, r, w)
        def G(fn, r=(), w=()): return fw.op("pool", fn, r, w)
        def T(fn, r=(), w=()): return fw.op("pe", fn, r, w)
        def D(fn, r=(), w=(), track=None): return fw.dma("sp", fn, r, w, track)

        uniq = [0]

        def sbt(st, name, shape, dt=F32):
            uniq[0] += 1
            name = f"{name}_{uniq[0]}"
            return st.enter_context(nc.sbuf_tensor(name, list(shape), dt)), fw.buf(name)

        def dump(name, ap, b, shape):
            if not dbg:
                return
            d = nc.dram_tensor("dbg_" + name, list(shape), ap.dtype, kind="ExternalOutput").ap()
            bo = fw.buf("dbgo_" + name)
            D(lambda h: h.dma_start(out=d, in_=ap), [b], [bo])
            dbg_outs[name] = bo

        BoutBlk = [fw.buf(f"o{i}") for i in range(NB * NT)]
        PF = []
        for i in range(5):
            t = gst.enter_context(nc.psum_tensor(f"pf{i}", [128, 512], F32)); PF.append((t, fw.buf(f"pf{i}")))
        PH = []
        for i in range(3):
            t = gst.enter_context(nc.psum_tensor(f"ph{i}", [128, 1024], BF16)); PH.append((t, fw.buf(f"ph{i}")))

        cstf, Bcstf = sbt(gst, "cstf", [128, 3, 128])
        D(lambda h: h.dma_start(out=cstf[:], in_=cst_d.rearrange("k p f -> p k f")), [], [Bcstf])
        identf = cstf[:, 0, :]
        cstb, Bcstb = sbt(gst, "cstb", [128, 3, 128], BF16)
        V(lambda h: h.tensor_copy(cstb[:], cstf[:]), [Bcstf], [Bcstb])
        identb = cstb[:, 0, :]; maskb = cstb[:, 1, :]; bd64 = cstb[:, 2, :]
        ones_f, Bones = sbt(gst, "ones_f", [128, 1])
        G(lambda h: h.memset(ones_f[:], 1.0), [], [Bones])
        onecol, Bonecol = sbt(gst, "onecol", [128, 1], BF16)
        G(lambda h: h.memset(onecol[:], 1.0), [], [Bonecol])
        epsT, Beps = sbt(gst, "epsT", [128, 1])
        G(lambda h: h.memset(epsT[:], EPS), [], [Beps])
        vecs, Bvecs = sbt(gst, "vecs", [128, 32])
        with nc.allow_non_contiguous_dma("tiny param vectors"):
            D(lambda h: h.dma_start(out=vecs[:, 0:8], in_=norm1_d.rearrange("(c p) -> p c", p=128)), [], [Bvecs])
            D(lambda h: h.dma_start(out=vecs[:, 8:16], in_=norm2_d.rearrange("(c p) -> p c", p=128)), [], [Bvecs])
            D(lambda h: h.dma_start(out=vecs[:, 16:20], in_=gao_d.rearrange("(c p) -> p c", p=128)), [], [Bvecs])
            D(lambda h: h.dma_start(out=vecs[:, 20:24], in_=gso_d.rearrange("(c p) -> p c", p=128)), [], [Bvecs])
            D(lambda h: h.dma_start(out=vecs[:, 24:28], in_=bglu_d.rearrange("(c p) -> p c", p=128)), [], [Bvecs])
            for hh in range(2):
                D(lambda h, hh=hh: h.dma_start(out=vecs[hh * 64:(hh + 1) * 64, 28:29], in_=qg_d.rearrange("(p o) -> p o", o=1)), [], [Bvecs])
                D(lambda h, hh=hh: h.dma_start(out=vecs[hh * 64:(hh + 1) * 64, 29:30], in_=kg_d.rearrange("(p o) -> p o", o=1)), [], [Bvecs])

        gqk, Bgqk = sbt(gst, "gqk", [128, 2, 64])
        D(lambda h: h.dma_start(out=gqk[:, 0, :], in_=qg_d.partition_broadcast(128)), [], [Bgqk])
        D(lambda h: h.dma_start(out=gqk[:, 1, :], in_=kg_d.partition_broadcast(128)), [], [Bgqk])

        def rstd_from_ss(ss_ap, Bss, out_ap, Bout, inv_n, tmp_ap, Btmp):
            A(lambda h: h.activation(tmp_ap, ss_ap, AF.Sqrt, bias=epsT[:], scale=inv_n), [Bss, Beps], [Btmp])
            V(lambda h: h.reciprocal(out_ap, tmp_ap), [Btmp], [Bout])

        with ExitStack() as p1:
            rho, Brho = sbt(p1, "rho", [128, 16])
            CSPEC = [("toep", [128, 32 * 128], BF16), ("wbr", [128, 32 * 128], BF16), ("wbi", [128, 32 * 128], BF16),
                     ("ceir", [128, 16 * 128], BF16), ("ceii", [128, 16 * 128], BF16), ("cosT", [128, 16 * C], F32), ("sinT", [128, 16 * C], F32)]
            scr = {n: (nc.dram_tensor("scr_" + n, shp, dt_, kind="Internal").ap(), fw.buf("scr_" + n)) for n, shp, dt_ in CSPEC}

            def alloc_consts(st):
                t = {n: sbt(st, n, shp, dt_) for n, shp, dt_ in CSPEC}
                return t
            wglu, Bwglu = sbt(p1, "wglu", [128, 4, 512], BF16)

            with ExitStack() as bs:
                _ct = alloc_consts(bs)
                toep = _ct["toep"][0][:].rearrange("p (g f) -> p g f", g=32); Btoep = _ct["toep"][1]
                wbr = _ct["wbr"][0][:].rearrange("p (g f) -> p g f", g=32); Bwbr = _ct["wbr"][1]
                wbi = _ct["wbi"][0][:].rearrange("p (g f) -> p g f", g=32); Bwbi = _ct["wbi"][1]
                ceir = _ct["ceir"][0][:].rearrange("p (g j o) -> p g j o", g=16, j=8); Bceir = _ct["ceir"][1]
                ceii = _ct["ceii"][0][:].rearrange("p (g j o) -> p g j o", g=16, j=8); Bceii = _ct["ceii"][1]
                cosT = _ct["cosT"][0][:].rearrange("p (g c) -> p g c", g=16); BcosT = _ct["cosT"][1]
                sinT = _ct["sinT"][0][:].rearrange("p (g c) -> p g c", g=16); BsinT = _ct["sinT"][1]
                def t32(name, shape):
                    return sbt(bs, name, shape)
                LR, BLR = t32("LR", [128, 16]); LI, BLI = t32("LI", [128, 16]); LDT, BLDT = t32("LDT", [128, 16])
                BR, BBR = t32("BR", [128, 16, 16]); BI, BBI = t32("BI", [128, 16, 16])
                CR, BCR = t32("CR", [128, 16, 16]); CI, BCI = t32("CI", [128, 16, 16]); CIn, BCIn = t32("CIn", [128, 16, 16])
                DCOL, BDCOL = t32("DCOL", [128, 32])
                cpad, Bcpad = t32("cpad", [128, 2, 4, 128])
                wg32, Bwg32 = t32("wg32", [128, 4, 512])
                with nc.allow_non_contiguous_dma("small transposed parameter loads"):
                    for hf in range(2):
                        qs = slice(hf * 64, hf * 64 + 64); gs = slice(hf * 16, hf * 16 + 16)
                        D(lambda h, qs=qs, gs=gs: h.dma_start(out=LR[qs, :], in_=lre_d[gs, :].rearrange("g p -> p g")), [], [BLR])
                        D(lambda h, qs=qs, gs=gs: h.dma_start(out=LI[qs, :], in_=lim_d[gs, :].rearrange("g p -> p g")), [], [BLI])
                        D(lambda h, qs=qs, gs=gs: h.dma_start(out=LDT[qs, :], in_=ldt_d[gs].partition_broadcast(64)), [], [BLDT])
                        D(lambda h, qs=qs, gs=gs: h.dma_start(out=BR[qs, :, :], in_=bre_d[gs].rearrange("g p h -> p g h")), [], [BBR])
                        D(lambda h, qs=qs, gs=gs: h.dma_start(out=BI[qs, :, :], in_=bim_d[gs].rearrange("g p h -> p g h")), [], [BBI])
                    for i in range(8):
                        D(lambda h, i=i: h.dma_start(out=DCOL[i * 16:(i + 1) * 16, :], in_=sd_d.rearrange("g h -> h g")), [], [BDCOL])
                G(lambda h: h.memset(cpad[:], 0.0), [], [Bcpad])
                for ri, cd in enumerate((cre_d, cim_d)):
                    for tg in range(4):
                        hf = tg // 2
                        D(lambda h, ri=ri, tg=tg, hf=hf, cd=cd: h.dma_start(
                            out=cpad[:, ri, tg, hf * 64:(hf + 1) * 64],
                            in_=cd[tg * 8:(tg + 1) * 8].rearrange("g o p -> (g o) p")), [], [Bcpad])
                D(lambda h: h.dma_start(out=wg32[:], in_=wglu_d.rearrange("(c p) n -> p c n", p=128)), [], [Bwg32])
                V(lambda h: h.tensor_copy(wglu[:], wg32[:]), [Bwg32], [Bwglu])
                for ri, (dst, Bdst) in enumerate(((CR, BCR), (CI, BCI))):
                    for tg in range(4):
                        hf = tg // 2
                        pf, Bpf = PF[tg % 2]
                        T(lambda h, ri=ri, tg=tg, pf=pf: h.matmul(pf[:, 0:128], cpad[:, ri, tg, :], identf, start=True, stop=True), [Bcpad, Bcstf], [Bpf])
                        qs = slice(hf * 64, hf * 64 + 64)
                        g0 = (tg % 2) * 8
                        V(lambda h, dst=dst, pf=pf, qs=qs, g0=g0: h.tensor_copy(
                            dst[qs, g0:g0 + 8, :], pf[qs, 0:128].rearrange("p (g o) -> p g o", o=16)), [Bpf], [Bdst])
                V(lambda h: h.tensor_single_scalar(CIn[:], CI[:], -1.0, ALU.mult), [BCI], [BCIn])

                cnt = [0]

                def newt(shape):
                    cnt[0] += 1
                    return t32(f"bt{cnt[0]}", shape)

                def tt(o, Bo, a, Ba, b, Bb, op):
                    V(lambda h: h.tensor_tensor(o, a, b, op), [Ba, Bb], [Bo])

                NBIG = 4 * C
                ft, Bft = t32("ft", [128, NBIG]); fti, Bfti = sbt(bs, "fti", [128, NBIG], I32); ftf, Bftf = t32("ftf", [128, NBIG])
                cu = [t32(f"cu{i}", [128, 16, 16]) for i in range(5)]

                def frac(dst, Bdst, src, Bsrc, add, n):
                    t, Bt = ft[:, 0:n], Bft; ti, Bti = fti[:, 0:n], Bfti; tf, Btf = ftf[:, 0:n], Bftf
                    V(lambda h: h.tensor_single_scalar(t, src, add, ALU.add), [Bsrc], [Bt])
                    V(lambda h: h.tensor_copy(ti, t), [Bt], [Bti])
                    V(lambda h: h.tensor_copy(tf, ti), [Bti], [Btf])
                    V(lambda h: h.tensor_tensor(dst, t, tf, ALU.subtract), [Bt, Btf], [Bdst])

                dt, Bdt = newt([128, 16]); are, Bare = newt([128, 16]); turns, Bturns = newt([128, 16]); mag, Bmag = newt([128, 16])
                A(lambda h: h.activation(dt[:], LDT[:], AF.Exp), [BLDT], [Bdt])
                tt(are[:], Bare, LR[:], BLR, dt[:], Bdt, ALU.mult)
                tt(turns[:], Bturns, LI[:], BLI, dt[:], Bdt, ALU.mult)
                V(lambda h: h.tensor_single_scalar(turns[:], turns[:], 1.0 / TWO_PI, ALU.mult), [Bturns], [Bturns])
                A(lambda h: h.activation(mag[:], are[:], AF.Exp), [Bare], [Bmag])
                A(lambda h: h.activation(rho[:], are[:], AF.Exp, scale=8.0), [Bare], [Brho])
                fs, Bfs = newt([128, 16]); fcn, Bfcn = newt([128, 16]); sA, BsA = newt([128, 16]); cA, BcA = newt([128, 16])
                frac(fs[:], Bfs, turns[:], Bturns, 0.0, 16)
                frac(fcn[:], Bfcn, turns[:], Bturns, 0.25, 16)
                A(lambda h: h.activation(sA[:], fs[:], AF.Sin, scale=TWO_PI), [Bfs], [BsA])
                A(lambda h: h.activation(cA[:], fcn[:], AF.Sin, scale=TWO_PI), [Bfcn], [BcA])
                lbr, Blbr = newt([128, 16]); lbi, Blbi = newt([128, 16])
                tt(lbr[:], Blbr, mag[:], Bmag, cA[:], BcA, ALU.mult)
                tt(lbi[:], Blbi, mag[:], Bmag, sA[:], BsA, ALU.mult)
                n2, Bn2 = newt([128, 16]); t1, Bt1 = newt([128, 16]); t2, Bt2 = newt([128, 16]); inv, Binv = newt([128, 16])
                nr, Bnr = newt([128, 16]); kr, Bkr = newt([128, 16]); ki, Bki = newt([128, 16])
                tt(t1[:], Bt1, LR[:], BLR, LR[:], BLR, ALU.mult)
                tt(t2[:], Bt2, LI[:], BLI, LI[:], BLI, ALU.mult)
                tt(n2[:], Bn2, t1[:], Bt1, t2[:], Bt2, ALU.add)
                V(lambda h: h.reciprocal(inv[:], n2[:]), [Bn2], [Binv])
                V(lambda h: h.tensor_single_scalar(nr[:], lbr[:], -1.0, ALU.add), [Blbr], [Bnr])
                tt(t1[:], Bt1, nr[:], Bnr, LR[:], BLR, ALU.mult)
                tt(t2[:], Bt2, lbi[:], Blbi, LI[:], BLI, ALU.mult)
                tt(kr[:], Bkr, t1[:], Bt1, t2[:], Bt2, ALU.add)
                tt(kr[:], Bkr, kr[:], Bkr, inv[:], Binv, ALU.mult)
                tt(t1[:], Bt1, lbi[:], Blbi, LR[:], BLR, ALU.mult)
                tt(t2[:], Bt2, nr[:], Bnr, LI[:], BLI, ALU.mult)
                tt(ki[:], Bki, t1[:], Bt1, t2[:], Bt2, ALU.subtract)
                tt(ki[:], Bki, ki[:], Bki, inv[:], Binv, ALU.mult)

                def bc(ap2):
                    return ap2.unsqueeze(2).to_broadcast([128, 16, 16])

                def cmul_b(outr, outi, Bor, Boi, sr, si, Bsr, Bsi, xr, xi, Bxr, Bxi, negate_i=False):
                    (u1, Bu1), (u2, Bu2), (u3, Bu3), (u4, Bu4), (u5, Bu5) = cu
                    V(lambda h: h.tensor_tensor(u1[:], xr, bc(sr), ALU.mult), [Bxr, Bsr], [Bu1])
                    V(lambda h: h.tensor_tensor(u2[:], xi, bc(si), ALU.mult), [Bxi, Bsi], [Bu2])
                    V(lambda h: h.tensor_tensor(outr, u1[:], u2[:], ALU.subtract), [Bu1, Bu2], [Bor])
                    V(lambda h: h.tensor_tensor(u3[:], xi, bc(sr), ALU.mult), [Bxi, Bsr], [Bu3])
                    V(lambda h: h.tensor_tensor(u4[:], xr, bc(si), ALU.mult), [Bxr, Bsi], [Bu4])
                    if negate_i:
                        V(lambda h: h.tensor_tensor(u5[:], u3[:], u4[:], ALU.add), [Bu3, Bu4], [Bu5])
                        V(lambda h: h.tensor_single_scalar(outi, u5[:], -1.0, ALU.mult), [Bu5], [Boi])
                    else:
                        V(lambda h: h.tensor_tensor(outi, u3[:], u4[:], ALU.add), [Bu3, Bu4], [Boi])

                BBr, BBBr = newt([128, 16, 16]); BBi, BBBi = newt([128, 16, 16])
                cmul_b(BBr[:], BBi[:], BBBr, BBBi, kr[:], ki[:], Bkr, Bki, BR[:], BI[:], BBR, BBI)
                PWr, BPWr = newt([128, 9, 16]); PWi, BPWi = newt([128, 9, 16])
                V(lambda h: h.memset(PWr[:, 0, :], 1.0), [], [BPWr]); V(lambda h: h.memset(PWi[:, 0, :], 0.0), [], [BPWi])
                for m in range(1, 9):
                    a1, Ba1 = newt([128, 16]); a2, Ba2 = newt([128, 16])
                    V(lambda h, m=m, a1=a1: h.tensor_tensor(a1[:], PWr[:, m - 1, :], lbr[:], ALU.mult), [BPWr, Blbr], [Ba1])
                    V(lambda h, m=m, a2=a2: h.tensor_tensor(a2[:], PWi[:, m - 1, :], lbi[:], ALU.mult), [BPWi, Blbi], [Ba2])
                    V(lambda h, m=m, a1=a1, a2=a2: h.tensor_tensor(PWr[:, m, :], a1[:], a2[:], ALU.subtract), [Ba1, Ba2], [BPWr])
                    a3, Ba3 = newt([128, 16]); a4, Ba4 = newt([128, 16])
                    V(lambda h, m=m, a3=a3: h.tensor_tensor(a3[:], PWr[:, m - 1, :], lbi[:], ALU.mult), [BPWr, Blbi], [Ba3])
                    V(lambda h, m=m, a4=a4: h.tensor_tensor(a4[:], PWi[:, m - 1, :], lbr[:], ALU.mult), [BPWi, Blbr], [Ba4])
                    V(lambda h, m=m, a3=a3, a4=a4: h.tensor_tensor(PWi[:, m, :], a3[:], a4[:], ALU.add), [Ba3, Ba4], [BPWi])
                BEr, BBEr = newt([128, 16, 15, 16]); BEi, BBEi = newt([128, 16, 15, 16])
                G(lambda h: h.memset(BEr[:], 0.0), [], [BBEr]); G(lambda h: h.memset(BEi[:], 0.0), [], [BBEi])
                for i in range(8):
                    cmul_b(BEr[:, :, i, :], BEi[:, :, i, :], BBEr, BBEi, PWr[:, 7 - i, :], PWi[:, 7 - i, :], BPWr, BPWi,
                           BBr[:], BBi[:], BBBr, BBBi)
                for j in range(8):
                    cmul_b(ceir[:, :, j, :], ceii[:, :, j, :], Bceir, Bceii, PWr[:, j + 1, :], PWi[:, j + 1, :], BPWr, BPWi,
                           CR[:], CI[:], BCR, BCI, negate_i=True)
                ph8, Bph8 = newt([128, 16]); t8, Bt8 = newt([128, 16])
                V(lambda h: h.tensor_single_scalar(t8[:], turns[:], 8.0, ALU.mult), [Bturns], [Bt8])
                frac(ph8[:], Bph8, t8[:], Bt8, 0.0, 16)
                iot, Biot = newt([128, C])
                D(lambda h: h.dma_start(out=iot[:], in_=iota_d[:, 0:C]), [], [Biot])
                TT, BTT = newt([128, 4, C]); FR, BFR = newt([128, 4 * C])
                for g4 in range(4):
                    for gl in range(4):
                        V(lambda h, gl=gl, g4=g4: h.tensor_scalar_mul(TT[:, gl, :], iot[:], ph8[:, g4 * 4 + gl:g4 * 4 + gl + 1]), [Biot, Bph8], [BTT])
                    frac(FR[:], BFR, TT[:].rearrange("p g c -> p (g c)"), BTT, 0.0, 4 * C)
                    A(lambda h, g4=g4: h.activation(sinT[:, g4 * 4:(g4 + 1) * 4, :].rearrange("p g c -> p (g c)"), FR[:], AF.Sin, scale=TWO_PI), [BFR], [BsinT])
                    frac(FR[:], BFR, TT[:].rearrange("p g c -> p (g c)"), BTT, 0.25, 4 * C)
                    A(lambda h, g4=g4: h.activation(cosT[:, g4 * 4:(g4 + 1) * 4, :].rearrange("p g c -> p (g c)"), FR[:], AF.Sin, scale=TWO_PI), [BFR], [BcosT])
                G(lambda h: h.memset(wbr, 0.0), [], [Bwbr]); G(lambda h: h.memset(wbi, 0.0), [], [Bwbi])
                for g in range(32):
                    hf = g // 16; gl = g % 16
                    qs = slice(hf * 64, hf * 64 + 64)
                    pf, Bpf = PF[g % 2]
                    BEr_f = BEr[:].rearrange("p g b h -> p g (b h)"); BEi_f = BEi[:].rearrange("p g b h -> p g (b h)")
                    if g == 0:
                        BE16r, BBE16r = sbt(bs, "BE16r", [128, 16, 240], BF16); BE16i, BBE16i = sbt(bs, "BE16i", [128, 16, 240], BF16)
                        C16r, BC16r = sbt(bs, "C16r", [128, 16, 16], BF16); C16n, BC16n = sbt(bs, "C16n", [128, 16, 16], BF16)
                        V(lambda h: h.tensor_copy(BE16r[:], BEr_f), [BBEr], [BBE16r]); V(lambda h: h.tensor_copy(BE16i[:], BEi_f), [BBEi], [BBE16i])
                        V(lambda h: h.tensor_copy(C16r[:], CR[:]), [BCR], [BC16r]); V(lambda h: h.tensor_copy(C16n[:], CIn[:]), [BCIn], [BC16n])
                    for j in range(8):
                        off = (7 - j) * 16
                        T(lambda h, pf=pf, j=j, qs=qs, gl=gl, off=off: h.matmul(pf[:, j * 16:(j + 1) * 16], BE16r[qs, gl, off:off + 128], C16r[qs, gl, :], start=True, stop=False),
                          [BBE16r, BC16r], [Bpf])
                        T(lambda h, pf=pf, j=j, qs=qs, gl=gl, off=off: h.matmul(pf[:, j * 16:(j + 1) * 16], BE16i[qs, gl, off:off + 128], C16n[qs, gl, :], start=False, stop=True),
                          [BBE16i, BC16n], [Bpf])
                    V(lambda h, pf=pf, g=g: h.scalar_tensor_tensor(toep[:, g, :], identf, DCOL[:, g:g + 1], pf[:, 0:128], ALU.mult, ALU.add),
                      [Bpf, Bcstf, BDCOL], [Btoep])
                    pg, Bpg = PF[2 + g % 2]
                    T(lambda h, pg=pg, qs=qs, gl=gl: h.matmul(pg[:, 0:64], BEr_f[qs, gl, 0:128], cstf[qs, 0, hf * 64:hf * 64 + 64], start=True, stop=True), [BBEr, Bcstf], [Bpg])
                    T(lambda h, pg=pg, qs=qs, gl=gl: h.matmul(pg[:, 64:128], BEi_f[qs, gl, 0:128], cstf[qs, 0, hf * 64:hf * 64 + 64], start=True, stop=True), [BBEi, Bcstf], [Bpg])
                    A(lambda h, pg=pg, g=g, hf=hf: h.copy(wbr[:, g, hf * 64:hf * 64 + 64], pg[:, 0:64]), [Bpg], [Bwbr])
                    A(lambda h, pg=pg, g=g, hf=hf: h.copy(wbi[:, g, hf * 64:hf * 64 + 64], pg[:, 64:128]), [Bpg], [Bwbi])
                dump("toep", toep, Btoep, [128, 32, 128]); dump("wbr", wbr, Bwbr, [128, 32, 128])
                dump("ceir", ceir, Bceir, [128, 16, 8, 16]); dump("cosT", cosT, BcosT, [128, 16, C])
                for n, shp, dt_ in CSPEC:
                    D(lambda h, n=n: h.dma_start(out=scr[n][0], in_=_ct[n][0][:]), [_ct[n][1]], [scr[n][1]])
                fw.barrier()

            SW = max(4 * L, NSP * 4096)
            Q1, BQ1 = sbt(p1, "Q1", [128, SW], BF16)
            Q2, BQ2 = sbt(p1, "Q2", [128, SW], BF16)
            Q3, BQ3 = sbt(p1, "Q3", [128, SW], BF16)
            S1, BS1 = sbt(p1, "S1", [128, SW], BF16)
            XSa, BXSa = sbt(p1, "XSa", [128, 4 * L], BF16)
            XSb, BXSb = sbt(p1, "XSb", [128, max(4 * L, 8192)], BF16)
            sbT = XSa[:].rearrange("p (t l) -> p t l", t=4); BsbT = BXSa
            xsTa = XSa[:].rearrange("p (c l) -> p c l", c=4); xsTb = XSb[:, 0:4 * L].rearrange("p (c l) -> p c l", c=4)
            BxsT2 = [BXSa, BXSb]
            wo = XSb[:, 0:8192].rearrange("p (c n) -> p c n", c=8); Bwo = BXSb

            def xsT(dc, sl):
                return (xsTa if dc < 4 else xsTb)[:, dc % 4, sl]
            stat, Bstat = sbt(p1, "stat", [128, 4, NT])
            rs, Brs = sbt(p1, "rs", [128, 4, NT])
            junk, Bjunk = sbt(p1, "junk", [128, 1024], BF16)

            for b in range(NB):
                qT = Q1[:, 0:4 * L].rearrange("p (t l) -> p t l", t=4)
                kTr = Q2[:, 0:4 * L].rearrange("p (t l) -> p t l", t=4)
                vrev = Q3[:, 0:4 * L].rearrange("p (n f) -> p n f", f=512)
                vTr = S1[:, 0:4 * L].rearrange("p (t l) -> p t l", t=4)
                u_tm = S1[:, 0:NSP * 4096].rearrange("p (s g f) -> p s g f", g=32, f=128)
                u_tmw = S1[:, 0:NSP * 4096].rearrange("p (s g j h) -> p s g j h", g=32, j=8, h=16)
                U2 = Q1[:, 0:32 * C].rearrange("p (g c) -> p g c", g=32)
                SPr = Q2[:, 0:16 * C].rearrange("p (g c) -> p g c", g=16)
                SPi = Q2[:, 16 * C:32 * C].rearrange("p (g c) -> p g c", g=16)
                G2 = S1[:, 0:32 * C].rearrange("p (g c) -> p g c", g=32)
                y_tm = Q1[:, 0:NSP * 4096].rearrange("p (s j f) -> p s j f", j=8, f=512)
                yT = Q2[:, 0:4 * L].rearrange("p (t l) -> p t l", t=4)
                ssmT = Q3[:, 0:4 * L].rearrange("p (t l) -> p t l", t=4)

                with ExitStack() as sa:
                    xb = [sbt(sa, f"xb{i}", [128, 1024]) for i in range(4)]
                    wsts = [sbt(sa, f"wst{k}", [128, 8, 128]) for k in range(2)]
                    wgrps = [sbt(sa, f"wgrp{k}", [128, 8, 512], BF16) for k in range(4)]
                    xss = [sbt(sa, f"xs{i}", [128, 1024], BF16) for i in range(2)]
                    sqfs = [sbt(sa, f"sqf{k}", [128, 512]) for k in range(2)]
                    kns = [sbt(sa, f"kn{k}", [128, 512]) for k in range(2)]
                    kn16s = [sbt(sa, f"kn16_{k}", [128, 512], BF16) for k in range(3)]
                    qkst, _ = sbt(sa, "qkst", [128, 3, 2 * NT * 8])
                    Bqk = [[fw.buf(f"qkst{b}_{i}_{r}") for r in range(3)] for i in range(2 * NT)]
                    def load_wgrp(c0):
                        wgrp, Bwgrp = wgrps[c0 // 512]
                        for hh in range(4):
                            wst, Bwst = wsts[hh % 2]
                            fw.dma("pool", lambda h, hh=hh, wst=wst: h.dma_start(out=wst[:], in_=w_in_d[:, c0 + hh * 128:c0 + (hh + 1) * 128].rearrange("(c p) n -> p c n", p=128)), [], [Bwst])
                            G(lambda h, hh=hh, wst=wst: h.tensor_tensor(wgrp[:, :, hh * 128:(hh + 1) * 128], wst[:], vecs[:, 0:8].unsqueeze(2).to_broadcast([128, 8, 128]), ALU.mult),
                              [Bwst, Bvecs], [Bwgrp])

                    G(lambda h: h.memset(stat[:], 0.0), [], [Bstat])
                    st1, _ = sbt(sa, "st1", [128, 3, NT])
                    Bst1 = [fw.buf(f"st1_{b}_{i}") for i in range(NT)]
                    G(lambda h: h.memset(st1[:], 0.0), [], Bst1)
                    for c0 in (0, 512, 1024, 1536):
                        load_wgrp(c0)
                    def a1_s1(n):
                        xt, Bxt = xb[n % 4]
                        xs, Bxs = xss[n % 2]
                        D(lambda h: h.dma_start(out=xt[:], in_=x_d[b, n * 128:(n + 1) * 128, :]), [], [Bxt])
                        A(lambda h: h.activation(junk[:], xt[:], AF.Square, accum_out=st1[:, 0, n:n + 1]), [Bxt], [Bjunk, Bst1[n]])
                        rstd_from_ss(st1[:, 0, n:n + 1], Bst1[n], st1[:, 1, n:n + 1], Bst1[n], 1.0 / 1024, st1[:, 2, n:n + 1], Bst1[n])
                        V(lambda h: h.tensor_scalar_mul(xs[:], xt[:], st1[:, 1, n:n + 1]), [Bxt, Bst1[n]], [Bxs])

                    def a1_s2(n):
                        xs, Bxs = xss[n % 2]
                        ph, Bph = PH[n % 2]
                        for dc in range(8):
                            T(lambda h, dc=dc: h.transpose(ph[:, dc * 128:(dc + 1) * 128], xs[:, dc * 128:(dc + 1) * 128], identb), [Bxs, Bcstb], [Bph])
                        A(lambda h: h.copy(xsTa[:, :, n * 128:(n + 1) * 128], ph[:, 0:512].rearrange("p (c t) -> p c t", c=4)), [Bph], [BXSa])
                        A(lambda h: h.copy(xsTb[:, :, n * 128:(n + 1) * 128], ph[:, 512:1024].rearrange("p (c t) -> p c t", c=4)), [Bph], [BXSb])

                    for n in range(NT):
                        a1_s1(n)
                        if n >= 1:
                            a1_s2(n - 1)
                    a1_s2(NT - 1)

                    qk_units = [(which, n) for which in range(2) for n in range(NT)]

                    def qk_s1(u):
                        which, n = qk_units[u]
                        wgrp, Bwgrp = wgrps[which]
                        pf, Bpf = PF[u % 3]
                        sqf, Bsqf = sqfs[u % 2]; kn, Bkn = kns[u % 2]; kn16, Bkn16 = kn16s[u % 3]
                        for dc in range(8):
                            T(lambda h, dc=dc: h.matmul(pf[:], xsT(dc, slice(n * 128, (n + 1) * 128)), wgrp[:, dc, :], start=(dc == 0), stop=(dc == 7)),
                              BxsT2 + [Bwgrp], [Bpf])
                        A(lambda h: h.activation(sqf[:], pf[:], AF.Square), [Bpf], [Bsqf])
                        c0 = (which * NT + n) * 8
                        Bq = Bqk[which * NT + n]
                        V(lambda h: h.tensor_reduce(qkst[:, 0, c0:c0 + 8], sqf[:].rearrange("p (a d) -> p a d", d=64), mybir.AxisListType.X, ALU.add), [Bsqf], [Bq[0]])
                        A(lambda h: h.activation(qkst[:, 1, c0:c0 + 8], qkst[:, 0, c0:c0 + 8], AF.Sqrt, bias=epsT[:], scale=1.0 / 64), [Bq[0], Beps], [Bq[1]])
                        V(lambda h: h.reciprocal(qkst[:, 2, c0:c0 + 8], qkst[:, 1, c0:c0 + 8]), [Bq[1]], [Bq[2]])
                        V(lambda h: h.tensor_tensor(kn16[:].rearrange("p (a d) -> p a d", d=64), pf[:].rearrange("p (a d) -> p a d", d=64),
                                                    qkst[:, 2, c0:c0 + 8].unsqueeze(2).to_broadcast([128, 8, 64]), ALU.mult), [Bpf, Bq[2]], [Bkn16])

                    def qk_s2(u):
                        which, n = qk_units[u]
                        kn16, Bkn16 = kn16s[u % 3]
                        ph, Bph = PH[u % 2]
                        for hp in range(4):
                            T(lambda h, hp=hp: h.transpose(ph[:, hp * 128:(hp + 1) * 128], kn16[:, hp * 128:(hp + 1) * 128], identb), [Bkn16, Bcstb], [Bph])
                        if which == 0:
                            A(lambda h: h.activation(qT[:, :, n * 128:(n + 1) * 128], ph[:, 0:512].rearrange("p (c t) -> p c t", c=4), AF.Copy, scale=vecs[:, 28:29]), [Bph, Bvecs], [BQ1])
                        else:
                            hi = L - n * 128 - 1
                            lo = L - (n + 1) * 128 - 1
                            A(lambda h: h.activation(kTr[:, :, hi:(lo if lo >= 0 else None):-1], ph[:, 0:512].rearrange("p (c t) -> p c t", c=4), AF.Copy, scale=vecs[:, 29:30]), [Bph, Bvecs], [BQ2])

                    for u in range(len(qk_units)):
                        qk_s1(u)
                        if u >= 2:
                            qk_s2(u - 2)
                    qk_s2(len(qk_units) - 2)
                    qk_s2(len(qk_units) - 1)
                    wgrp, Bwgrp = wgrps[2]
                    for hp in range(4):
                        for bk in range(NBK):
                            pf, Bpf = PF[(hp * NBK + bk) % 2]
                            for dc in range(8):
                                T(lambda h, pf=pf, dc=dc, hp=hp, bk=bk: h.matmul(pf[:, 0:BW], wgrp[:, dc, hp * 128:(hp + 1) * 128], xsT(dc, slice(bk * BW, (bk + 1) * BW)),
                                                                              start=(dc == 0), stop=(dc == 7)), [Bwgrp] + BxsT2, [Bpf])
                            lo = L - (bk + 1) * BW
                            stop = lo - 1 if lo > 0 else None
                            A(lambda h, pf=pf, hp=hp, lo=lo, stop=stop: h.copy(vTr[:, hp, lo + BW - 1:stop:-1], pf[:, 0:BW]), [Bpf], [BS1])
                    for sbk in range(NT):
                        ph, Bph = PH[sbk % 2]
                        for hp in range(4):
                            T(lambda h, ph=ph, hp=hp, sbk=sbk: h.transpose(ph[:, hp * 128:(hp + 1) * 128], vTr[:, hp, sbk * 128:(sbk + 1) * 128], identb), [BS1, Bcstb], [Bph])
                        V(lambda h, ph=ph, sbk=sbk: h.tensor_copy(vrev[:, sbk, :], ph[:, 0:512]), [Bph], [BQ3])
                    wgrp, Bwgrp = wgrps[3]
                    for sp in range(NSP):
                        for j in range(8):
                            pf, Bpf = PF[(sp * 8 + j) % 2]
                            for dc in range(8):
                                T(lambda h, pf=pf, dc=dc, sp=sp, j=j: h.matmul(pf[0:CS, :], xsT(dc, slice(sp * SPT + j, (sp + 1) * SPT, 8)), wgrp[:, dc, :],
                                                                            start=(dc == 0), stop=(dc == 7)), BxsT2 + [Bwgrp], [Bpf])
                            A(lambda h, pf=pf, sp=sp, j=j: h.copy(u_tmw[0:CS, sp, :, j, :], pf[0:CS, :].rearrange("p (g h) -> p g h", h=16)), [Bpf], [BS1])
                    dump(f"qT{b}", qT, BQ1, [128, 4, L]); dump(f"kTr{b}", kTr, BQ2, [128, 4, L]); dump(f"vrev{b}", vrev, BQ3, [128, NT, 512])
                    dump(f"utm{b}", S1[0:CS, 0:NSP * 4096], BS1, [CS, NSP * 4096])
                    fw.barrier()

                with ExitStack() as sj:
                    gams = [sbt(sj, f"gam{k}", [128, L]) for k in range(2)]
                    Pcs = [sbt(sj, f"Pc{k}", [128, L + 1]) for k in range(2)]
                    a16s = [sbt(sj, f"a16_{k}", [128, L], BF16) for k in range(3)]
                    aTs = [sbt(sj, f"aT_{k}", [128, NT, 128], BF16) for k in range(2)]
                    sbs, Bsbs = sbt(sj, "sbs", [128, 512], BF16)
                    wst2s = [sbt(sj, f"wst2_{k}", [128, 8, 128]) for k in range(2)]
                    for q4 in range(8):
                        wst2, Bwst2 = wst2s[q4 % 2]
                        D(lambda h, q4=q4: h.dma_start(out=wst2[:], in_=wout_d[:, q4 * 128:(q4 + 1) * 128].rearrange("(c p) n -> p c n", p=128)), [], [Bwst2])
                        G(lambda h, q4=q4: h.tensor_tensor(wo[:, :, q4 * 128:(q4 + 1) * 128], wst2[:], vecs[:, 16:24].unsqueeze(2).to_broadcast([128, 8, 128]), ALU.mult),
                          [Bwst2, Bvecs], [Bwo])
                    for Pc_, BPc_ in Pcs:
                        V(lambda h, Pc_=Pc_: h.memset(Pc_[:, 0:1], 1.0), [], [BPc_])
                    osb, Bosb = PF[4]
                    units = [(i, hd) for i in range(NT) for hd in range(8)]

                    def stage1(n):
                        i, hd = units[n]
                        a16, Ba16 = a16s[n % 3]
                        gam, Bgam = gams[n % 2]
                        Pc, BPc = Pcs[n % 2]
                        S = (i + 1) * 128
                        k0 = L - S
                        hp = hd // 2; hs = slice((hd % 2) * 64, (hd % 2) * 64 + 64)
                        npc = (S + 511) // 512
                        for pc in range(npc):
                            w = min(512, S - pc * 512)
                            pf, Bpf = PF[pc % 4]
                            if pc == 0:
                                T(lambda h, pf=pf: h.matmul(pf[:, 0:128], identb, maskb, start=True, stop=False), [Bcstb], [Bpf])
                                T(lambda h, pf=pf: h.matmul(pf[:, 0:128], qT[hs, hp, i * 128:(i + 1) * 128], kTr[hs, hp, k0:k0 + 128], start=False, stop=True),
                                  [BQ1, BQ2], [Bpf])
                                if w > 128:
                                    T(lambda h, pf=pf, w=w: h.matmul(pf[:, 128:w], qT[hs, hp, i * 128:(i + 1) * 128], kTr[hs, hp, k0 + 128:k0 + w], start=True, stop=True),
                                      [BQ1, BQ2], [Bpf])
                            else:
                                T(lambda h, pf=pf, w=w, pc=pc: h.matmul(pf[:, 0:w], qT[hs, hp, i * 128:(i + 1) * 128], kTr[hs, hp, k0 + pc * 512:k0 + pc * 512 + w], start=True, stop=True),
                                  [BQ1, BQ2], [Bpf])
                            A(lambda h, pf=pf, pc=pc, w=w: h.activation(gam[:, pc * 512:pc * 512 + w], pf[:, 0:w], AF.Sigmoid, scale=-0.125), [Bpf], [Bgam])
                        V(lambda h: h.tensor_tensor_scan(Pc[:, 1:S + 1], gam[:, 0:S], gam[:, 0:S], 1.0, ALU.mult, ALU.min), [Bgam], [BPc])
                        V(lambda h: h.tensor_tensor(a16[:, 0:S], Pc[:, 0:S], Pc[:, 1:S + 1], ALU.subtract), [BPc], [Ba16])

                    def stage2(n):
                        i, hd = units[n]
                        a16, Ba16 = a16s[n % 3]
                        aT, BaT = aTs[n % 2]
                        for b4 in range(0, i + 1, 4):
                            nb4 = min(4, i + 1 - b4)
                            ph, Bph = PH[(b4 // 4) % 2]
                            for q in range(nb4):
                                T(lambda h, ph=ph, q=q, b4=b4: h.transpose(ph[:, q * 128:(q + 1) * 128], a16[:, (b4 + q) * 128:(b4 + q + 1) * 128], identb), [Ba16, Bcstb], [Bph])
                            A(lambda h, ph=ph, b4=b4, nb4=nb4: h.copy(aT[:, b4:b4 + nb4, :], ph[:, 0:nb4 * 128].rearrange("p (q t) -> p q t", t=128)), [Bph], [BaT])
                        for blk in range(i + 1):
                            T(lambda h, blk=blk: h.matmul(osb[:, hd * 64:(hd + 1) * 64], aT[:, blk, :], vrev[:, NT - 1 - i + blk, hd * 64:(hd + 1) * 64],
                                                          start=(blk == 0), stop=(blk == i)), [BaT, BQ3], [Bosb])
                        if hd == 7:
                            A(lambda h: h.copy(sbs[:], osb[:]), [Bosb], [Bsbs])
                            A(lambda h: h.activation(junk[:, 0:512], osb[:], AF.Square, accum_out=stat[:, 1, i:i + 1]), [Bosb], [Bjunk, Bstat])
                            ph, Bph = PH[2]
                            for hp in range(4):
                                T(lambda h, ph=ph, hp=hp: h.transpose(ph[:, hp * 128:(hp + 1) * 128], sbs[:, hp * 128:(hp + 1) * 128], identb), [Bsbs, Bcstb], [Bph])
                            V(lambda h, ph=ph: h.tensor_copy(sbT[:, :, i * 128:(i + 1) * 128], ph[:, 0:512].rearrange("p (c t) -> p c t", c=4)), [Bph], [BsbT])

                    for n in range(len(units)):
                        stage1(n)
                        if n >= 2:
                            stage2(n - 2)
                    stage2(len(units) - 2)
                    stage2(len(units) - 1)
                    dump(f"sbT{b}", sbT, BsbT, [128, 4, L])
                    fw.barrier()

                with ExitStack() as ss_:
                    _ct = alloc_consts(ss_)
                    for n, shp, dt_ in CSPEC:
                        D(lambda h, n=n, _ct=_ct: h.dma_start(out=_ct[n][0][:], in_=scr[n][0]), [scr[n][1]], [_ct[n][1]])
                    toep = _ct["toep"][0][:].rearrange("p (g f) -> p g f", g=32); Btoep = _ct["toep"][1]
                    wbr = _ct["wbr"][0][:].rearrange("p (g f) -> p g f", g=32); Bwbr = _ct["wbr"][1]
                    wbi = _ct["wbi"][0][:].rearrange("p (g f) -> p g f", g=32); Bwbi = _ct["wbi"][1]
                    ceir = _ct["ceir"][0][:].rearrange("p (g j o) -> p g j o", g=16, j=8); Bceir = _ct["ceir"][1]
                    ceii = _ct["ceii"][0][:].rearrange("p (g j o) -> p g j o", g=16, j=8); Bceii = _ct["ceii"][1]
                    cosT = _ct["cosT"][0][:].rearrange("p (g c) -> p g c", g=16); BcosT = _ct["cosT"][1]
                    sinT = _ct["sinT"][0][:].rearrange("p (g c) -> p g c", g=16); BsinT = _ct["sinT"][1]
                    BU2 = [fw.buf(f"U2_{g}") for g in range(32)]
                    BG2 = [fw.buf(f"G2_{g}") for g in range(32)]
                    BSPr = [fw.buf(f"SPr_{g}") for g in range(16)]; BSPi = [fw.buf(f"SPi_{g}") for g in range(16)]
                    tset = [[sbt(ss_, f"st{k}_{q}", [128, C]) for q in range(8)] for k in range(2)]
                    gset = [[sbt(ss_, f"gt{k}_{q}", [128, C]) for q in range(3)] for k in range(4)]
                    gT, BgT = sbt(ss_, "gT", [128, 512], BF16); sq4 = [sbt(ss_, f"sq4_{k}", [128, 4, 512], BF16) for k in range(1)]
                    for g0 in range(0, 32, 4):
                        ph, Bph = PH[(g0 // 4) % 2]
                        for gg in range(4):
                            g = g0 + gg
                            for sp in range(NSP):
                                T(lambda h, ph=ph, gg=gg, g=g, sp=sp: h.transpose(ph[:, gg * C + sp * CS:gg * C + (sp + 1) * CS], u_tm[0:CS, sp, g, :], cstb[0:CS, 0, 0:CS]),
                                  [BS1, Bcstb], [Bph])
                        A(lambda h, ph=ph, g0=g0: h.copy(U2[:, g0:g0 + 4, :], ph[:, 0:4 * C].rearrange("p (g c) -> p g c", g=4)), [Bph], BU2[g0:g0 + 4] + ([BQ1] if g0 == 0 else []))
                    V(lambda h: h.memset(SPr[:, :, 0:1], 0.0), [], [BQ2] + BSPr); V(lambda h: h.memset(SPi[:, :, 0:1], 0.0), [], [BQ2] + BSPi)
                    for gl in range(16):
                        pr, Bpr = PF[2 * (gl % 2)]; pi_, Bpi = PF[2 * (gl % 2) + 1]
                        (mr, Bmr), (mi, Bmi), (c1, Bc1), (c2, Bc2), (zr, Bzr), (zi, Bzi), (c3, Bc3), (c4, Bc4) = tset[gl % 2]
                        (c5, Bc5), (c6, Bc6) = (c1, Bc1), (c2, Bc2)
                        for (pp, Bpp, wb, Bwb) in ((pr, Bpr, wbr, Bwbr), (pi_, Bpi, wbi, Bwbi)):
                            T(lambda h, pp=pp, wb=wb, gl=gl: h.matmul(pp[:, 0:C], wb[:, gl, :], U2[:, gl, :], start=True, stop=False), [Bwb, BU2[gl]], [Bpp])
                            T(lambda h, pp=pp, wb=wb, gl=gl: h.matmul(pp[:, 0:C], wb[:, gl + 16, :], U2[:, gl + 16, :], start=False, stop=True), [Bwb, BU2[gl + 16]], [Bpp])
                        V(lambda h, gl=gl: h.tensor_tensor(c1[:], pr[:, 0:C], cosT[:, gl, :], ALU.mult), [Bpr, BcosT], [Bc1])
                        V(lambda h, gl=gl: h.tensor_tensor(c2[:], pi_[:, 0:C], sinT[:, gl, :], ALU.mult), [Bpi, BsinT], [Bc2])
                        G(lambda h: h.tensor_tensor(mr[:], c1[:], c2[:], ALU.add), [Bc1, Bc2], [Bmr])
                        V(lambda h, gl=gl, c3=c3, pi_=pi_: h.tensor_tensor(c3[:], pi_[:, 0:C], cosT[:, gl, :], ALU.mult), [Bpi, BcosT], [Bc3])
                        V(lambda h, gl=gl, c4=c4, pr=pr: h.tensor_tensor(c4[:], pr[:, 0:C], sinT[:, gl, :], ALU.mult), [Bpr, BsinT], [Bc4])
                        G(lambda h, mi=mi, c3=c3, c4=c4: h.tensor_tensor(mi[:], c3[:], c4[:], ALU.subtract), [Bc3, Bc4], [Bmi])
                        V(lambda h, gl=gl: h.tensor_tensor_scan(zr[:], rho[:, gl:gl + 1].to_broadcast([128, C]), mr[:], 0.0, ALU.mult, ALU.add), [Brho, Bmr], [Bzr])
                        V(lambda h, gl=gl: h.tensor_tensor_scan(zi[:], rho[:, gl:gl + 1].to_broadcast([128, C]), mi[:], 0.0, ALU.mult, ALU.add), [Brho, Bmi], [Bzi])
                        V(lambda h, gl=gl: h.tensor_tensor(c5[:, 0:C - 1], zr[:, 0:C - 1], cosT[:, gl, 0:C - 1], ALU.mult), [Bzr, BcosT], [Bc5])
                        V(lambda h, gl=gl: h.tensor_tensor(c6[:, 0:C - 1], zi[:, 0:C - 1], sinT[:, gl, 0:C - 1], ALU.mult), [Bzi, BsinT], [Bc6])
                        G(lambda h, gl=gl: h.tensor_tensor(SPr[:, gl, 1:C], c5[:, 0:C - 1], c6[:, 0:C - 1], ALU.subtract), [Bc5, Bc6], [BSPr[gl]])
                        V(lambda h, gl=gl, c3=c3, zr=zr: h.tensor_tensor(c3[:, 0:C - 1], zr[:, 0:C - 1], sinT[:, gl, 0:C - 1], ALU.mult), [Bzr, BsinT], [Bc3])
                        V(lambda h, gl=gl, c4=c4, zi=zi: h.tensor_tensor(c4[:, 0:C - 1], zi[:, 0:C - 1], cosT[:, gl, 0:C - 1], ALU.mult), [Bzi, BcosT], [Bc4])
                        G(lambda h, gl=gl, c3=c3, c4=c4: h.tensor_tensor(SPi[:, gl, 1:C], c3[:, 0:C - 1], c4[:, 0:C - 1], ALU.add), [Bc3, Bc4], [BSPi[gl]])
                    ceir_f = ceir.rearrange("p g j o -> p g (j o)"); ceii_f = ceii.rearrange("p g j o -> p g (j o)")
                    def s3_a(g):
                        hf = g // 16; gl = g % 16; qs = slice(hf * 64, hf * 64 + 64)
                        py, Bpy = PF[g % 4]
                        (gsq, Bgsq), (gw, Bgw), (gs_, Bgs) = gset[g % 4]
                        T(lambda h: h.matmul(py[:, 0:C], toep[:, g, :], U2[:, g, :], start=True, stop=False), [Btoep, BU2[g]], [Bpy])
                        T(lambda h: h.matmul(py[:, 0:C], ceir_f[qs, gl, :], SPr[qs, gl, :], start=False, stop=False), [Bceir, BSPr[gl]], [Bpy])
                        T(lambda h: h.matmul(py[:, 0:C], ceii_f[qs, gl, :], SPi[qs, gl, :], start=False, stop=True), [Bceii, BSPi[gl]], [Bpy])
                        A(lambda h: h.activation(gsq[:], py[:, 0:C], AF.Square), [Bpy], [Bgsq])
                        V(lambda h: h.tensor_scalar(gw[:], gsq[:], 0.044715, 1.0, ALU.mult, ALU.add), [Bgsq], [Bgw])
                        V(lambda h: h.tensor_tensor(gw[:], gw[:], py[:, 0:C], ALU.mult), [Bgw, Bpy], [Bgw])

                    def s3_b(g):
                        py, Bpy = PF[g % 4]
                        (gsq, Bgsq), (gw, Bgw), (gs_, Bgs) = gset[g % 4]
                        A(lambda h: h.activation(gs_[:], gw[:], AF.Sigmoid, scale=1.5957691216), [Bgw], [Bgs])
                        V(lambda h: h.tensor_tensor(G2[:, g, :], gs_[:], py[:, 0:C], ALU.mult), [Bgs, Bpy], [BG2[g]] + ([BS1] if g == 0 else []))

                    for g in range(32):
                        s3_a(g)
                        if g >= 2:
                            s3_b(g - 2)
                    s3_b(30)
                    s3_b(31)
                    for sp in range(NSP):
                        for g0 in range(0, 32, 8):
                            ph, Bph = PH[(g0 // 8) % 2]
                            for gg in range(8):
                                T(lambda h, ph=ph, gg=gg, g0=g0, sp=sp: h.transpose(ph[0:CS, gg * 128:(gg + 1) * 128], G2[:, g0 + gg, sp * CS:(sp + 1) * CS], identb), [BG2[g0 + gg], Bcstb], [Bph])
                            A(lambda h, ph=ph, g0=g0, sp=sp: h.copy(y_tm[0:CS, sp, :, g0 * 16:(g0 + 8) * 16].rearrange("p j (g o) -> p j g o", o=16),
                                                                    ph[0:CS, :].rearrange("p (g j o) -> p j g o", g=8, j=8)), [Bph], [BQ1] + (BU2 if (sp == 0 and g0 == 0) else []))
                    for sp in range(NSP):
                        for tg in range(4):
                            ph, Bph = PH[tg % 2]
                            for j in range(8):
                                T(lambda h, ph=ph, j=j, sp=sp, tg=tg: h.transpose(ph[:, j * CS:(j + 1) * CS], y_tm[0:CS, sp, j, tg * 128:(tg + 1) * 128], cstb[0:CS, 0, 0:CS]), [BQ1, Bcstb], [Bph])
                            V(lambda h, ph=ph, sp=sp, tg=tg: h.tensor_copy(yT[:, tg, sp * SPT:(sp + 1) * SPT].rearrange("p (c j) -> p c j", j=8),
                                                                           ph[:, 0:8 * CS].rearrange("p (j c) -> p c j", j=8)), [Bph], [BQ2] + ((BSPr + BSPi) if (sp == 0 and tg == 0) else []))
                    dump(f"yT{b}", yT, BQ2, [128, 4, L])
                    for bk in range(NBK):
                        nblk = BW // 128
                        pss, Bpss = PF[4]
                        for co in range(4):
                            pf, Bpf = PF[co % 2]
                            for tg in range(4):
                                T(lambda h, pf=pf, tg=tg, co=co, bk=bk: h.matmul(pf[:, 0:BW], wglu[:, tg, co * 128:(co + 1) * 128], yT[:, tg, bk * BW:(bk + 1) * BW], start=(tg == 0), stop=(tg == 3)),
                                  [Bwglu, BQ2], [Bpf])
                            A(lambda h, pf=pf, co=co: h.activation(gT[:, 0:BW], pf[:, 0:BW], AF.Sigmoid, bias=vecs[:, 24 + co:25 + co]), [Bpf, Bvecs], [BgT])
                            V(lambda h, co=co, bk=bk: h.tensor_tensor(ssmT[:, co, bk * BW:(bk + 1) * BW], yT[:, co, bk * BW:(bk + 1) * BW], gT[:, 0:BW], ALU.mult), [BQ2, BgT], [BQ3])
                            sqc, Bsqc = sq4[0]
                            A(lambda h, co=co, bk=bk, sqc=sqc: h.activation(sqc[:, co, 0:BW], ssmT[:, co, bk * BW:(bk + 1) * BW], AF.Square), [BQ3], [Bsqc])
                        for tb in range(nblk):
                            for co in range(4):
                                T(lambda h, tb=tb, co=co, sqc=sqc: h.matmul(pss[:, tb:tb + 1], sqc[:, co, tb * 128:(tb + 1) * 128], onecol[:], start=(co == 0), stop=(co == 3)), [Bsqc, Bonecol], [Bpss])
                        V(lambda h, bk=bk, nblk=nblk: h.tensor_copy(stat[:, 2, bk * nblk:(bk + 1) * nblk], pss[:, 0:nblk]), [Bpss], [Bstat])
                    dump(f"ssmT{b}", ssmT, BQ3, [128, 4, L])
                    fw.barrier()

                with ExitStack() as sk:
                    xb = [sbt(sk, f"xbk{i}", [128, 1024]) for i in range(4)]
                    hb = [sbt(sk, f"hb{i}", [128, 1024]) for i in range(2)]
                    rstd_from_ss(stat[:, 1, :], Bstat, rs[:, 1, :], Brs, 1.0 / 512, stat[:, 3, :], Bstat)
                    rstd_from_ss(stat[:, 2, :], Bstat, rs[:, 2, :], Brs, 1.0 / 512, stat[:, 3, :], Bstat)
                    for n in range(NT):
                        xt, Bxt = xb[n % 4]
                        D(lambda h, xt=xt, n=n: h.dma_start(out=xt[:], in_=x_d[b, n * 128:(n + 1) * 128, :]), [], [Bxt])
                        for hf in range(2):
                            pa, Bpa = PF[hf]; ps_, Bps = PF[2 + hf]
                            for ct in range(4):
                                T(lambda h, pa=pa, ct=ct, n=n, hf=hf: h.matmul(pa[:], sbT[:, ct, n * 128:(n + 1) * 128], wo[:, ct, hf * 512:(hf + 1) * 512], start=(ct == 0), stop=(ct == 3)),
                                  [BsbT, Bwo], [Bpa])
                            for ct in range(4):
                                T(lambda h, ps_=ps_, ct=ct, n=n, hf=hf: h.matmul(ps_[:], ssmT[:, ct, n * 128:(n + 1) * 128], wo[:, 4 + ct, hf * 512:(hf + 1) * 512], start=(ct == 0), stop=(ct == 3)),
                                  [BQ3, Bwo], [Bps])
                        ht, Bht = hb[n % 2]
                        for hf in range(2):
                            pa, Bpa = PF[hf]; ps_, Bps = PF[2 + hf]
                            V(lambda h, pa=pa, xt=xt, ht=ht, n=n, hf=hf: h.scalar_tensor_tensor(ht[:, hf * 512:(hf + 1) * 512], pa[:], rs[:, 1, n:n + 1], xt[:, hf * 512:(hf + 1) * 512], ALU.mult, ALU.add),
                              [Bpa, Brs, Bxt], [Bht])
                            V(lambda h, ps_=ps_, ht=ht, n=n, hf=hf: h.scalar_tensor_tensor(ht[:, hf * 512:(hf + 1) * 512], ps_[:], rs[:, 2, n:n + 1], ht[:, hf * 512:(hf + 1) * 512], ALU.mult, ALU.add),
                              [Bps, Brs, Bht], [Bht])
                        fw.dma("pool", lambda h, ht=ht, n=n: h.dma_start(out=out_d[b, n * 128:(n + 1) * 128, :], in_=ht[:]), [Bht], [BoutBlk[b * NT + n]], track=Bht)
                        dump(f"h{b}_{n}", ht[:], Bht, [128, 1024])
                    dump(f"rs{b}", rs[:], Brs, [128, 4, NT]); dump(f"stat{b}", stat[:], Bstat, [128, 4, NT])
                    fw.barrier()
            fw.barrier()

        with ExitStack() as p2:
            w1, Bw1 = sbt(p2, "w1", [128, 8, 4096], BF16)
            w2, Bw2 = sbt(p2, "w2", [128, 32, 1024], BF16)
            wscope = ExitStack()
            wsas = [sbt(wscope, f"wsa{k}", [128, 8, 256]) for k in range(2)]
            for q in range(16):
                wsa, Bwsa = wsas[q % 2]
                D(lambda h, q=q, wsa=wsa: h.dma_start(out=wsa[:], in_=w1_d[:, q * 256:(q + 1) * 256].rearrange("(c p) n -> p c n", p=128)), [], [Bwsa])
                (G if q % 2 == 0 else V)(lambda h, q=q, wsa=wsa: h.tensor_tensor(w1[:, :, q * 256:(q + 1) * 256], wsa[:], vecs[:, 8:16].unsqueeze(2).to_broadcast([128, 8, 256]), ALU.mult), [Bwsa, Bvecs], [Bw1])
            for q in range(16):
                wsa, Bwsa = wsas[q % 2]
                D(lambda h, q=q, wsa=wsa: h.dma_start(out=wsa[:].rearrange("p c n -> p (c n)").rearrange("p (c n) -> p c n", c=2), in_=w2_d[q * 256:(q + 1) * 256, :].rearrange("(c p) n -> p c n", p=128)), [], [Bwsa])
                (V if q % 2 == 0 else G)(lambda h, q=q, wsa=wsa: h.tensor_copy(w2[:, q * 2:(q + 1) * 2, :], wsa[:].rearrange("p c n -> p (c n)").rearrange("p (c n) -> p c n", c=2)), [Bwsa], [Bw2])
            fw.barrier()
            wscope.close()
            TBG = min(4, NT)
            NG = NB * NT // TBG
            hld = [sbt(p2, f"hld{i}", [128, 1024]) for i in range(2)]
            hrs = [sbt(p2, f"hrs{i}", [128, 1024]) for i in range(2)]
            hs, Bhs = sbt(p2, "hs", [128, 1024], BF16)
            hsTs = [sbt(p2, f"hsT{i}", [128, 8, TBG * 128], BF16) for i in range(2)]
            aT2, BaT2 = sbt(p2, "aT2", [128, 32, TBG * 128], BF16)
            rl, Brl = sbt(p2, "rl", [128, TBG * 128], BF16)
            obs = [sbt(p2, f"ob{i}", [128, 1024]) for i in range(2)]
            st2, Bst2_ = sbt(p2, "st2", [128, 3, NG * TBG])
            Bst2 = [fw.buf(f"st2_{i}") for i in range(NG * TBG)]
            NW = TBG * 128
            G(lambda h: h.memset(st2[:], 0.0), [], Bst2)

            def mlp_prep(grp):
                hsT, BhsT = hsTs[grp % 2]
                for tb in range(TBG):
                    blk = grp * TBG + tb
                    bb, n = blk // NT, blk % NT
                    ht, Bht = hld[blk % 2]
                    Bs = Bst2[blk]
                    D(lambda h: h.dma_start(out=ht[:], in_=out_d[bb, n * 128:(n + 1) * 128, :]), [BoutBlk[blk]], [Bht])
                    A(lambda h: h.activation(hs[:], ht[:], AF.Square, accum_out=st2[:, 0, blk:blk + 1]), [Bht], [Bhs, Bs])
                    rstd_from_ss(st2[:, 0, blk:blk + 1], Bs, st2[:, 1, blk:blk + 1], Bs, 1.0 / 1024, st2[:, 2, blk:blk + 1], Bs)
                    V(lambda h: h.tensor_scalar_mul(hs[:], ht[:], st2[:, 1, blk:blk + 1]), [Bht, Bs], [Bhs])
                    ph, Bph = PH[tb % 2]
                    for dc in range(8):
                        T(lambda h, dc=dc: h.transpose(ph[:, dc * 128:(dc + 1) * 128], hs[:, dc * 128:(dc + 1) * 128], identb), [Bhs, Bcstb], [Bph])
                    A(lambda h: h.copy(hsT[:, :, tb * 128:(tb + 1) * 128], ph[:].rearrange("p (c t) -> p c t", c=8)), [Bph], [BhsT])

            def mlp_main(grp):
                hsT, BhsT = hsTs[grp % 2]
                for ht_ in range(32):
                    pf, Bpf = PF[ht_ % 2]
                    for dc in range(8):
                        T(lambda h, dc=dc: h.matmul(pf[:, 0:NW], w1[:, dc, ht_ * 128:(ht_ + 1) * 128], hsT[:, dc, :], start=(dc == 0), stop=(dc == 7)), [Bw1, BhsT], [Bpf])
                    A(lambda h: h.activation(rl[:], pf[:, 0:NW], AF.Relu), [Bpf], [Brl])
                    if ht_ % 2 == 0:
                        V(lambda h: h.tensor_tensor(aT2[:, ht_, :], rl[:], rl[:], ALU.mult), [Brl], [BaT2])
                    else:
                        G(lambda h: h.tensor_tensor(aT2[:, ht_, :], rl[:], rl[:], ALU.mult), [Brl], [BaT2])
                for tb in range(TBG):
                    blk = grp * TBG + tb
                    bb, n = blk // NT, blk % NT
                    hr, Bhr = hrs[blk % 2]
                    ob, Bob = obs[blk % 2]
                    D(lambda h: h.dma_start(out=hr[:], in_=out_d[bb, n * 128:(n + 1) * 128, :]), [BoutBlk[blk]], [Bhr])
                    for hf in range(2):
                        po, Bpo = PF[2 + hf]
                        for k in range(32):
                            T(lambda h, k=k: h.matmul(po[:], aT2[:, k, tb * 128:(tb + 1) * 128], w2[:, k, hf * 512:(hf + 1) * 512], start=(k == 0), stop=(k == 31)), [BaT2, Bw2], [Bpo])
                        V(lambda h: h.tensor_tensor(ob[:, hf * 512:(hf + 1) * 512], po[:], hr[:, hf * 512:(hf + 1) * 512], ALU.add), [Bpo, Bhr], [Bob])
                    fw.dma("pool", lambda h: h.dma_start(out=out_d[bb, n * 128:(n + 1) * 128, :], in_=ob[:]), [Bob, Bhr], [BoutBlk[blk]], track=Bob)

            mlp_prep(0)
            for grp in range(NG):
                if grp + 1 < NG:
                    mlp_prep(grp + 1)
                mlp_main(grp)
            e = fw.E["sp"]
            waits = fw._waits(e, [], BoutBlk + list(dbg_outs.values()))
            fw._do(e, waits, None, None)
            fw.barrier()
    return nc


def _consts():
    ident = np.eye(128, dtype=np.float32)
    t = np.arange(128)[:, None]
    s = np.arange(128)[None, :]
    maskb = np.where(s <= 127 - t, -1000.0, 0.0).astype(np.float32)
    bd = np.zeros((128, 128), np.float32)
    bd[:64, :64] = 1.0
    bd[64:, 64:] = 1.0
    iota = np.tile(np.arange(256, dtype=np.float32)[None, :], (128, 1))
    return np.stack([ident, maskb, bd]), iota


_NC_CACHE = {}


def run(inputs, L, NB, ncores, dbg=False):
    key = (L, NB, dbg)
    if key not in _NC_CACHE:
        _NC_CACHE[key] = build_nc(L, NB, dbg)
    nc = _NC_CACHE[key]
    consts, iota = _consts()
    x = np.ascontiguousarray(inputs["x"], dtype=np.float32)
    in_maps = []
    for c in range(ncores):
        m = {k: np.ascontiguousarray(v, dtype=np.float32) for k, v in inputs.items() if k != "x"}
        m["x"] = np.ascontiguousarray(x[c * NB:(c + 1) * NB])
        m["consts"] = consts
        m["iota"] = iota
        in_maps.append(m)
    res = run_bass_kernel_spmd(nc, in_maps, core_ids=list(range(ncores)))
    return res


def kernel(**inputs):
    res = run(inputs, 2048, 2, 8)
    out = np.concatenate([r["out"] for r in res.results], axis=0)
    return out.astype(np.float32)
```

```python
import math
import numpy as np
from contextlib import ExitStack
import concourse.bass as bass
import concourse.mybir as mybir
from concourse.bass_utils import run_bass_kernel_spmd

F32 = mybir.dt.float32
BF16 = mybir.dt.bfloat16
I32 = mybir.dt.int32
AF = mybir.ActivationFunctionType
ALU = mybir.AluOpType
EPS = 1e-6
TWO_PI = 2.0 * math.pi


class Buf:
    __slots__ = ("name", "w", "r", "dsem", "dcnt")

    def __init__(self, name):
        self.name = name
        self.w = None
        self.r = []
        self.dsem = None
        self.dcnt = 0


class Eng:
    def __init__(self, name, sem, same_sync=True):
        self.name = name
        self.sem = sem
        self.cnt = 0
        self.seen = {}
        self.same_sync = same_sync


class FW:
    def __init__(self, nc, stack):
        self.H = {"pe": nc.tensor, "act": nc.scalar, "dve": nc.vector, "pool": nc.gpsimd, "sp": nc.sync}
        self.nc = nc
        self.stack = stack
        self.E = {}
        for n, ss in (("pe", False), ("act", True), ("dve", True), ("pool", True), ("sp", True)):
            sem = stack.enter_context(nc.semaphore("s_" + n))
            self.E[n] = Eng(n, sem, ss)
        self.nbuf = 0
        self.dma_last = {}

    def buf(self, name=None):
        self.nbuf += 1
        return Buf(name or f"b{self.nbuf}")

    def _waits(self, e, reads, writes):
        need = {}

        def add(ev):
            if ev is None:
                return
            sem, val = ev
            if (not e.same_sync) and sem is e.sem:
                return
            k = id(sem)
            if e.seen.get(k, 0) >= val:
                return
            if k not in need or need[k][1] < val:
                need[k] = (sem, val)

        for b in reads:
            add(b.w)
        for b in writes:
            add(b.w)
            for ev in b.r:
                add(ev)
        out = list(need.values())
        for sem, val in out:
            e.seen[id(sem)] = val
        return out

    def _do(self, e, waits, fn, inc):
        h = self.H[e.name]
        for sem, val in waits:
            h.wait_ge(sem, val)
        if fn is not None:
            fn(h).then_inc(inc[0], inc[1])

    def op(self, eng, fn, reads=(), writes=()):
        e = self.E[eng]
        waits = self._waits(e, reads, writes)
        e.cnt += 1
        ev = (e.sem, e.cnt)
        self._do(e, waits, fn, (e.sem, 1))
        for b in reads:
            b.r.append(ev)
        for b in writes:
            b.w = ev
            b.r = []
        return ev

    def dma(self, eng, fn, reads=(), writes=(), track=None):
        e = self.E[eng]
        waits = self._waits(e, reads, writes)
        tb = track or (writes[0] if writes else reads[0])
        if tb.dsem is None:
            tb.dsem = self.stack.enter_context(self.nc.semaphore("d_" + tb.name))
        tb.dcnt += 16
        ev = (tb.dsem, tb.dcnt)
        self.dma_last[id(tb.dsem)] = ev
        self._do(e, waits, fn, (tb.dsem, 16))
        for b in reads:
            b.r.append(ev)
        for b in writes:
            b.w = ev
            b.r = []
        return ev

    def barrier(self):
        evs = [(x.sem, x.cnt) for x in self.E.values() if x.cnt > 0] + list(self.dma_last.values())
        for e in self.E.values():
            ws = []
            for sem, val in evs:
                if sem is e.sem and not e.same_sync:
                    continue
                if e.seen.get(id(sem), 0) < val:
                    ws.append((sem, val))
                    e.seen[id(sem)] = val
            if ws:
                self._do(e, ws, None, None)


def build_nc(L, NB, dbg=False):
    NT = L // 128
    C = L // 8
    CS = min(128, C)
    NSP = C // CS
    SPT = CS * 8
    NBK = max(1, L // 512)
    BW = min(512, L)
    nc = bass.Bass("TRN2", target_bir_lowering=False)

    def din(name, shape):
        return nc.dram_tensor(name, list(shape), F32, kind="ExternalInput").ap()

    x_d = din("x", [NB, L, 1024])
    out_d = nc.dram_tensor("out", [NB, L, 1024], F32, kind="ExternalOutput").ap()
    norm1_d = din("norm1_g", [1024]); w_in_d = din("w_in", [1024, 2048])
    qg_d = din("q_norm_g", [64]); kg_d = din("k_norm_g", [64])
    lre_d = din("ssm_lambda_re", [32, 64]); lim_d = din("ssm_lambda_im", [32, 64]); ldt_d = din("ssm_log_dt", [32])
    bre_d = din("ssm_b_re", [32, 64, 16]); bim_d = din("ssm_b_im", [32, 64, 16])
    cre_d = din("ssm_c_re", [32, 16, 64]); cim_d = din("ssm_c_im", [32, 16, 64])
    sd_d = din("ssm_d", [32, 16]); wglu_d = din("w_glu", [512, 512]); bglu_d = din("b_glu", [512])
    gao_d = din("attn_out_g", [512]); gso_d = din("ssm_out_g", [512]); wout_d = din("w_out", [1024, 1024])
    norm2_d = din("norm2_g", [1024]); w1_d = din("w_mlp_in", [1024, 4096]); w2_d = din("w_mlp_out", [4096, 1024])
    cst_d = din("consts", [3, 128, 128])
    iota_d = din("iota", [128, 256])
    dbg_outs = {}

    with ExitStack() as gst:
        fw = FW(nc, gst)

        def V(fn, r=(), w=()): return fw.op("dve", fn, r, w)
        def A(fn, r=(), w=()): return fw.op("act", fn, r, w)
        def G(fn, r=(), w=()): return fw.op("pool", fn, r, w)
        def T(fn, r=(), w=()): return fw.op("pe", fn, r, w)
        def D(fn, r=(), w=(), track=None): return fw.dma("sp", fn, r, w, track)

        uniq = [0]

        def sbt(st, name, shape, dt=F32):
            uniq[0] += 1
            name = f"{name}_{uniq[0]}"
            return st.enter_context(nc.sbuf_tensor(name, list(shape), dt)), fw.buf(name)

        def dump(name, ap, b, shape):
            if not dbg:
                return
            d = nc.dram_tensor("dbg_" + name, list(shape), ap.dtype, kind="ExternalOutput").ap()
            bo = fw.buf("dbgo_" + name)
            D(lambda h: h.dma_start(out=d, in_=ap), [b], [bo])
            dbg_outs[name] = bo

        BoutBlk = [fw.buf(f"o{i}") for i in range(NB * NT)]
        PF = []
        for i in range(5):
            t = gst.enter_context(nc.psum_tensor(f"pf{i}", [128, 512], F32)); PF.append((t, fw.buf(f"pf{i}")))
        PH = []
        for i in range(3):
            t = gst.enter_context(nc.psum_tensor(f"ph{i}", [128, 1024], BF16)); PH.append((t, fw.buf(f"ph{i}")))

        cstf, Bcstf = sbt(gst, "cstf", [128, 3, 128])
        D(lambda h: h.dma_start(out=cstf[:], in_=cst_d.rearrange("k p f -> p k f")), [], [Bcstf])
        identf = cstf[:, 0, :]
        cstb, Bcstb = sbt(gst, "cstb", [128, 3, 128], BF16)
        V(lambda h: h.tensor_copy(cstb[:], cstf[:]), [Bcstf], [Bcstb])
        identb = cstb[:, 0, :]; maskb = cstb[:, 1, :]; bd64 = cstb[:, 2, :]
        ones_f, Bones = sbt(gst, "ones_f", [128, 1])
        G(lambda h: h.memset(ones_f[:], 1.0), [], [Bones])
        onecol, Bonecol = sbt(gst, "onecol", [128, 1], BF16)
        G(lambda h: h.memset(onecol[:], 1.0), [], [Bonecol])
        epsT, Beps = sbt(gst, "epsT", [128, 1])
        G(lambda h: h.memset(epsT[:], EPS), [], [Beps])
        vecs, Bvecs = sbt(gst, "vecs", [128, 32])
        with nc.allow_non_contiguous_dma("tiny param vectors"):
            D(lambda h: h.dma_start(out=vecs[:, 0:8], in_=norm1_d.rearrange("(c p) -> p c", p=128)), [], [Bvecs])
            D(lambda h: h.dma_start(out=vecs[:, 8:16], in_=norm2_d.rearrange("(c p) -> p c", p=128)), [], [Bvecs])
            D(lambda h: h.dma_start(out=vecs[:, 16:20], in_=gao_d.rearrange("(c p) -> p c", p=128)), [], [Bvecs])
            D(lambda h: h.dma_start(out=vecs[:, 20:24], in_=gso_d.rearrange("(c p) -> p c", p=128)), [], [Bvecs])
            D(lambda h: h.dma_start(out=vecs[:, 24:28], in_=bglu_d.rearrange("(c p) -> p c", p=128)), [], [Bvecs])
            for hh in range(2):
                D(lambda h, hh=hh: h.dma_start(out=vecs[hh * 64:(hh + 1) * 64, 28:29], in_=qg_d.rearrange("(p o) -> p o", o=1)), [], [Bvecs])
                D(lambda h, hh=hh: h.dma_start(out=vecs[hh * 64:(hh + 1) * 64, 29:30], in_=kg_d.rearrange("(p o) -> p o", o=1)), [], [Bvecs])

        gqk, Bgqk = sbt(gst, "gqk", [128, 2, 64])
        D(lambda h: h.dma_start(out=gqk[:, 0, :], in_=qg_d.partition_broadcast(128)), [], [Bgqk])
        D(lambda h: h.dma_start(out=gqk[:, 1, :], in_=kg_d.partition_broadcast(128)), [], [Bgqk])

        def rstd_from_ss(ss_ap, Bss, out_ap, Bout, inv_n, tmp_ap, Btmp):
            A(lambda h: h.activation(tmp_ap, ss_ap, AF.Sqrt, bias=epsT[:], scale=inv_n), [Bss, Beps], [Btmp])
            V(lambda h: h.reciprocal(out_ap, tmp_ap), [Btmp], [Bout])

        with ExitStack() as p1:
            rho, Brho = sbt(p1, "rho", [128, 16])
            CSPEC = [("toep", [128, 32 * 128], BF16), ("wbr", [128, 32 * 128], BF16), ("wbi", [128, 32 * 128], BF16),
                     ("ceir", [128, 16 * 128], BF16), ("ceii", [128, 16 * 128], BF16), ("cosT", [128, 16 * C], F32), ("sinT", [128, 16 * C], F32)]
            scr = {n: (nc.dram_tensor("scr_" + n, shp, dt_, kind="Internal").ap(), fw.buf("scr_" + n)) for n, shp, dt_ in CSPEC}

            def alloc_consts(st):
                t = {n: sbt(st, n, shp, dt_) for n, shp, dt_ in CSPEC}
                return t
            wglu, Bwglu = sbt(p1, "wglu", [128, 4, 512], BF16)

            with ExitStack() as bs:
                _ct = alloc_consts(bs)
                toep = _ct["toep"][0][:].rearrange("p (g f) -> p g f", g=32); Btoep = _ct["toep"][1]
                wbr = _ct["wbr"][0][:].rearrange("p (g f) -> p g f", g=32); Bwbr = _ct["wbr"][1]
                wbi = _ct["wbi"][0][:].rearrange("p (g f) -> p g f", g=32); Bwbi = _ct["wbi"][1]
                ceir = _ct["ceir"][0][:].rearrange("p (g j o) -> p g j o", g=16, j=8); Bceir = _ct["ceir"][1]
                ceii = _ct["ceii"][0][:].rearrange("p (g j o) -> p g j o", g=16, j=8); Bceii = _ct["ceii"][1]
                cosT = _ct["cosT"][0][:].rearrange("p (g c) -> p g c", g=16); BcosT = _ct["cosT"][1]
                sinT = _ct["sinT"][0][:].rearrange("p (g c) -> p g c", g=16); BsinT = _ct["sinT"][1]
                def t32(name, shape):
                    return sbt(bs, name, shape)
                LR, BLR = t32("LR", [128, 16]); LI, BLI = t32("LI", [128, 16]); LDT, BLDT = t32("LDT", [128, 16])
                BR, BBR = t32("BR", [128, 16, 16]); BI, BBI = t32("BI", [128, 16, 16])
                CR, BCR = t32("CR", [128, 16, 16]); CI, BCI = t32("CI", [128, 16, 16]); CIn, BCIn = t32("CIn", [128, 16, 16])
                DCOL, BDCOL = t32("DCOL", [128, 32])
                cpad, Bcpad = t32("cpad", [128, 2, 4, 128])
                wg32, Bwg32 = t32("wg32", [128, 4, 512])
                with nc.allow_non_contiguous_dma("small transposed parameter loads"):
                    for hf in range(2):
                        qs = slice(hf * 64, hf * 64 + 64); gs = slice(hf * 16, hf * 16 + 16)
                        D(lambda h, qs=qs, gs=gs: h.dma_start(out=LR[qs, :], in_=lre_d[gs, :].rearrange("g p -> p g")), [], [BLR])
                        D(lambda h, qs=qs, gs=gs: h.dma_start(out=LI[qs, :], in_=lim_d[gs, :].rearrange("g p -> p g")), [], [BLI])
                        D(lambda h, qs=qs, gs=gs: h.dma_start(out=LDT[qs, :], in_=ldt_d[gs].partition_broadcast(64)), [], [BLDT])
                        D(lambda h, qs=qs, gs=gs: h.dma_start(out=BR[qs, :, :], in_=bre_d[gs].rearrange("g p h -> p g h")), [], [BBR])
                        D(lambda h, qs=qs, gs=gs: h.dma_start(out=BI[qs, :, :], in_=bim_d[gs].rearrange("g p h -> p g h")), [], [BBI])
                    for i in range(8):
                        D(lambda h, i=i: h.dma_start(out=DCOL[i * 16:(i + 1) * 16, :], in_=sd_d.rearrange("g h -> h g")), [], [BDCOL])
                G(lambda h: h.memset(cpad[:], 0.0), [], [Bcpad])
                for ri, cd in enumerate((cre_d, cim_d)):
                    for tg in range(4):
                        hf = tg // 2
                        D(lambda h, ri=ri, tg=tg, hf=hf, cd=cd: h.dma_start(
                            out=cpad[:, ri, tg, hf * 64:(hf + 1) * 64],
                            in_=cd[tg * 8:(tg + 1) * 8].rearrange("g o p -> (g o) p")), [], [Bcpad])
                D(lambda h: h.dma_start(out=wg32[:], in_=wglu_d.rearrange("(c p) n -> p c n", p=128)), [], [Bwg32])
                V(lambda h: h.tensor_copy(wglu[:], wg32[:]), [Bwg32], [Bwglu])
                for ri, (dst, Bdst) in enumerate(((CR, BCR), (CI, BCI))):
                    for tg in range(4):
                        hf = tg // 2
                        pf, Bpf = PF[tg % 2]
                        T(lambda h, ri=ri, tg=tg, pf=pf: h.matmul(pf[:, 0:128], cpad[:, ri, tg, :], identf, start=True, stop=True), [Bcpad, Bcstf], [Bpf])
                        qs = slice(hf * 64, hf * 64 + 64)
                        g0 = (tg % 2) * 8
                        V(lambda h, dst=dst, pf=pf, qs=qs, g0=g0: h.tensor_copy(
                            dst[qs, g0:g0 + 8, :], pf[qs, 0:128].rearrange("p (g o) -> p g o", o=16)), [Bpf], [Bdst])
                V(lambda h: h.tensor_single_scalar(CIn[:], CI[:], -1.0, ALU.mult), [BCI], [BCIn])

                cnt = [0]

                def newt(shape):
                    cnt[0] += 1
                    return t32(f"bt{cnt[0]}", shape)

                def tt(o, Bo, a, Ba, b, Bb, op):
                    V(lambda h: h.tensor_tensor(o, a, b, op), [Ba, Bb], [Bo])

                NBIG = 4 * C
                ft, Bft = t32("ft", [128, NBIG]); fti, Bfti = sbt(bs, "fti", [128, NBIG], I32); ftf, Bftf = t32("ftf", [128, NBIG])
                cu = [t32(f"cu{i}", [128, 16, 16]) for i in range(5)]

                def frac(dst, Bdst, src, Bsrc, add, n):
                    t, Bt = ft[:, 0:n], Bft; ti, Bti = fti[:, 0:n], Bfti; tf, Btf = ftf[:, 0:n], Bftf
                    V(lambda h: h.tensor_single_scalar(t, src, add, ALU.add), [Bsrc], [Bt])
                    V(lambda h: h.tensor_copy(ti, t), [Bt], [Bti])
                    V(lambda h: h.tensor_copy(tf, ti), [Bti], [Btf])
                    V(lambda h: h.tensor_tensor(dst, t, tf, ALU.subtract), [Bt, Btf], [Bdst])

                dt, Bdt = newt([128, 16]); are, Bare = newt([128, 16]); turns, Bturns = newt([128, 16]); mag, Bmag = newt([128, 16])
                A(lambda h: h.activation(dt[:], LDT[:], AF.Exp), [BLDT], [Bdt])
                tt(are[:], Bare, LR[:], BLR, dt[:], Bdt, ALU.mult)
                tt(turns[:], Bturns, LI[:], BLI, dt[:], Bdt, ALU.mult)
                V(lambda h: h.tensor_single_scalar(turns[:], turns[:], 1.0 / TWO_PI, ALU.mult), [Bturns], [Bturns])
                A(lambda h: h.activation(mag[:], are[:], AF.Exp), [Bare], [Bmag])
                A(lambda h: h.activation(rho[:], are[:], AF.Exp, scale=8.0), [Bare], [Brho])
                fs, Bfs = newt([128, 16]); fcn, Bfcn = newt([128, 16]); sA, BsA = newt([128, 16]); cA, BcA = newt([128, 16])
                frac(fs[:], Bfs, turns[:], Bturns, 0.0, 16)
                frac(fcn[:], Bfcn, turns[:], Bturns, 0.25, 16)
                A(lambda h: h.activation(sA[:], fs[:], AF.Sin, scale=TWO_PI), [Bfs], [BsA])
                A(lambda h: h.activation(cA[:], fcn[:], AF.Sin, scale=TWO_PI), [Bfcn], [BcA])
                lbr, Blbr = newt([128, 16]); lbi, Blbi = newt([128, 16])
                tt(lbr[:], Blbr, mag[:], Bmag, cA[:], BcA, ALU.mult)
                tt(lbi[:], Blbi, mag[:], Bmag, sA[:], BsA, ALU.mult)
                n2, Bn2 = newt([128, 16]); t1, Bt1 = newt([128, 16]); t2, Bt2 = newt([128, 16]); inv, Binv = newt([128, 16])
                nr, Bnr = newt([128, 16]); kr, Bkr = newt([128, 16]); ki, Bki = newt([128, 16])
                tt(t1[:], Bt1, LR[:], BLR, LR[:], BLR, ALU.mult)
                tt(t2[:], Bt2, LI[:], BLI, LI[:], BLI, ALU.mult)
                tt(n2[:], Bn2, t1[:], Bt1, t2[:], Bt2, ALU.add)
                V(lambda h: h.reciprocal(inv[:], n2[:]), [Bn2], [Binv])
                V(lambda h: h.tensor_single_scalar(nr[:], lbr[:], -1.0, ALU.add), [Blbr], [Bnr])
                tt(t1[:], Bt1, nr[:], Bnr, LR[:], BLR, ALU.mult)
                tt(t2[:], Bt2, lbi[:], Blbi, LI[:], BLI, ALU.mult)
                tt(kr[:], Bkr, t1[:], Bt1, t2[:], Bt2, ALU.add)
                tt(kr[:], Bkr, kr[:], Bkr, inv[:], Binv, ALU.mult)
                tt(t1[:], Bt1, lbi[:], Blbi, LR[:], BLR, ALU.mult)
                tt(t2[:], Bt2, nr[:], Bnr, LI[:], BLI, ALU.mult)
                tt(ki[:], Bki, t1[:], Bt1, t2[:], Bt2, ALU.subtract)
                tt(ki[:], Bki, ki[:], Bki, inv[:], Binv, ALU.mult)

                def bc(ap2):
                    return ap2.unsqueeze(2).to_broadcast([128, 16, 16])

                def cmul_b(outr, outi, Bor, Boi, sr, si, Bsr, Bsi, xr, xi, Bxr, Bxi, negate_i=False):
                    (u1, Bu1), (u2, Bu2), (u3, Bu3), (u4, Bu4), (u5, Bu5) = cu
                    V(lambda h: h.tensor_tensor(u1[:], xr, bc(sr), ALU.mult), [Bxr, Bsr], [Bu1])
                    V(lambda h: h.tensor_tensor(u2[:], xi, bc(si), ALU.mult), [Bxi, Bsi], [Bu2])
                    V(lambda h: h.tensor_tensor(outr, u1[:], u2[:], ALU.subtract), [Bu1, Bu2], [Bor])
                    V(lambda h: h.tensor_tensor(u3[:], xi, bc(sr), ALU.mult), [Bxi, Bsr], [Bu3])
                    V(lambda h: h.tensor_tensor(u4[:], xr, bc(si), ALU.mult), [Bxr, Bsi], [Bu4])
                    if negate_i:
                        V(lambda h: h.tensor_tensor(u5[:], u3[:], u4[:], ALU.add), [Bu3, Bu4], [Bu5])
                        V(lambda h: h.tensor_single_scalar(outi, u5[:], -1.0, ALU.mult), [Bu5], [Boi])
                    else:
                        V(lambda h: h.tensor_tensor(outi, u3[:], u4[:], ALU.add), [Bu3, Bu4], [Boi])

                BBr, BBBr = newt([128, 16, 16]); BBi, BBBi = newt([128, 16, 16])
                cmul_b(BBr[:], BBi[:], BBBr, BBBi, kr[:], ki[:], Bkr, Bki, BR[:], BI[:], BBR, BBI)
                PWr, BPWr = newt([128, 9, 16]); PWi, BPWi = newt([128, 9, 16])
                V(lambda h: h.memset(PWr[:, 0, :], 1.0), [], [BPWr]); V(lambda h: h.memset(PWi[:, 0, :], 0.0), [], [BPWi])
                for m in range(1, 9):
                    a1, Ba1 = newt([128, 16]); a2, Ba2 = newt([128, 16])
                    V(lambda h, m=m, a1=a1: h.tensor_tensor(a1[:], PWr[:, m - 1, :], lbr[:], ALU.mult), [BPWr, Blbr], [Ba1])
                    V(lambda h, m=m, a2=a2: h.tensor_tensor(a2[:], PWi[:, m - 1, :], lbi[:], ALU.mult), [BPWi, Blbi], [Ba2])
                    V(lambda h, m=m, a1=a1, a2=a2: h.tensor_tensor(PWr[:, m, :], a1[:], a2[:], ALU.subtract), [Ba1, Ba2], [BPWr])
                    a3, Ba3 = newt([128, 16]); a4, Ba4 = newt([128, 16])
                    V(lambda h, m=m, a3=a3: h.tensor_tensor(a3[:], PWr[:, m - 1, :], lbi[:], ALU.mult), [BPWr, Blbi], [Ba3])
                    V(lambda h, m=m, a4=a4: h.tensor_tensor(a4[:], PWi[:, m - 1, :], lbr[:], ALU.mult), [BPWi, Blbr], [Ba4])
                    V(lambda h, m=m, a3=a3, a4=a4: h.tensor_tensor(PWi[:, m, :], a3[:], a4[:], ALU.add), [Ba3, Ba4], [BPWi])
                BEr, BBEr = newt([128, 16, 15, 16]); BEi, BBEi = newt([128, 16, 15, 16])
                G(lambda h: h.memset(BEr[:], 0.0), [], [BBEr]); G(lambda h: h.memset(BEi[:], 0.0), [], [BBEi])
                for i in range(8):
                    cmul_b(BEr[:, :, i, :], BEi[:, :, i, :], BBEr, BBEi, PWr[:, 7 - i, :], PWi[:, 7 - i, :], BPWr, BPWi,
                           BBr[:], BBi[:], BBBr, BBBi)
                for j in range(8):
                    cmul_b(ceir[:, :, j, :], ceii[:, :, j, :], Bceir, Bceii, PWr[:, j + 1, :], PWi[:, j + 1, :], BPWr, BPWi,
                           CR[:], CI[:], BCR, BCI, negate_i=True)
                ph8, Bph8 = newt([128, 16]); t8, Bt8 = newt([128, 16])
                V(lambda h: h.tensor_single_scalar(t8[:], turns[:], 8.0, ALU.mult), [Bturns], [Bt8])
                frac(ph8[:], Bph8, t8[:], Bt8, 0.0, 16)
                iot, Biot = newt([128, C])
                D(lambda h: h.dma_start(out=iot[:], in_=iota_d[:, 0:C]), [], [Biot])
                TT, BTT = newt([128, 4, C]); FR, BFR = newt([128, 4 * C])
                for g4 in range(4):
                    for gl in range(4):
                        V(lambda h, gl=gl, g4=g4: h.tensor_scalar_mul(TT[:, gl, :], iot[:], ph8[:, g4 * 4 + gl:g4 * 4 + gl + 1]), [Biot, Bph8], [BTT])
                    frac(FR[:], BFR, TT[:].rearrange("p g c -> p (g c)"), BTT, 0.0, 4 * C)
                    A(lambda h, g4=g4: h.activation(sinT[:, g4 * 4:(g4 + 1) * 4, :].rearrange("p g c -> p (g c)"), FR[:], AF.Sin, scale=TWO_PI), [BFR], [BsinT])
                    frac(FR[:], BFR, TT[:].rearrange("p g c -> p (g c)"), BTT, 0.25, 4 * C)
                    A(lambda h, g4=g4: h.activation(cosT[:, g4 * 4:(g4 + 1) * 4, :].rearrange("p g c -> p (g c)"), FR[:], AF.Sin, scale=TWO_PI), [BFR], [BcosT])
                G(lambda h: h.memset(wbr, 0.0), [], [Bwbr]); G(lambda h: h.memset(wbi, 0.0), [], [Bwbi])
                for g in range(32):
                    hf = g // 16; gl = g % 16
                    qs = slice(hf * 64, hf * 64 + 64)
                    pf, Bpf = PF[g % 2]
                    BEr_f = BEr[:].rearrange("p g b h -> p g (b h)"); BEi_f = BEi[:].rearrange("p g b h -> p g (b h)")
                    if g == 0:
                        BE16r, BBE16r = sbt(bs, "BE16r", [128, 16, 240], BF16); BE16i, BBE16i = sbt(bs, "BE16i", [128, 16, 240], BF16)
                        C16r, BC16r = sbt(bs, "C16r", [128, 16, 16], BF16); C16n, BC16n = sbt(bs, "C16n", [128, 16, 16], BF16)
                        V(lambda h: h.tensor_copy(BE16r[:], BEr_f), [BBEr], [BBE16r]); V(lambda h: h.tensor_copy(BE16i[:], BEi_f), [BBEi], [BBE16i])
                        V(lambda h: h.tensor_copy(C16r[:], CR[:]), [BCR], [BC16r]); V(lambda h: h.tensor_copy(C16n[:], CIn[:]), [BCIn], [BC16n])
                    for j in range(8):
                        off = (7 - j) * 16
                        T(lambda h, pf=pf, j=j, qs=qs, gl=gl, off=off: h.matmul(pf[:, j * 16:(j + 1) * 16], BE16r[qs, gl, off:off + 128], C16r[qs, gl, :], start=True, stop=False),
                          [BBE16r, BC16r], [Bpf])
                        T(lambda h, pf=pf, j=j, qs=qs, gl=gl, off=off: h.matmul(pf[:, j * 16:(j + 1) * 16], BE16i[qs, gl, off:off + 128], C16n[qs, gl, :], start=False, stop=True),
                          [BBE16i, BC16n], [Bpf])
                    V(lambda h, pf=pf, g=g: h.scalar_tensor_tensor(toep[:, g, :], identf, DCOL[:, g:g + 1], pf[:, 0:128], ALU.mult, ALU.add),
                      [Bpf, Bcstf, BDCOL], [Btoep])
                    pg, Bpg = PF[2 + g % 2]
                    T(lambda h, pg=pg, qs=qs, gl=gl: h.matmul(pg[:, 0:64], BEr_f[qs, gl, 0:128], cstf[qs, 0, hf * 64:hf * 64 + 64], start=True, stop=True), [BBEr, Bcstf], [Bpg])
                    T(lambda h, pg=pg, qs=qs, gl=gl: h.matmul(pg[:, 64:128], BEi_f[qs, gl, 0:128], cstf[qs, 0, hf * 64:hf * 64 + 64], start=True, stop=True), [BBEi, Bcstf], [Bpg])
                    A(lambda h, pg=pg, g=g, hf=hf: h.copy(wbr[:, g, hf * 64:hf * 64 + 64], pg[:, 0:64]), [Bpg], [Bwbr])
                    A(lambda h, pg=pg, g=g, hf=hf: h.copy(wbi[:, g, hf * 64:hf * 64 + 64], pg[:, 64:128]), [Bpg], [Bwbi])
                dump("toep", toep, Btoep, [128, 32, 128]); dump("wbr", wbr, Bwbr, [128, 32, 128])
                dump("ceir", ceir, Bceir, [128, 16, 8, 16]); dump("cosT", cosT, BcosT, [128, 16, C])
                for n, shp, dt_ in CSPEC:
                    D(lambda h, n=n: h.dma_start(out=scr[n][0], in_=_ct[n][0][:]), [_ct[n][1]], [scr[n][1]])
                fw.barrier()

            SW = max(4 * L, NSP * 4096)
            Q1, BQ1 = sbt(p1, "Q1", [128, SW], BF16)
            Q2, BQ2 = sbt(p1, "Q2", [128, SW], BF16)
            Q3, BQ3 = sbt(p1, "Q3", [128, SW], BF16)
            S1, BS1 = sbt(p1, "S1", [128, SW], BF16)
            XSa, BXSa = sbt(p1, "XSa", [128, 4 * L], BF16)
            XSb, BXSb = sbt(p1, "XSb", [128, max(4 * L, 8192)], BF16)
            sbT = XSa[:].rearrange("p (t l) -> p t l", t=4); BsbT = BXSa
            xsTa = XSa[:].rearrange("p (c l) -> p c l", c=4); xsTb = XSb[:, 0:4 * L].rearrange("p (c l) -> p c l", c=4)
            BxsT2 = [BXSa, BXSb]
            wo = XSb[:, 0:8192].rearrange("p (c n) -> p c n", c=8); Bwo = BXSb

            def xsT(dc, sl):
                return (xsTa if dc < 4 else xsTb)[:, dc % 4, sl]
            stat, Bstat = sbt(p1, "stat", [128, 4, NT])
            rs, Brs = sbt(p1, "rs", [128, 4, NT])
            junk, Bjunk = sbt(p1, "junk", [128, 1024], BF16)

            for b in range(NB):
                qT = Q1[:, 0:4 * L].rearrange("p (t l) -> p t l", t=4)
                kTr = Q2[:, 0:4 * L].rearrange("p (t l) -> p t l", t=4)
                vrev = Q3[:, 0:4 * L].rearrange("p (n f) -> p n f", f=512)
                vTr = S1[:, 0:4 * L].rearrange("p (t l) -> p t l", t=4)
                u_tm = S1[:, 0:NSP * 4096].rearrange("p (s g f) -> p s g f", g=32, f=128)
                u_tmw = S1[:, 0:NSP * 4096].rearrange("p (s g j h) -> p s g j h", g=32, j=8, h=16)
                U2 = Q1[:, 0:32 * C].rearrange("p (g c) -> p g c", g=32)
                SPr = Q2[:, 0:16 * C].rearrange("p (g c) -> p g c", g=16)
                SPi = Q2[:, 16 * C:32 * C].rearrange("p (g c) -> p g c", g=16)
                G2 = S1[:, 0:32 * C].rearrange("p (g c) -> p g c", g=32)
                y_tm = Q1[:, 0:NSP * 4096].rearrange("p (s j f) -> p s j f", j=8, f=512)
                yT = Q2[:, 0:4 * L].rearrange("p (t l) -> p t l", t=4)
                ssmT = Q3[:, 0:4 * L].rearrange("p (t l) -> p t l", t=4)

                with ExitStack() as sa:
                    xb = [sbt(sa, f"xb{i}", [128, 1024]) for i in range(4)]
                    wsts = [sbt(sa, f"wst{k}", [128, 8, 128]) for k in range(2)]
                    wgrps = [sbt(sa, f"wgrp{k}", [128, 8, 512], BF16) for k in range(4)]
                    xss = [sbt(sa, f"xs{i}", [128, 1024], BF16) for i in range(2)]
                    sqfs = [sbt(sa, f"sqf{k}", [128, 512]) for k in range(2)]
                    kns = [sbt(sa, f"kn{k}", [128, 512]) for k in range(2)]
                    kn16s = [sbt(sa, f"kn16_{k}", [128, 512], BF16) for k in range(3)]
                    qkst, _ = sbt(sa, "qkst", [128, 3, 2 * NT * 8])
                    Bqk = [[fw.buf(f"qkst{b}_{i}_{r}") for r in range(3)] for i in range(2 * NT)]
                    def load_wgrp(c0):
                        wgrp, Bwgrp = wgrps[c0 // 512]
                        for hh in range(4):
                            wst, Bwst = wsts[hh % 2]
                            fw.dma("pool", lambda h, hh=hh, wst=wst: h.dma_start(out=wst[:], in_=w_in_d[:, c0 + hh * 128:c0 + (hh + 1) * 128].rearrange("(c p) n -> p c n", p=128)), [], [Bwst])
                            G(lambda h, hh=hh, wst=wst: h.tensor_tensor(wgrp[:, :, hh * 128:(hh + 1) * 128], wst[:], vecs[:, 0:8].unsqueeze(2).to_broadcast([128, 8, 128]), ALU.mult),
                              [Bwst, Bvecs], [Bwgrp])

                    G(lambda h: h.memset(stat[:], 0.0), [], [Bstat])
                    st1, _ = sbt(sa, "st1", [128, 3, NT])
                    Bst1 = [fw.buf(f"st1_{b}_{i}") for i in range(NT)]
                    G(lambda h: h.memset(st1[:], 0.0), [], Bst1)
                    for c0 in (0, 512, 1024, 1536):
                        load_wgrp(c0)
                    def a1_s1(n):
                        xt, Bxt = xb[n % 4]
                        xs, Bxs = xss[n % 2]
                        D(lambda h: h.dma_start(out=xt[:], in_=x_d[b, n * 128:(n + 1) * 128, :]), [], [Bxt])
                        A(lambda h: h.activation(junk[:], xt[:], AF.Square, accum_out=st1[:, 0, n:n + 1]), [Bxt], [Bjunk, Bst1[n]])
                        rstd_from_ss(st1[:, 0, n:n + 1], Bst1[n], st1[:, 1, n:n + 1], Bst1[n], 1.0 / 1024, st1[:, 2, n:n + 1], Bst1[n])
                        V(lambda h: h.tensor_scalar_mul(xs[:], xt[:], st1[:, 1, n:n + 1]), [Bxt, Bst1[n]], [Bxs])

                    def a1_s2(n):
                        xs, Bxs = xss[n % 2]
                        ph, Bph = PH[n % 2]
                        for dc in range(8):
                            T(lambda h, dc=dc: h.transpose(ph[:, dc * 128:(dc + 1) * 128], xs[:, dc * 128:(dc + 1) * 128], identb), [Bxs, Bcstb], [Bph])
                        A(lambda h: h.copy(xsTa[:, :, n * 128:(n + 1) * 128], ph[:, 0:512].rearrange("p (c t) -> p c t", c=4)), [Bph], [BXSa])
                        A(lambda h: h.copy(xsTb[:, :, n * 128:(n + 1) * 128], ph[:, 512:1024].rearrange("p (c t) -> p c t", c=4)), [Bph], [BXSb])

                    for n in range(NT):
                        a1_s1(n)
                        if n >= 1:
                            a1_s2(n - 1)
                    a1_s2(NT - 1)

                    qk_units = [(which, n) for which in range(2) for n in range(NT)]

                    def qk_s1(u):
                        which, n = qk_units[u]
                        wgrp, Bwgrp = wgrps[which]
                        pf, Bpf = PF[u % 3]
                        sqf, Bsqf = sqfs[u % 2]; kn, Bkn = kns[u % 2]; kn16, Bkn16 = kn16s[u % 3]
                        for dc in range(8):
                            T(lambda h, dc=dc: h.matmul(pf[:], xsT(dc, slice(n * 128, (n + 1) * 128)), wgrp[:, dc, :], start=(dc == 0), stop=(dc == 7)),
                              BxsT2 + [Bwgrp], [Bpf])
                        A(lambda h: h.activation(sqf[:], pf[:], AF.Square), [Bpf], [Bsqf])
                        c0 = (which * NT + n) * 8
                        Bq = Bqk[which * NT + n]
                        V(lambda h: h.tensor_reduce(qkst[:, 0, c0:c0 + 8], sqf[:].rearrange("p (a d) -> p a d", d=64), mybir.AxisListType.X, ALU.add), [Bsqf], [Bq[0]])
                        A(lambda h: h.activation(qkst[:, 1, c0:c0 + 8], qkst[:, 0, c0:c0 + 8], AF.Sqrt, bias=epsT[:], scale=1.0 / 64), [Bq[0], Beps], [Bq[1]])
                        V(lambda h: h.reciprocal(qkst[:, 2, c0:c0 + 8], qkst[:, 1, c0:c0 + 8]), [Bq[1]], [Bq[2]])
                        V(lambda h: h.tensor_tensor(kn16[:].rearrange("p (a d) -> p a d", d=64), pf[:].rearrange("p (a d) -> p a d", d=64),
                                                    qkst[:, 2, c0:c0 + 8].unsqueeze(2).to_broadcast([128, 8, 64]), ALU.mult), [Bpf, Bq[2]], [Bkn16])

                    def qk_s2(u):
                        which, n = qk_units[u]
                        kn16, Bkn16 = kn16s[u % 3]
                        ph, Bph = PH[u % 2]
                        for hp in range(4):
                            T(lambda h, hp=hp: h.transpose(ph[:, hp * 128:(hp + 1) * 128], kn16[:, hp * 128:(hp + 1) * 128], identb), [Bkn16, Bcstb], [Bph])
                        if which == 0:
                            A(lambda h: h.activation(qT[:, :, n * 128:(n + 1) * 128], ph[:, 0:512].rearrange("p (c t) -> p c t", c=4), AF.Copy, scale=vecs[:, 28:29]), [Bph, Bvecs], [BQ1])
                        else:
                            hi = L - n * 128 - 1
                            lo = L - (n + 1) * 128 - 1
                            A(lambda h: h.activation(kTr[:, :, hi:(lo if lo >= 0 else None):-1], ph[:, 0:512].rearrange("p (c t) -> p c t", c=4), AF.Copy, scale=vecs[:, 29:30]), [Bph, Bvecs], [BQ2])

                    for u in range(len(qk_units)):
                        qk_s1(u)
                        if u >= 2:
                            qk_s2(u - 2)
                    qk_s2(len(qk_units) - 2)
                    qk_s2(len(qk_units) - 1)
                    wgrp, Bwgrp = wgrps[2]
                    for hp in range(4):
                        for bk in range(NBK):
                            pf, Bpf = PF[(hp * NBK + bk) % 2]
                            for dc in range(8):
                                T(lambda h, pf=pf, dc=dc, hp=hp, bk=bk: h.matmul(pf[:, 0:BW], wgrp[:, dc, hp * 128:(hp + 1) * 128], xsT(dc, slice(bk * BW, (bk + 1) * BW)),
                                                                              start=(dc == 0), stop=(dc == 7)), [Bwgrp] + BxsT2, [Bpf])
                            lo = L - (bk + 1) * BW
                            stop = lo - 1 if lo > 0 else None
                            A(lambda h, pf=pf, hp=hp, lo=lo, stop=stop: h.copy(vTr[:, hp, lo + BW - 1:stop:-1], pf[:, 0:BW]), [Bpf], [BS1])
                    for sbk in range(NT):
                        ph, Bph = PH[sbk % 2]
                        for hp in range(4):
                            T(lambda h, ph=ph, hp=hp, sbk=sbk: h.transpose(ph[:, hp * 128:(hp + 1) * 128], vTr[:, hp, sbk * 128:(sbk + 1) * 128], identb), [BS1, Bcstb], [Bph])
                        V(lambda h, ph=ph, sbk=sbk: h.tensor_copy(vrev[:, sbk, :], ph[:, 0:512]), [Bph], [BQ3])
                    wgrp, Bwgrp = wgrps[3]
                    for sp in range(NSP):
                        for j in range(8):
                            pf, Bpf = PF[(sp * 8 + j) % 2]
                            for dc in range(8):
                                T(lambda h, pf=pf, dc=dc, sp=sp, j=j: h.matmul(pf[0:CS, :], xsT(dc, slice(sp * SPT + j, (sp + 1) * SPT, 8)), wgrp[:, dc, :],
                                                                            start=(dc == 0), stop=(dc == 7)), BxsT2 + [Bwgrp], [Bpf])
                            A(lambda h, pf=pf, sp=sp, j=j: h.copy(u_tmw[0:CS, sp, :, j, :], pf[0:CS, :].rearrange("p (g h) -> p g h", h=16)), [Bpf], [BS1])
                    dump(f"qT{b}", qT, BQ1, [128, 4, L]); dump(f"kTr{b}", kTr, BQ2, [128, 4, L]); dump(f"vrev{b}", vrev, BQ3, [128, NT, 512])
                    dump(f"utm{b}", S1[0:CS, 0:NSP * 4096], BS1, [CS, NSP * 4096])
                    fw.barrier()

                with ExitStack() as sj:
                    gams = [sbt(sj, f"gam{k}", [128, L]) for k in range(2)]
                    Pcs = [sbt(sj, f"Pc{k}", [128, L + 1]) for k in range(2)]
                    a16s = [sbt(sj, f"a16_{k}", [128, L], BF16) for k in range(3)]
                    aTs = [sbt(sj, f"aT_{k}", [128, NT, 128], BF16) for k in range(2)]
                    sbs, Bsbs = sbt(sj, "sbs", [128, 512], BF16)
                    wst2s = [sbt(sj, f"wst2_{k}", [128, 8, 128]) for k in range(2)]
                    for q4 in range(8):
                        wst2, Bwst2 = wst2s[q4 % 2]
                        D(lambda h, q4=q4: h.dma_start(out=wst2[:], in_=wout_d[:, q4 * 128:(q4 + 1) * 128].rearrange("(c p) n -> p c n", p=128)), [], [Bwst2])
                        G(lambda h, q4=q4: h.tensor_tensor(wo[:, :, q4 * 128:(q4 + 1) * 128], wst2[:], vecs[:, 16:24].unsqueeze(2).to_broadcast([128, 8, 128]), ALU.mult),
                          [Bwst2, Bvecs], [Bwo])
                    for Pc_, BPc_ in Pcs:
                        V(lambda h, Pc_=Pc_: h.memset(Pc_[:, 0:1], 1.0), [], [BPc_])
                    osb, Bosb = PF[4]
                    units = [(i, hd) for i in range(NT) for hd in range(8)]

                    def stage1(n):
                        i, hd = units[n]
                        a16, Ba16 = a16s[n % 3]
                        gam, Bgam = gams[n % 2]
                        Pc, BPc = Pcs[n % 2]
                        S = (i + 1) * 128
                        k0 = L - S
                        hp = hd // 2; hs = slice((hd % 2) * 64, (hd % 2) * 64 + 64)
                        npc = (S + 511) // 512
                        for pc in range(npc):
                            w = min(512, S - pc * 512)
                            pf, Bpf = PF[pc % 4]
                            if pc == 0:
                                T(lambda h, pf=pf: h.matmul(pf[:, 0:128], identb, maskb, start=True, stop=False), [Bcstb], [Bpf])
                                T(lambda h, pf=pf: h.matmul(pf[:, 0:128], qT[hs, hp, i * 128:(i + 1) * 128], kTr[hs, hp, k0:k0 + 128], start=False, stop=True),
                                  [BQ1, BQ2], [Bpf])
                                if w > 128:
                                    T(lambda h, pf=pf, w=w: h.matmul(pf[:, 128:w], qT[hs, hp, i * 128:(i + 1) * 128], kTr[hs, hp, k0 + 128:k0 + w], start=True, stop=True),
                                      [BQ1, BQ2], [Bpf])
                            else:
                                T(lambda h, pf=pf, w=w, pc=pc: h.matmul(pf[:, 0:w], qT[hs, hp, i * 128:(i + 1) * 128], kTr[hs, hp, k0 + pc * 512:k0 + pc * 512 + w], start=True, stop=True),
                                  [BQ1, BQ2], [Bpf])
                            A(lambda h, pf=pf, pc=pc, w=w: h.activation(gam[:, pc * 512:pc * 512 + w], pf[:, 0:w], AF.Sigmoid, scale=-0.125), [Bpf], [Bgam])
                        V(lambda h: h.tensor_tensor_scan(Pc[:, 1:S + 1], gam[:, 0:S], gam[:, 0:S], 1.0, ALU.mult, ALU.min), [Bgam], [BPc])
                        V(lambda h: h.tensor_tensor(a16[:, 0:S], Pc[:, 0:S], Pc[:, 1:S + 1], ALU.subtract), [BPc], [Ba16])

                    def stage2(n):
                        i, hd = units[n]
                        a16, Ba16 = a16s[n % 3]
                        aT, BaT = aTs[n % 2]
                        for b4 in range(0, i + 1, 4):
                            nb4 = min(4, i + 1 - b4)
                            ph, Bph = PH[(b4 // 4) % 2]
                            for q in range(nb4):
                                T(lambda h, ph=ph, q=q, b4=b4: h.transpose(ph[:, q * 128:(q + 1) * 128], a16[:, (b4 + q) * 128:(b4 + q + 1) * 128], identb), [Ba16, Bcstb], [Bph])
                            A(lambda h, ph=ph, b4=b4, nb4=nb4: h.copy(aT[:, b4:b4 + nb4, :], ph[:, 0:nb4 * 128].rearrange("p (q t) -> p q t", t=128)), [Bph], [BaT])
                        for blk in range(i + 1):
                            T(lambda h, blk=blk: h.matmul(osb[:, hd * 64:(hd + 1) * 64], aT[:, blk, :], vrev[:, NT - 1 - i + blk, hd * 64:(hd + 1) * 64],
                                                          start=(blk == 0), stop=(blk == i)), [BaT, BQ3], [Bosb])
                        if hd == 7:
                            A(lambda h: h.copy(sbs[:], osb[:]), [Bosb], [Bsbs])
                            A(lambda h: h.activation(junk[:, 0:512], osb[:], AF.Square, accum_out=stat[:, 1, i:i + 1]), [Bosb], [Bjunk, Bstat])
                            ph, Bph = PH[2]
                            for hp in range(4):
                                T(lambda h, ph=ph, hp=hp: h.transpose(ph[:, hp * 128:(hp + 1) * 128], sbs[:, hp * 128:(hp + 1) * 128], identb), [Bsbs, Bcstb], [Bph])
                            V(lambda h, ph=ph: h.tensor_copy(sbT[:, :, i * 128:(i + 1) * 128], ph[:, 0:512].rearrange("p (c t) -> p c t", c=4)), [Bph], [BsbT])

                    for n in range(len(units)):
                        stage1(n)
                        if n >= 2:
                            stage2(n - 2)
                    stage2(len(units) - 2)
                    stage2(len(units) - 1)
                    dump(f"sbT{b}", sbT, BsbT, [128, 4, L])
                    fw.barrier()

                with ExitStack() as ss_:
                    _ct = alloc_consts(ss_)
                    for n, shp, dt_ in CSPEC:
                        D(lambda h, n=n, _ct=_ct: h.dma_start(out=_ct[n][0][:], in_=scr[n][0]), [scr[n][1]], [_ct[n][1]])
                    toep = _ct["toep"][0][:].rearrange("p (g f) -> p g f", g=32); Btoep = _ct["toep"][1]
                    wbr = _ct["wbr"][0][:].rearrange("p (g f) -> p g f", g=32); Bwbr = _ct["wbr"][1]
                    wbi = _ct["wbi"][0][:].rearrange("p (g f) -> p g f", g=32); Bwbi = _ct["wbi"][1]
                    ceir = _ct["ceir"][0][:].rearrange("p (g j o) -> p g j o", g=16, j=8); Bceir = _ct["ceir"][1]
                    ceii = _ct["ceii"][0][:].rearrange("p (g j o) -> p g j o", g=16, j=8); Bceii = _ct["ceii"][1]
                    cosT = _ct["cosT"][0][:].rearrange("p (g c) -> p g c", g=16); BcosT = _ct["cosT"][1]
                    sinT = _ct["sinT"][0][:].rearrange("p (g c) -> p g c", g=16); BsinT = _ct["sinT"][1]
                    BU2 = [fw.buf(f"U2_{g}") for g in range(32)]
                    BG2 = [fw.buf(f"G2_{g}") for g in range(32)]
                    BSPr = [fw.buf(f"SPr_{g}") for g in range(16)]; BSPi = [fw.buf(f"SPi_{g}") for g in range(16)]
                    tset = [[sbt(ss_, f"st{k}_{q}", [128, C]) for q in range(8)] for k in range(2)]
                    gset = [[sbt(ss_, f"gt{k}_{q}", [128, C]) for q in range(3)] for k in range(4)]
                    gT, BgT = sbt(ss_, "gT", [128, 512], BF16); sq4 = [sbt(ss_, f"sq4_{k}", [128, 4, 512], BF16) for k in range(1)]
                    for g0 in range(0, 32, 4):
                        ph, Bph = PH[(g0 // 4) % 2]
                        for gg in range(4):
                            g = g0 + gg
                            for sp in range(NSP):
                                T(lambda h, ph=ph, gg=gg, g=g, sp=sp: h.transpose(ph[:, gg * C + sp * CS:gg * C + (sp + 1) * CS], u_tm[0:CS, sp, g, :], cstb[0:CS, 0, 0:CS]),
                                  [BS1, Bcstb], [Bph])
                        A(lambda h, ph=ph, g0=g0: h.copy(U2[:, g0:g0 + 4, :], ph[:, 0:4 * C].rearrange("p (g c) -> p g c", g=4)), [Bph], BU2[g0:g0 + 4] + ([BQ1] if g0 == 0 else []))
                    V(lambda h: h.memset(SPr[:, :, 0:1], 0.0), [], [BQ2] + BSPr); V(lambda h: h.memset(SPi[:, :, 0:1], 0.0), [], [BQ2] + BSPi)
                    for gl in range(16):
                        pr, Bpr = PF[2 * (gl % 2)]; pi_, Bpi = PF[2 * (gl % 2) + 1]
                        (mr, Bmr), (mi, Bmi), (c1, Bc1), (c2, Bc2), (zr, Bzr), (zi, Bzi), (c3, Bc3), (c4, Bc4) = tset[gl % 2]
                        (c5, Bc5), (c6, Bc6) = (c1, Bc1), (c2, Bc2)
                        for (pp, Bpp, wb, Bwb) in ((pr, Bpr, wbr, Bwbr), (pi_, Bpi, wbi, Bwbi)):
                            T(lambda h, pp=pp, wb=wb, gl=gl: h.matmul(pp[:, 0:C], wb[:, gl, :], U2[:, gl, :], start=True, stop=False), [Bwb, BU2[gl]], [Bpp])
                            T(lambda h, pp=pp, wb=wb, gl=gl: h.matmul(pp[:, 0:C], wb[:, gl + 16, :], U2[:, gl + 16, :], start=False, stop=True), [Bwb, BU2[gl + 16]], [Bpp])
                        V(lambda h, gl=gl: h.tensor_tensor(c1[:], pr[:, 0:C], cosT[:, gl, :], ALU.mult), [Bpr, BcosT], [Bc1])
                        V(lambda h, gl=gl: h.tensor_tensor(c2[:], pi_[:, 0:C], sinT[:, gl, :], ALU.mult), [Bpi, BsinT], [Bc2])
                        G(lambda h: h.tensor_tensor(mr[:], c1[:], c2[:], ALU.add), [Bc1, Bc2], [Bmr])
                        V(lambda h, gl=gl, c3=c3, pi_=pi_: h.tensor_tensor(c3[:], pi_[:, 0:C], cosT[:, gl, :], ALU.mult), [Bpi, BcosT], [Bc3])
                        V(lambda h, gl=gl, c4=c4, pr=pr: h.tensor_tensor(c4[:], pr[:, 0:C], sinT[:, gl, :], ALU.mult), [Bpr, BsinT], [Bc4])
                        G(lambda h, mi=mi, c3=c3, c4=c4: h.tensor_tensor(mi[:], c3[:], c4[:], ALU.subtract), [Bc3, Bc4], [Bmi])
                        V(lambda h, gl=gl: h.tensor_tensor_scan(zr[:], rho[:, gl:gl + 1].to_broadcast([128, C]), mr[:], 0.0, ALU.mult, ALU.add), [Brho, Bmr], [Bzr])
                        V(lambda h, gl=gl: h.tensor_tensor_scan(zi[:], rho[:, gl:gl + 1].to_broadcast([128, C]), mi[:], 0.0, ALU.mult, ALU.add), [Brho, Bmi], [Bzi])
                        V(lambda h, gl=gl: h.tensor_tensor(c5[:, 0:C - 1], zr[:, 0:C - 1], cosT[:, gl, 0:C - 1], ALU.mult), [Bzr, BcosT], [Bc5])
                        V(lambda h, gl=gl: h.tensor_tensor(c6[:, 0:C - 1], zi[:, 0:C - 1], sinT[:, gl, 0:C - 1], ALU.mult), [Bzi, BsinT], [Bc6])
                        G(lambda h, gl=gl: h.tensor_tensor(SPr[:, gl, 1:C], c5[:, 0:C - 1], c6[:, 0:C - 1], ALU.subtract), [Bc5, Bc6], [BSPr[gl]])
                        V(lambda h, gl=gl, c3=c3, zr=zr: h.tensor_tensor(c3[:, 0:C - 1], zr[:, 0:C - 1], sinT[:, gl, 0:C - 1], ALU.mult), [Bzr, BsinT], [Bc3])
                        V(lambda h, gl=gl, c4=c4, zi=zi: h.tensor_tensor(c4[:, 0:C - 1], zi[:, 0:C - 1], cosT[:, gl, 0:C - 1], ALU.mult), [Bzi, BcosT], [Bc4])
                        G(lambda h, gl=gl, c3=c3, c4=c4: h.tensor_tensor(SPi[:, gl, 1:C], c3[:, 0:C - 1], c4[:, 0:C - 1], ALU.add), [Bc3, Bc4], [BSPi[gl]])
                    ceir_f = ceir.rearrange("p g j o -> p g (j o)"); ceii_f = ceii.rearrange("p g j o -> p g (j o)")
                    def s3_a(g):
                        hf = g // 16; gl = g % 16; qs = slice(hf * 64, hf * 64 + 64)
                        py, Bpy = PF[g % 4]
                        (gsq, Bgsq), (gw, Bgw), (gs_, Bgs) = gset[g % 4]
                        T(lambda h: h.matmul(py[:, 0:C], toep[:, g, :], U2[:, g, :], start=True, stop=False), [Btoep, BU2[g]], [Bpy])
                        T(lambda h: h.matmul(py[:, 0:C], ceir_f[qs, gl, :], SPr[qs, gl, :], start=False, stop=False), [Bceir, BSPr[gl]], [Bpy])
                        T(lambda h: h.matmul(py[:, 0:C], ceii_f[qs, gl, :], SPi[qs, gl, :], start=False, stop=True), [Bceii, BSPi[gl]], [Bpy])
                        A(lambda h: h.activation(gsq[:], py[:, 0:C], AF.Square), [Bpy], [Bgsq])
                        V(lambda h: h.tensor_scalar(gw[:], gsq[:], 0.044715, 1.0, ALU.mult, ALU.add), [Bgsq], [Bgw])
                        V(lambda h: h.tensor_tensor(gw[:], gw[:], py[:, 0:C], ALU.mult), [Bgw, Bpy], [Bgw])

                    def s3_b(g):
                        py, Bpy = PF[g % 4]
                        (gsq, Bgsq), (gw, Bgw), (gs_, Bgs) = gset[g % 4]
                        A(lambda h: h.activation(gs_[:], gw[:], AF.Sigmoid, scale=1.5957691216), [Bgw], [Bgs])
                        V(lambda h: h.tensor_tensor(G2[:, g, :], gs_[:], py[:, 0:C], ALU.mult), [Bgs, Bpy], [BG2[g]] + ([BS1] if g == 0 else []))

                    for g in range(32):
                        s3_a(g)
                        if g >= 2:
                            s3_b(g - 2)
                    s3_b(30)
                    s3_b(31)
                    for sp in range(NSP):
                        for g0 in range(0, 32, 8):
                            ph, Bph = PH[(g0 // 8) % 2]
                            for gg in range(8):
                                T(lambda h, ph=ph, gg=gg, g0=g0, sp=sp: h.transpose(ph[0:CS, gg * 128:(gg + 1) * 128], G2[:, g0 + gg, sp * CS:(sp + 1) * CS], identb), [BG2[g0 + gg], Bcstb], [Bph])
                            A(lambda h, ph=ph, g0=g0, sp=sp: h.copy(y_tm[0:CS, sp, :, g0 * 16:(g0 + 8) * 16].rearrange("p j (g o) -> p j g o", o=16),
                                                                    ph[0:CS, :].rearrange("p (g j o) -> p j g o", g=8, j=8)), [Bph], [BQ1] + (BU2 if (sp == 0 and g0 == 0) else []))
                    for sp in range(NSP):
                        for tg in range(4):
                            ph, Bph = PH[tg % 2]
                            for j in range(8):
                                T(lambda h, ph=ph, j=j, sp=sp, tg=tg: h.transpose(ph[:, j * CS:(j + 1) * CS], y_tm[0:CS, sp, j, tg * 128:(tg + 1) * 128], cstb[0:CS, 0, 0:CS]), [BQ1, Bcstb], [Bph])
                            V(lambda h, ph=ph, sp=sp, tg=tg: h.tensor_copy(yT[:, tg, sp * SPT:(sp + 1) * SPT].rearrange("p (c j) -> p c j", j=8),
                                                                           ph[:, 0:8 * CS].rearrange("p (j c) -> p c j", j=8)), [Bph], [BQ2] + ((BSPr + BSPi) if (sp == 0 and tg == 0) else []))
                    dump(f"yT{b}", yT, BQ2, [128, 4, L])
                    for bk in range(NBK):
                        nblk = BW // 128
                        pss, Bpss = PF[4]
                        for co in range(4):
                            pf, Bpf = PF[co % 2]
                            for tg in range(4):
                                T(lambda h, pf=pf, tg=tg, co=co, bk=bk: h.matmul(pf[:, 0:BW], wglu[:, tg, co * 128:(co + 1) * 128], yT[:, tg, bk * BW:(bk + 1) * BW], start=(tg == 0), stop=(tg == 3)),
                                  [Bwglu, BQ2], [Bpf])
                            A(lambda h, pf=pf, co=co: h.activation(gT[:, 0:BW], pf[:, 0:BW], AF.Sigmoid, bias=vecs[:, 24 + co:25 + co]), [Bpf, Bvecs], [BgT])
                            V(lambda h, co=co, bk=bk: h.tensor_tensor(ssmT[:, co, bk * BW:(bk + 1) * BW], yT[:, co, bk * BW:(bk + 1) * BW], gT[:, 0:BW], ALU.mult), [BQ2, BgT], [BQ3])
                            sqc, Bsqc = sq4[0]
                            A(lambda h, co=co, bk=bk, sqc=sqc: h.activation(sqc[:, co, 0:BW], ssmT[:, co, bk * BW:(bk + 1) * BW], AF.Square), [BQ3], [Bsqc])
                        for tb in range(nblk):
                            for co in range(4):
                                T(lambda h, tb=tb, co=co, sqc=sqc: h.matmul(pss[:, tb:tb + 1], sqc[:, co, tb * 128:(tb + 1) * 128], onecol[:], start=(co == 0), stop=(co == 3)), [Bsqc, Bonecol], [Bpss])
                        V(lambda h, bk=bk, nblk=nblk: h.tensor_copy(stat[:, 2, bk * nblk:(bk + 1) * nblk], pss[:, 0:nblk]), [Bpss], [Bstat])
                    dump(f"ssmT{b}", ssmT, BQ3, [128, 4, L])
                    fw.barrier()

                with ExitStack() as sk:
                    xb = [sbt(sk, f"xbk{i}", [128, 1024]) for i in range(4)]
                    hb = [sbt(sk, f"hb{i}", [128, 1024]) for i in range(2)]
                    rstd_from_ss(stat[:, 1, :], Bstat, rs[:, 1, :], Brs, 1.0 / 512, stat[:, 3, :], Bstat)
                    rstd_from_ss(stat[:, 2, :], Bstat, rs[:, 2, :], Brs, 1.0 / 512, stat[:, 3, :], Bstat)
                    for n in range(NT):
                        xt, Bxt = xb[n % 4]
                        D(lambda h, xt=xt, n=n: h.dma_start(out=xt[:], in_=x_d[b, n * 128:(n + 1) * 128, :]), [], [Bxt])
                        for hf in range(2):
                            pa, Bpa = PF[hf]; ps_, Bps = PF[2 + hf]
                            for ct in range(4):
                                T(lambda h, pa=pa, ct=ct, n=n, hf=hf: h.matmul(pa[:], sbT[:, ct, n * 128:(n + 1) * 128], wo[:, ct, hf * 512:(hf + 1) * 512], start=(ct == 0), stop=(ct == 3)),
                                  [BsbT, Bwo], [Bpa])
                            for ct in range(4):
                                T(lambda h, ps_=ps_, ct=ct, n=n, hf=hf: h.matmul(ps_[:], ssmT[:, ct, n * 128:(n + 1) * 128], wo[:, 4 + ct, hf * 512:(hf + 1) * 512], start=(ct == 0), stop=(ct == 3)),
                                  [BQ3, Bwo], [Bps])
                        ht, Bht = hb[n % 2]
                        for hf in range(2):
                            pa, Bpa = PF[hf]; ps_, Bps = PF[2 + hf]
                            V(lambda h, pa=pa, xt=xt, ht=ht, n=n, hf=hf: h.scalar_tensor_tensor(ht[:, hf * 512:(hf + 1) * 512], pa[:], rs[:, 1, n:n + 1], xt[:, hf * 512:(hf + 1) * 512], ALU.mult, ALU.add),
                              [Bpa, Brs, Bxt], [Bht])
                            V(lambda h, ps_=ps_, ht=ht, n=n, hf=hf: h.scalar_tensor_tensor(ht[:, hf * 512:(hf + 1) * 512], ps_[:], rs[:, 2, n:n + 1], ht[:, hf * 512:(hf + 1) * 512], ALU.mult, ALU.add),
                              [Bps, Brs, Bht], [Bht])
                        fw.dma("pool", lambda h, ht=ht, n=n: h.dma_start(out=out_d[b, n * 128:(n + 1) * 128, :], in_=ht[:]), [Bht], [BoutBlk[b * NT + n]], track=Bht)
                        dump(f"h{b}_{n}", ht[:], Bht, [128, 1024])
                    dump(f"rs{b}", rs[:], Brs, [128, 4, NT]); dump(f"stat{b}", stat[:], Bstat, [128, 4, NT])
                    fw.barrier()
            fw.barrier()

        with ExitStack() as p2:
            w1, Bw1 = sbt(p2, "w1", [128, 8, 4096], BF16)
            w2, Bw2 = sbt(p2, "w2", [128, 32, 1024], BF16)
            TBG = min(4, NT)
            NG = NB * NT // TBG
            hld = [sbt(p2, f"hld{i}", [128, 1024]) for i in range(2)]
            hrs = [sbt(p2, f"hrs{i}", [128, 1024]) for i in range(2)]
            hs, Bhs = sbt(p2, "hs", [128, 1024], BF16)
            hsTs = [sbt(p2, f"hsT{i}", [128, 8, TBG * 128], BF16) for i in range(2)]
            aT2, BaT2 = sbt(p2, "aT2", [128, 32, TBG * 128], BF16)
            rl, Brl = sbt(p2, "rl", [128, TBG * 128], BF16)
            obs = [sbt(p2, f"ob{i}", [128, 1024]) for i in range(2)]
            st2, Bst2_ = sbt(p2, "st2", [128, 3, NG * TBG])
            Bst2 = [fw.buf(f"st2_{i}") for i in range(NG * TBG)]
            NW = TBG * 128
            G(lambda h: h.memset(st2[:], 0.0), [], Bst2)

            def mlp_prep(grp):
                hsT, BhsT = hsTs[grp % 2]
                for tb in range(TBG):
                    blk = grp * TBG + tb
                    bb, n = blk // NT, blk % NT
                    ht, Bht = hld[blk % 2]
                    Bs = Bst2[blk]
                    D(lambda h: h.dma_start(out=ht[:], in_=out_d[bb, n * 128:(n + 1) * 128, :]), [BoutBlk[blk]], [Bht])
                    A(lambda h: h.activation(hs[:], ht[:], AF.Square, accum_out=st2[:, 0, blk:blk + 1]), [Bht], [Bhs, Bs])
                    rstd_from_ss(st2[:, 0, blk:blk + 1], Bs, st2[:, 1, blk:blk + 1], Bs, 1.0 / 1024, st2[:, 2, blk:blk + 1], Bs)
                    V(lambda h: h.tensor_scalar_mul(hs[:], ht[:], st2[:, 1, blk:blk + 1]), [Bht, Bs], [Bhs])
                    ph, Bph = PH[tb % 2]
                    for dc in range(8):
                        T(lambda h, dc=dc: h.transpose(ph[:, dc * 128:(dc + 1) * 128], hs[:, dc * 128:(dc + 1) * 128], identb), [Bhs, Bcstb], [Bph])
                    A(lambda h: h.copy(hsT[:, :, tb * 128:(tb + 1) * 128], ph[:].rearrange("p (c t) -> p c t", c=8)), [Bph], [BhsT])

            def mlp_main(grp):
                hsT, BhsT = hsTs[grp % 2]
                for ht_ in range(32):
                    pf, Bpf = PF[ht_ % 2]
                    for dc in range(8):
                        T(lambda h, dc=dc: h.matmul(pf[:, 0:NW], w1[:, dc, ht_ * 128:(ht_ + 1) * 128], hsT[:, dc, :], start=(dc == 0), stop=(dc == 7)), [Bw1, BhsT], [Bpf])
                    A(lambda h: h.activation(rl[:], pf[:, 0:NW], AF.Relu), [Bpf], [Brl])
                    if ht_ % 2 == 0:
                        V(lambda h: h.tensor_tensor(aT2[:, ht_, :], rl[:], rl[:], ALU.mult), [Brl], [BaT2])
                    else:
                        G(lambda h: h.tensor_tensor(aT2[:, ht_, :], rl[:], rl[:], ALU.mult), [Brl], [BaT2])
                for tb in range(TBG):
                    blk = grp * TBG + tb
                    bb, n = blk // NT, blk % NT
                    hr, Bhr = hrs[blk % 2]
                    ob, Bob = obs[blk % 2]
                    D(lambda h: h.dma_start(out=hr[:], in_=out_d[bb, n * 128:(n + 1) * 128, :]), [BoutBlk[blk]], [Bhr])
                    for hf in range(2):
                        po, Bpo = PF[2 + hf]
                        for k in range(32):
                            T(lambda h, k=k: h.matmul(po[:], aT2[:, k, tb * 128:(tb + 1) * 128], w2[:, k, hf * 512:(hf + 1) * 512], start=(k == 0), stop=(k == 31)), [BaT2, Bw2], [Bpo])
                        V(lambda h: h.tensor_tensor(ob[:, hf * 512:(hf + 1) * 512], po[:], hr[:, hf * 512:(hf + 1) * 512], ALU.add), [Bpo, Bhr], [Bob])
                    fw.dma("pool", lambda h: h.dma_start(out=out_d[bb, n * 128:(n + 1) * 128, :], in_=ob[:]), [Bob, Bhr], [BoutBlk[blk]], track=Bob)

            mlp_prep(0)
            stg = [obs[0], obs[1], hrs[0], hrs[1]]
            for q in range(32):
                wsa, Bwsa = stg[q % 4]
                D(lambda h, q=q, wsa=wsa: h.dma_start(out=wsa[:].rearrange("p (c n) -> p c n", c=8), in_=w1_d[:, q * 128:(q + 1) * 128].rearrange("(c p) n -> p c n", p=128)), [], [Bwsa])
                (G if q % 2 == 0 else V)(lambda h, q=q, wsa=wsa: h.tensor_tensor(w1[:, :, q * 128:(q + 1) * 128], wsa[:].rearrange("p (c n) -> p c n", c=8),
                                                                               vecs[:, 8:16].unsqueeze(2).to_broadcast([128, 8, 128]), ALU.mult), [Bwsa, Bvecs], [Bw1])
            for q in range(32):
                wsa, Bwsa = stg[q % 4]
                D(lambda h, q=q, wsa=wsa: h.dma_start(out=wsa[:], in_=w2_d[q * 128:(q + 1) * 128, :]), [], [Bwsa])
                (V if q % 2 == 0 else G)(lambda h, q=q, wsa=wsa: h.tensor_copy(w2[:, q, :], wsa[:]), [Bwsa], [Bw2])
            for grp in range(NG):
                if grp + 1 < NG:
                    mlp_prep(grp + 1)
                mlp_main(grp)
            e = fw.E["sp"]
            waits = fw._waits(e, [], BoutBlk + list(dbg_outs.values()))
            fw._do(e, waits, None, None)
            fw.barrier()
    return nc


def _consts():
    ident = np.eye(128, dtype=np.float32)
    t = np.arange(128)[:, None]
    s = np.arange(128)[None, :]
    maskb = np.where(s <= 127 - t, -1000.0, 0.0).astype(np.float32)
    bd = np.zeros((128, 128), np.float32)
    bd[:64, :64] = 1.0
    bd[64:, 64:] = 1.0
    iota = np.tile(np.arange(256, dtype=np.float32)[None, :], (128, 1))
    return np.stack([ident, maskb, bd]), iota


_NC_CACHE = {}


def run(inputs, L, NB, ncores, dbg=False):
    key = (L, NB, dbg)
    if key not in _NC_CACHE:
        _NC_CACHE[key] = build_nc(L, NB, dbg)
    nc = _NC_CACHE[key]
    consts, iota = _consts()
    x = np.ascontiguousarray(inputs["x"], dtype=np.float32)
    in_maps = []
    for c in range(ncores):
        m = {k: np.ascontiguousarray(v, dtype=np.float32) for k, v in inputs.items() if k != "x"}
        m["x"] = np.ascontiguousarray(x[c * NB:(c + 1) * NB])
        m["consts"] = consts
        m["iota"] = iota
        in_maps.append(m)
    res = run_bass_kernel_spmd(nc, in_maps, core_ids=list(range(ncores)))
    return res


def kernel(**inputs):
    res = run(inputs, 2048, 2, 8)
    out = np.concatenate([r["out"] for r in res.results], axis=0)
    return out.astype(np.float32)
```

```python
import math
import numpy as np
from contextlib import ExitStack
import concourse.bass as bass
import concourse.mybir as mybir
from concourse.bass_utils import run_bass_kernel_spmd

F32 = mybir.dt.float32
BF16 = mybir.dt.bfloat16
I32 = mybir.dt.int32
AF = mybir.ActivationFunctionType
ALU = mybir.AluOpType
EPS = 1e-6
TWO_PI = 2.0 * math.pi


class Buf:
    __slots__ = ("name", "w", "r", "dsem", "dcnt")

    def __init__(self, name):
        self.name = name
        self.w = None
        self.r = []
        self.dsem = None
        self.dcnt = 0


class Eng:
    def __init__(self, name, sem, same_sync=True):
        self.name = name
        self.sem = sem
        self.cnt = 0
        self.seen = {}
        self.same_sync = same_sync


class FW:
    def __init__(self, nc, stack):
        self.H = {"pe": nc.tensor, "act": nc.scalar, "dve": nc.vector, "pool": nc.gpsimd, "sp": nc.sync}
        self.nc = nc
        self.stack = stack
        self.E = {}
        for n, ss in (("pe", False), ("act", True), ("dve", True), ("pool", True), ("sp", True)):
            sem = stack.enter_context(nc.semaphore("s_" + n))
            self.E[n] = Eng(n, sem, ss)
        self.nbuf = 0
        self.dma_last = {}

    def buf(self, name=None):
        self.nbuf += 1
        return Buf(name or f"b{self.nbuf}")

    def _waits(self, e, reads, writes):
        need = {}

        def add(ev):
            if ev is None:
                return
            sem, val = ev
            if (not e.same_sync) and sem is e.sem:
                return
            k = id(sem)
            if e.seen.get(k, 0) >= val:
                return
            if k not in need or need[k][1] < val:
                need[k] = (sem, val)

        for b in reads:
            add(b.w)
        for b in writes:
            add(b.w)
            for ev in b.r:
                add(ev)
        out = list(need.values())
        for sem, val in out:
            e.seen[id(sem)] = val
        return out

    def _do(self, e, waits, fn, inc):
        h = self.H[e.name]
        for sem, val in waits:
            h.wait_ge(sem, val)
        if fn is not None:
            fn(h).then_inc(inc[0], inc[1])

    def op(self, eng, fn, reads=(), writes=()):
        e = self.E[eng]
        waits = self._waits(e, reads, writes)
        e.cnt += 1
        ev = (e.sem, e.cnt)
        self._do(e, waits, fn, (e.sem, 1))
        for b in reads:
            b.r.append(ev)
        for b in writes:
            b.w = ev
            b.r = []
        return ev

    def dma(self, eng, fn, reads=(), writes=(), track=None):
        e = self.E[eng]
        waits = self._waits(e, reads, writes)
        tb = track or (writes[0] if writes else reads[0])
        if tb.dsem is None:
            tb.dsem = self.stack.enter_context(self.nc.semaphore("d_" + tb.name))
        tb.dcnt += 16
        ev = (tb.dsem, tb.dcnt)
        self.dma_last[id(tb.dsem)] = ev
        self._do(e, waits, fn, (tb.dsem, 16))
        for b in reads:
            b.r.append(ev)
        for b in writes:
            b.w = ev
            b.r = []
        return ev

    def barrier(self):
        evs = [(x.sem, x.cnt) for x in self.E.values() if x.cnt > 0] + list(self.dma_last.values())
        for e in self.E.values():
            ws = []
            for sem, val in evs:
                if sem is e.sem and not e.same_sync:
                    continue
                if e.seen.get(id(sem), 0) < val:
                    ws.append((sem, val))
                    e.seen[id(sem)] = val
            if ws:
                self._do(e, ws, None, None)


def build_nc(L, NB, dbg=False):
    NT = L // 128
    C = L // 8
    CS = min(128, C)
    NSP = C // CS
    SPT = CS * 8
    NBK = max(1, L // 512)
    BW = min(512, L)
    nc = bass.Bass("TRN2", target_bir_lowering=False)

    def din(name, shape):
        return nc.dram_tensor(name, list(shape), F32, kind="ExternalInput").ap()

    x_d = din("x", [NB, L, 1024])
    out_d = nc.dram_tensor("out", [NB, L, 1024], F32, kind="ExternalOutput").ap()
    norm1_d = din("norm1_g", [1024]); w_in_d = din("w_in", [1024, 2048])
    qg_d = din("q_norm_g", [64]); kg_d = din("k_norm_g", [64])
    lre_d = din("ssm_lambda_re", [32, 64]); lim_d = din("ssm_lambda_im", [32, 64]); ldt_d = din("ssm_log_dt", [32])
    bre_d = din("ssm_b_re", [32, 64, 16]); bim_d = din("ssm_b_im", [32, 64, 16])
    cre_d = din("ssm_c_re", [32, 16, 64]); cim_d = din("ssm_c_im", [32, 16, 64])
    sd_d = din("ssm_d", [32, 16]); wglu_d = din("w_glu", [512, 512]); bglu_d = din("b_glu", [512])
    gao_d = din("attn_out_g", [512]); gso_d = din("ssm_out_g", [512]); wout_d = din("w_out", [1024, 1024])
    norm2_d = din("norm2_g", [1024]); w1_d = din("w_mlp_in", [1024, 4096]); w2_d = din("w_mlp_out", [4096, 1024])
    cst_d = din("consts", [3, 128, 128])
    iota_d = din("iota", [128, 256])
    dbg_outs = {}

    with ExitStack() as gst:
        fw = FW(nc, gst)

        def V(fn, r=(), w=()): return fw.op("dve", fn, r, w)
        def A(fn, r=(), w=()): return fw.op("act", fn, r, w)
        def G(fn, r=(), w=()): return fw.op("pool", fn, r, w)
        def T(fn, r=(), w=()): return fw.op("pe", fn, r, w)
        def D(fn, r=(), w=(), track=None): return fw.dma("sp", fn, r, w, track)

        uniq = [0]

        def sbt(st, name, shape, dt=F32):
            uniq[0] += 1
            name = f"{name}_{uniq[0]}"
            return st.enter_context(nc.sbuf_tensor(name, list(shape), dt)), fw.buf(name)

        def dump(name, ap, b, shape):
            if not dbg:
                return
            d = nc.dram_tensor("dbg_" + name, list(shape), ap.dtype, kind="ExternalOutput").ap()
            bo = fw.buf("dbgo_" + name)
            D(lambda h: h.dma_start(out=d, in_=ap), [b], [bo])
            dbg_outs[name] = bo

        BoutBlk = [fw.buf(f"o{i}") for i in range(NB * NT)]
        PF = []
        for i in range(5):
            t = gst.enter_context(nc.psum_tensor(f"pf{i}", [128, 512], F32)); PF.append((t, fw.buf(f"pf{i}")))
        PH = []
        for i in range(3):
            t = gst.enter_context(nc.psum_tensor(f"ph{i}", [128, 1024], BF16)); PH.append((t, fw.buf(f"ph{i}")))

        cstf, Bcstf = sbt(gst, "cstf", [128, 3, 128])
        D(lambda h: h.dma_start(out=cstf[:], in_=cst_d.rearrange("k p f -> p k f")), [], [Bcstf])
        identf = cstf[:, 0, :]
        cstb, Bcstb = sbt(gst, "cstb", [128, 3, 128], BF16)
        V(lambda h: h.tensor_copy(cstb[:], cstf[:]), [Bcstf], [Bcstb])
        identb = cstb[:, 0, :]; maskb = cstb[:, 1, :]; bd64 = cstb[:, 2, :]
        ones_f, Bones = sbt(gst, "ones_f", [128, 1])
        G(lambda h: h.memset(ones_f[:], 1.0), [], [Bones])
        onecol, Bonecol = sbt(gst, "onecol", [128, 1], BF16)
        G(lambda h: h.memset(onecol[:], 1.0), [], [Bonecol])
        epsT, Beps = sbt(gst, "epsT", [128, 1])
        G(lambda h: h.memset(epsT[:], EPS), [], [Beps])
        vecs, Bvecs = sbt(gst, "vecs", [128, 32])
        with nc.allow_non_contiguous_dma("tiny param vectors"):
            D(lambda h: h.dma_start(out=vecs[:, 0:8], in_=norm1_d.rearrange("(c p) -> p c", p=128)), [], [Bvecs])
            D(lambda h: h.dma_start(out=vecs[:, 8:16], in_=norm2_d.rearrange("(c p) -> p c", p=128)), [], [Bvecs])
            D(lambda h: h.dma_start(out=vecs[:, 16:20], in_=gao_d.rearrange("(c p) -> p c", p=128)), [], [Bvecs])
            D(lambda h: h.dma_start(out=vecs[:, 20:24], in_=gso_d.rearrange("(c p) -> p c", p=128)), [], [Bvecs])
            D(lambda h: h.dma_start(out=vecs[:, 24:28], in_=bglu_d.rearrange("(c p) -> p c", p=128)), [], [Bvecs])
            for hh in range(2):
                D(lambda h, hh=hh: h.dma_start(out=vecs[hh * 64:(hh + 1) * 64, 28:29], in_=qg_d.rearrange("(p o) -> p o", o=1)), [], [Bvecs])
                D(lambda h, hh=hh: h.dma_start(out=vecs[hh * 64:(hh + 1) * 64, 29:30], in_=kg_d.rearrange("(p o) -> p o", o=1)), [], [Bvecs])

        gqk, Bgqk = sbt(gst, "gqk", [128, 2, 64])
        D(lambda h: h.dma_start(out=gqk[:, 0, :], in_=qg_d.partition_broadcast(128)), [], [Bgqk])
        D(lambda h: h.dma_start(out=gqk[:, 1, :], in_=kg_d.partition_broadcast(128)), [], [Bgqk])

        def rstd_from_ss(ss_ap, Bss, out_ap, Bout, inv_n, tmp_ap, Btmp):
            A(lambda h: h.activation(tmp_ap, ss_ap, AF.Sqrt, bias=epsT[:], scale=inv_n), [Bss, Beps], [Btmp])
            V(lambda h: h.reciprocal(out_ap, tmp_ap), [Btmp], [Bout])

        with ExitStack() as p1:
            rho, Brho = sbt(p1, "rho", [128, 16])
            CSPEC = [("toep", [128, 32 * 128], BF16), ("wbr", [128, 32 * 128], BF16), ("wbi", [128, 32 * 128], BF16),
                     ("ceir", [128, 16 * 128], BF16), ("ceii", [128, 16 * 128], BF16), ("cosT", [128, 16 * C], F32), ("sinT", [128, 16 * C], F32)]
            scr = {n: (nc.dram_tensor("scr_" + n, shp, dt_, kind="Internal").ap(), fw.buf("scr_" + n)) for n, shp, dt_ in CSPEC}

            def alloc_consts(st):
                t = {n: sbt(st, n, shp, dt_) for n, shp, dt_ in CSPEC}
                return t
            wglu, Bwglu = sbt(p1, "wglu", [128, 4, 512], BF16)

            with ExitStack() as bs:
                _ct = alloc_consts(bs)
                toep = _ct["toep"][0][:].rearrange("p (g f) -> p g f", g=32); Btoep = _ct["toep"][1]
                wbr = _ct["wbr"][0][:].rearrange("p (g f) -> p g f", g=32); Bwbr = _ct["wbr"][1]
                wbi = _ct["wbi"][0][:].rearrange("p (g f) -> p g f", g=32); Bwbi = _ct["wbi"][1]
                ceir = _ct["ceir"][0][:].rearrange("p (g j o) -> p g j o", g=16, j=8); Bceir = _ct["ceir"][1]
                ceii = _ct["ceii"][0][:].rearrange("p (g j o) -> p g j o", g=16, j=8); Bceii = _ct["ceii"][1]
                cosT = _ct["cosT"][0][:].rearrange("p (g c) -> p g c", g=16); BcosT = _ct["cosT"][1]
                sinT = _ct["sinT"][0][:].rearrange("p (g c) -> p g c", g=16); BsinT = _ct["sinT"][1]
                def t32(name, shape):
                    return sbt(bs, name, shape)
                LR, BLR = t32("LR", [128, 16]); LI, BLI = t32("LI", [128, 16]); LDT, BLDT = t32("LDT", [128, 16])
                BR, BBR = t32("BR", [128, 16, 16]); BI, BBI = t32("BI", [128, 16, 16])
                CR, BCR = t32("CR", [128, 16, 16]); CI, BCI = t32("CI", [128, 16, 16]); CIn, BCIn = t32("CIn", [128, 16, 16])
                DCOL, BDCOL = t32("DCOL", [128, 32])
                cpad, Bcpad = t32("cpad", [128, 2, 4, 128])
                wg32, Bwg32 = t32("wg32", [128, 4, 512])
                with nc.allow_non_contiguous_dma("small transposed parameter loads"):
                    for hf in range(2):
                        qs = slice(hf * 64, hf * 64 + 64); gs = slice(hf * 16, hf * 16 + 16)
                        D(lambda h, qs=qs, gs=gs: h.dma_start(out=LR[qs, :], in_=lre_d[gs, :].rearrange("g p -> p g")), [], [BLR])
                        D(lambda h, qs=qs, gs=gs: h.dma_start(out=LI[qs, :], in_=lim_d[gs, :].rearrange("g p -> p g")), [], [BLI])
                        D(lambda h, qs=qs, gs=gs: h.dma_start(out=LDT[qs, :], in_=ldt_d[gs].partition_broadcast(64)), [], [BLDT])
                        D(lambda h, qs=qs, gs=gs: h.dma_start(out=BR[qs, :, :], in_=bre_d[gs].rearrange("g p h -> p g h")), [], [BBR])
                        D(lambda h, qs=qs, gs=gs: h.dma_start(out=BI[qs, :, :], in_=bim_d[gs].rearrange("g p h -> p g h")), [], [BBI])
                    for i in range(8):
                        D(lambda h, i=i: h.dma_start(out=DCOL[i * 16:(i + 1) * 16, :], in_=sd_d.rearrange("g h -> h g")), [], [BDCOL])
                G(lambda h: h.memset(cpad[:], 0.0), [], [Bcpad])
                for ri, cd in enumerate((cre_d, cim_d)):
                    for tg in range(4):
                        hf = tg // 2
                        D(lambda h, ri=ri, tg=tg, hf=hf, cd=cd: h.dma_start(
                            out=cpad[:, ri, tg, hf * 64:(hf + 1) * 64],
                            in_=cd[tg * 8:(tg + 1) * 8].rearrange("g o p -> (g o) p")), [], [Bcpad])
                D(lambda h: h.dma_start(out=wg32[:], in_=wglu_d.rearrange("(c p) n -> p c n", p=128)), [], [Bwg32])
                V(lambda h: h.tensor_copy(wglu[:], wg32[:]), [Bwg32], [Bwglu])
                for ri, (dst, Bdst) in enumerate(((CR, BCR), (CI, BCI))):
                    for tg in range(4):
                        hf = tg // 2
                        pf, Bpf = PF[tg % 2]
                        T(lambda h, ri=ri, tg=tg, pf=pf: h.matmul(pf[:, 0:128], cpad[:, ri, tg, :], identf, start=True, stop=True), [Bcpad, Bcstf], [Bpf])
                        qs = slice(hf * 64, hf * 64 + 64)
                        g0 = (tg % 2) * 8
                        V(lambda h, dst=dst, pf=pf, qs=qs, g0=g0: h.tensor_copy(
                            dst[qs, g0:g0 + 8, :], pf[qs, 0:128].rearrange("p (g o) -> p g o", o=16)), [Bpf], [Bdst])
                V(lambda h: h.tensor_single_scalar(CIn[:], CI[:], -1.0, ALU.mult), [BCI], [BCIn])

                cnt = [0]

                def newt(shape):
                    cnt[0] += 1
                    return t32(f"bt{cnt[0]}", shape)

                def tt(o, Bo, a, Ba, b, Bb, op):
                    V(lambda h: h.tensor_tensor(o, a, b, op), [Ba, Bb], [Bo])

                NBIG = 4 * C
                ft, Bft = t32("ft", [128, NBIG]); fti, Bfti = sbt(bs, "fti", [128, NBIG], I32); ftf, Bftf = t32("ftf", [128, NBIG])
                cu = [t32(f"cu{i}", [128, 16, 16]) for i in range(5)]

                def frac(dst, Bdst, src, Bsrc, add, n):
                    t, Bt = ft[:, 0:n], Bft; ti, Bti = fti[:, 0:n], Bfti; tf, Btf = ftf[:, 0:n], Bftf
                    V(lambda h: h.tensor_single_scalar(t, src, add, ALU.add), [Bsrc], [Bt])
                    V(lambda h: h.tensor_copy(ti, t), [Bt], [Bti])
                    V(lambda h: h.tensor_copy(tf, ti), [Bti], [Btf])
                    V(lambda h: h.tensor_tensor(dst, t, tf, ALU.subtract), [Bt, Btf], [Bdst])

                dt, Bdt = newt([128, 16]); are, Bare = newt([128, 16]); turns, Bturns = newt([128, 16]); mag, Bmag = newt([128, 16])
                A(lambda h: h.activation(dt[:], LDT[:], AF.Exp), [BLDT], [Bdt])
                tt(are[:], Bare, LR[:], BLR, dt[:], Bdt, ALU.mult)
                tt(turns[:], Bturns, LI[:], BLI, dt[:], Bdt, ALU.mult)
                V(lambda h: h.tensor_single_scalar(turns[:], turns[:], 1.0 / TWO_PI, ALU.mult), [Bturns], [Bturns])
                A(lambda h: h.activation(mag[:], are[:], AF.Exp), [Bare], [Bmag])
                A(lambda h: h.activation(rho[:], are[:], AF.Exp, scale=8.0), [Bare], [Brho])
                fs, Bfs = newt([128, 16]); fcn, Bfcn = newt([128, 16]); sA, BsA = newt([128, 16]); cA, BcA = newt([128, 16])
                frac(fs[:], Bfs, turns[:], Bturns, 0.0, 16)
                frac(fcn[:], Bfcn, turns[:], Bturns, 0.25, 16)
                A(lambda h: h.activation(sA[:], fs[:], AF.Sin, scale=TWO_PI), [Bfs], [BsA])
                A(lambda h: h.activation(cA[:], fcn[:], AF.Sin, scale=TWO_PI), [Bfcn], [BcA])
                lbr, Blbr = newt([128, 16]); lbi, Blbi = newt([128, 16])
                tt(lbr[:], Blbr, mag[:], Bmag, cA[:], BcA, ALU.mult)
                tt(lbi[:], Blbi, mag[:], Bmag, sA[:], BsA, ALU.mult)
                n2, Bn2 = newt([128, 16]); t1, Bt1 = newt([128, 16]); t2, Bt2 = newt([128, 16]); inv, Binv = newt([128, 16])
                nr, Bnr = newt([128, 16]); kr, Bkr = newt([128, 16]); ki, Bki = newt([128, 16])
                tt(t1[:], Bt1, LR[:], BLR, LR[:], BLR, ALU.mult)
                tt(t2[:], Bt2, LI[:], BLI, LI[:], BLI, ALU.mult)
                tt(n2[:], Bn2, t1[:], Bt1, t2[:], Bt2, ALU.add)
                V(lambda h: h.reciprocal(inv[:], n2[:]), [Bn2], [Binv])
                V(lambda h: h.tensor_single_scalar(nr[:], lbr[:], -1.0, ALU.add), [Blbr], [Bnr])
                tt(t1[:], Bt1, nr[:], Bnr, LR[:], BLR, ALU.mult)
                tt(t2[:], Bt2, lbi[:], Blbi, LI[:], BLI, ALU.mult)
                tt(kr[:], Bkr, t1[:], Bt1, t2[:], Bt2, ALU.add)
                tt(kr[:], Bkr, kr[:], Bkr, inv[:], Binv, ALU.mult)
                tt(t1[:], Bt1, lbi[:], Blbi, LR[:], BLR, ALU.mult)
                tt(t2[:], Bt2, nr[:], Bnr, LI[:], BLI, ALU.mult)
                tt(ki[:], Bki, t1[:], Bt1, t2[:], Bt2, ALU.subtract)
                tt(ki[:], Bki, ki[:], Bki, inv[:], Binv, ALU.mult)

                def bc(ap2):
                    return ap2.unsqueeze(2).to_broadcast([128, 16, 16])

                def cmul_b(outr, outi, Bor, Boi, sr, si, Bsr, Bsi, xr, xi, Bxr, Bxi, negate_i=False):
                    (u1, Bu1), (u2, Bu2), (u3, Bu3), (u4, Bu4), (u5, Bu5) = cu
                    V(lambda h: h.tensor_tensor(u1[:], xr, bc(sr), ALU.mult), [Bxr, Bsr], [Bu1])
                    V(lambda h: h.tensor_tensor(u2[:], xi, bc(si), ALU.mult), [Bxi, Bsi], [Bu2])
                    V(lambda h: h.tensor_tensor(outr, u1[:], u2[:], ALU.subtract), [Bu1, Bu2], [Bor])
                    V(lambda h: h.tensor_tensor(u3[:], xi, bc(sr), ALU.mult), [Bxi, Bsr], [Bu3])
                    V(lambda h: h.tensor_tensor(u4[:], xr, bc(si), ALU.mult), [Bxr, Bsi], [Bu4])
                    if negate_i:
                        V(lambda h: h.tensor_tensor(u5[:], u3[:], u4[:], ALU.add), [Bu3, Bu4], [Bu5])
                        V(lambda h: h.tensor_single_scalar(outi, u5[:], -1.0, ALU.mult), [Bu5], [Boi])
                    else:
                        V(lambda h: h.tensor_tensor(outi, u3[:], u4[:], ALU.add), [Bu3, Bu4], [Boi])

                BBr, BBBr = newt([128, 16, 16]); BBi, BBBi = newt([128, 16, 16])
                cmul_b(BBr[:], BBi[:], BBBr, BBBi, kr[:], ki[:], Bkr, Bki, BR[:], BI[:], BBR, BBI)
                PWr, BPWr = newt([128, 9, 16]); PWi, BPWi = newt([128, 9, 16])
                V(lambda h: h.memset(PWr[:, 0, :], 1.0), [], [BPWr]); V(lambda h: h.memset(PWi[:, 0, :], 0.0), [], [BPWi])
                for m in range(1, 9):
                    a1, Ba1 = newt([128, 16]); a2, Ba2 = newt([128, 16])
                    V(lambda h, m=m, a1=a1: h.tensor_tensor(a1[:], PWr[:, m - 1, :], lbr[:], ALU.mult), [BPWr, Blbr], [Ba1])
                    V(lambda h, m=m, a2=a2: h.tensor_tensor(a2[:], PWi[:, m - 1, :], lbi[:], ALU.mult), [BPWi, Blbi], [Ba2])
                    V(lambda h, m=m, a1=a1, a2=a2: h.tensor_tensor(PWr[:, m, :], a1[:], a2[:], ALU.subtract), [Ba1, Ba2], [BPWr])
                    a3, Ba3 = newt([128, 16]); a4, Ba4 = newt([128, 16])
                    V(lambda h, m=m, a3=a3: h.tensor_tensor(a3[:], PWr[:, m - 1, :], lbi[:], ALU.mult), [BPWr, Blbi], [Ba3])
                    V(lambda h, m=m, a4=a4: h.tensor_tensor(a4[:], PWi[:, m - 1, :], lbr[:], ALU.mult), [BPWi, Blbr], [Ba4])
                    V(lambda h, m=m, a3=a3, a4=a4: h.tensor_tensor(PWi[:, m, :], a3[:], a4[:], ALU.add), [Ba3, Ba4], [BPWi])
                BEr, BBEr = newt([128, 16, 15, 16]); BEi, BBEi = newt([128, 16, 15, 16])
                G(lambda h: h.memset(BEr[:], 0.0), [], [BBEr]); G(lambda h: h.memset(BEi[:], 0.0), [], [BBEi])
                for i in range(8):
                    cmul_b(BEr[:, :, i, :], BEi[:, :, i, :], BBEr, BBEi, PWr[:, 7 - i, :], PWi[:, 7 - i, :], BPWr, BPWi,
                           BBr[:], BBi[:], BBBr, BBBi)
                for j in range(8):
                    cmul_b(ceir[:, :, j, :], ceii[:, :, j, :], Bceir, Bceii, PWr[:, j + 1, :], PWi[:, j + 1, :], BPWr, BPWi,
                           CR[:], CI[:], BCR, BCI, negate_i=True)
                ph8, Bph8 = newt([128, 16]); t8, Bt8 = newt([128, 16])
                V(lambda h: h.tensor_single_scalar(t8[:], turns[:], 8.0, ALU.mult), [Bturns], [Bt8])
                frac(ph8[:], Bph8, t8[:], Bt8, 0.0, 16)
                iot, Biot = newt([128, C])
                D(lambda h: h.dma_start(out=iot[:], in_=iota_d[:, 0:C]), [], [Biot])
                TT, BTT = newt([128, 4, C]); FR, BFR = newt([128, 4 * C])
                for g4 in range(4):
                    for gl in range(4):
                        V(lambda h, gl=gl, g4=g4: h.tensor_scalar_mul(TT[:, gl, :], iot[:], ph8[:, g4 * 4 + gl:g4 * 4 + gl + 1]), [Biot, Bph8], [BTT])
                    frac(FR[:], BFR, TT[:].rearrange("p g c -> p (g c)"), BTT, 0.0, 4 * C)
                    A(lambda h, g4=g4: h.activation(sinT[:, g4 * 4:(g4 + 1) * 4, :].rearrange("p g c -> p (g c)"), FR[:], AF.Sin, scale=TWO_PI), [BFR], [BsinT])
                    frac(FR[:], BFR, TT[:].rearrange("p g c -> p (g c)"), BTT, 0.25, 4 * C)
                    A(lambda h, g4=g4: h.activation(cosT[:, g4 * 4:(g4 + 1) * 4, :].rearrange("p g c -> p (g c)"), FR[:], AF.Sin, scale=TWO_PI), [BFR], [BcosT])
                G(lambda h: h.memset(wbr, 0.0), [], [Bwbr]); G(lambda h: h.memset(wbi, 0.0), [], [Bwbi])
                for g in range(32):
                    hf = g // 16; gl = g % 16
                    qs = slice(hf * 64, hf * 64 + 64)
                    pf, Bpf = PF[g % 2]
                    BEr_f = BEr[:].rearrange("p g b h -> p g (b h)"); BEi_f = BEi[:].rearrange("p g b h -> p g (b h)")
                    if g == 0:
                        BE16r, BBE16r = sbt(bs, "BE16r", [128, 16, 240], BF16); BE16i, BBE16i = sbt(bs, "BE16i", [128, 16, 240], BF16)
                        C16r, BC16r = sbt(bs, "C16r", [128, 16, 16], BF16); C16n, BC16n = sbt(bs, "C16n", [128, 16, 16], BF16)
                        V(lambda h: h.tensor_copy(BE16r[:], BEr_f), [BBEr], [BBE16r]); V(lambda h: h.tensor_copy(BE16i[:], BEi_f), [BBEi], [BBE16i])
                        V(lambda h: h.tensor_copy(C16r[:], CR[:]), [BCR], [BC16r]); V(lambda h: h.tensor_copy(C16n[:], CIn[:]), [BCIn], [BC16n])
                    for j in range(8):
                        off = (7 - j) * 16
                        T(lambda h, pf=pf, j=j, qs=qs, gl=gl, off=off: h.matmul(pf[:, j * 16:(j + 1) * 16], BE16r[qs, gl, off:off + 128], C16r[qs, gl, :], start=True, stop=False),
                          [BBE16r, BC16r], [Bpf])
                        T(lambda h, pf=pf, j=j, qs=qs, gl=gl, off=off: h.matmul(pf[:, j * 16:(j + 1) * 16], BE16i[qs, gl, off:off + 128], C16n[qs, gl, :], start=False, stop=True),
                          [BBE16i, BC16n], [Bpf])
                    V(lambda h, pf=pf, g=g: h.scalar_tensor_tensor(toep[:, g, :], identf, DCOL[:, g:g + 1], pf[:, 0:128], ALU.mult, ALU.add),
                      [Bpf, Bcstf, BDCOL], [Btoep])
                    pg, Bpg = PF[2 + g % 2]
                    T(lambda h, pg=pg, qs=qs, gl=gl: h.matmul(pg[:, 0:64], BEr_f[qs, gl, 0:128], cstf[qs, 0, hf * 64:hf * 64 + 64], start=True, stop=True), [BBEr, Bcstf], [Bpg])
                    T(lambda h, pg=pg, qs=qs, gl=gl: h.matmul(pg[:, 64:128], BEi_f[qs, gl, 0:128], cstf[qs, 0, hf * 64:hf * 64 + 64], start=True, stop=True), [BBEi, Bcstf], [Bpg])
                    A(lambda h, pg=pg, g=g, hf=hf: h.copy(wbr[:, g, hf * 64:hf * 64 + 64], pg[:, 0:64]), [Bpg], [Bwbr])
                    A(lambda h, pg=pg, g=g, hf=hf: h.copy(wbi[:, g, hf * 64:hf * 64 + 64], pg[:, 64:128]), [Bpg], [Bwbi])
                dump("toep", toep, Btoep, [128, 32, 128]); dump("wbr", wbr, Bwbr, [128, 32, 128])
                dump("ceir", ceir, Bceir, [128, 16, 8, 16]); dump("cosT", cosT, BcosT, [128, 16, C])
                for n, shp, dt_ in CSPEC:
                    D(lambda h, n=n: h.dma_start(out=scr[n][0], in_=_ct[n][0][:]), [_ct[n][1]], [scr[n][1]])
                fw.barrier()

            SW = max(4 * L, NSP * 4096)
            Q1, BQ1 = sbt(p1, "Q1", [128, SW], BF16)
            Q2, BQ2 = sbt(p1, "Q2", [128, SW], BF16)
            Q3, BQ3 = sbt(p1, "Q3", [128, SW], BF16)
            S1, BS1 = sbt(p1, "S1", [128, SW], BF16)
            XSa, BXSa = sbt(p1, "XSa", [128, 4 * L], BF16)
            XSb, BXSb = sbt(p1, "XSb", [128, max(4 * L, 8192)], BF16)
            sbT = XSa[:].rearrange("p (t l) -> p t l", t=4); BsbT = BXSa
            xsTa = XSa[:].rearrange("p (c l) -> p c l", c=4); xsTb = XSb[:, 0:4 * L].rearrange("p (c l) -> p c l", c=4)
            BxsT2 = [BXSa, BXSb]
            wo = XSb[:, 0:8192].rearrange("p (c n) -> p c n", c=8); Bwo = BXSb

            def xsT(dc, sl):
                return (xsTa if dc < 4 else xsTb)[:, dc % 4, sl]
            stat, Bstat = sbt(p1, "stat", [128, 4, NT])
            rs, Brs = sbt(p1, "rs", [128, 4, NT])
            junk, Bjunk = sbt(p1, "junk", [128, 1024], BF16)

            for b in range(NB):
                qT = Q1[:, 0:4 * L].rearrange("p (t l) -> p t l", t=4)
                kTr = Q2[:, 0:4 * L].rearrange("p (t l) -> p t l", t=4)
                vrev = Q3[:, 0:4 * L].rearrange("p (n f) -> p n f", f=512)
                vTr = S1[:, 0:4 * L].rearrange("p (t l) -> p t l", t=4)
                u_tm = S1[:, 0:NSP * 4096].rearrange("p (s g f) -> p s g f", g=32, f=128)
                u_tmw = S1[:, 0:NSP * 4096].rearrange("p (s g j h) -> p s g j h", g=32, j=8, h=16)
                U2 = Q1[:, 0:32 * C].rearrange("p (g c) -> p g c", g=32)
                SPr = Q2[:, 0:16 * C].rearrange("p (g c) -> p g c", g=16)
                SPi = Q2[:, 16 * C:32 * C].rearrange("p (g c) -> p g c", g=16)
                G2 = S1[:, 0:32 * C].rearrange("p (g c) -> p g c", g=32)
                y_tm = Q1[:, 0:NSP * 4096].rearrange("p (s j f) -> p s j f", j=8, f=512)
                yT = Q2[:, 0:4 * L].rearrange("p (t l) -> p t l", t=4)
                ssmT = Q3[:, 0:4 * L].rearrange("p (t l) -> p t l", t=4)

                with ExitStack() as sa:
                    xb = [sbt(sa, f"xb{i}", [128, 1024]) for i in range(4)]
                    wsts = [sbt(sa, f"wst{k}", [128, 8, 128]) for k in range(2)]
                    wgrps = [sbt(sa, f"wgrp{k}", [128, 8, 512], BF16) for k in range(4)]
                    xss = [sbt(sa, f"xs{i}", [128, 1024], BF16) for i in range(2)]
                    sqfs = [sbt(sa, f"sqf{k}", [128, 512]) for k in range(2)]
                    kns = [sbt(sa, f"kn{k}", [128, 512]) for k in range(2)]
                    kn16s = [sbt(sa, f"kn16_{k}", [128, 512], BF16) for k in range(3)]
                    qkst, _ = sbt(sa, "qkst", [128, 3, 2 * NT * 8])
                    Bqk = [[fw.buf(f"qkst{b}_{i}_{r}") for r in range(3)] for i in range(2 * NT)]
                    def load_wgrp(c0):
                        wgrp, Bwgrp = wgrps[c0 // 512]
                        for hh in range(4):
                            wst, Bwst = wsts[hh % 2]
                            fw.dma("pool", lambda h, hh=hh, wst=wst: h.dma_start(out=wst[:], in_=w_in_d[:, c0 + hh * 128:c0 + (hh + 1) * 128].rearrange("(c p) n -> p c n", p=128)), [], [Bwst])
                            G(lambda h, hh=hh, wst=wst: h.tensor_tensor(wgrp[:, :, hh * 128:(hh + 1) * 128], wst[:], vecs[:, 0:8].unsqueeze(2).to_broadcast([128, 8, 128]), ALU.mult),
                              [Bwst, Bvecs], [Bwgrp])

                    G(lambda h: h.memset(stat[:], 0.0), [], [Bstat])
                    st1, _ = sbt(sa, "st1", [128, 3, NT])
                    Bst1 = [fw.buf(f"st1_{b}_{i}") for i in range(NT)]
                    G(lambda h: h.memset(st1[:], 0.0), [], Bst1)
                    for c0 in (0, 512, 1024, 1536):
                        load_wgrp(c0)
                    def a1_s1(n):
                        xt, Bxt = xb[n % 4]
                        xs, Bxs = xss[n % 2]
                        D(lambda h: h.dma_start(out=xt[:], in_=x_d[b, n * 128:(n + 1) * 128, :]), [], [Bxt])
                        A(lambda h: h.activation(junk[:], xt[:], AF.Square, accum_out=st1[:, 0, n:n + 1]), [Bxt], [Bjunk, Bst1[n]])
                        rstd_from_ss(st1[:, 0, n:n + 1], Bst1[n], st1[:, 1, n:n + 1], Bst1[n], 1.0 / 1024, st1[:, 2, n:n + 1], Bst1[n])
                        V(lambda h: h.tensor_scalar_mul(xs[:], xt[:], st1[:, 1, n:n + 1]), [Bxt, Bst1[n]], [Bxs])

                    def a1_s2(n):
                        xs, Bxs = xss[n % 2]
                        ph, Bph = PH[n % 2]
                        for dc in range(8):
                            T(lambda h, dc=dc: h.transpose(ph[:, dc * 128:(dc + 1) * 128], xs[:, dc * 128:(dc + 1) * 128], identb), [Bxs, Bcstb], [Bph])
                        A(lambda h: h.copy(xsTa[:, :, n * 128:(n + 1) * 128], ph[:, 0:512].rearrange("p (c t) -> p c t", c=4)), [Bph], [BXSa])
                        A(lambda h: h.copy(xsTb[:, :, n * 128:(n + 1) * 128], ph[:, 512:1024].rearrange("p (c t) -> p c t", c=4)), [Bph], [BXSb])

                    for n in range(NT):
                        a1_s1(n)
                        if n >= 1:
                            a1_s2(n - 1)
                    a1_s2(NT - 1)

                    qk_units = [(which, n) for which in range(2) for n in range(NT)]

                    def qk_s1(u):
                        which, n = qk_units[u]
                        wgrp, Bwgrp = wgrps[which]
                        pf, Bpf = PF[u % 3]
                        sqf, Bsqf = sqfs[u % 2]; kn, Bkn = kns[u % 2]; kn16, Bkn16 = kn16s[u % 3]
                        for dc in range(8):
                            T(lambda h, dc=dc: h.matmul(pf[:], xsT(dc, slice(n * 128, (n + 1) * 128)), wgrp[:, dc, :], start=(dc == 0), stop=(dc == 7)),
                              BxsT2 + [Bwgrp], [Bpf])
                        A(lambda h: h.activation(sqf[:], pf[:], AF.Square), [Bpf], [Bsqf])
                        c0 = (which * NT + n) * 8
                        Bq = Bqk[which * NT + n]
                        V(lambda h: h.tensor_reduce(qkst[:, 0, c0:c0 + 8], sqf[:].rearrange("p (a d) -> p a d", d=64), mybir.AxisListType.X, ALU.add), [Bsqf], [Bq[0]])
                        A(lambda h: h.activation(qkst[:, 1, c0:c0 + 8], qkst[:, 0, c0:c0 + 8], AF.Sqrt, bias=epsT[:], scale=1.0 / 64), [Bq[0], Beps], [Bq[1]])
                        V(lambda h: h.reciprocal(qkst[:, 2, c0:c0 + 8], qkst[:, 1, c0:c0 + 8]), [Bq[1]], [Bq[2]])
                        V(lambda h: h.tensor_tensor(kn16[:].rearrange("p (a d) -> p a d", d=64), pf[:].rearrange("p (a d) -> p a d", d=64),
                                                    qkst[:, 2, c0:c0 + 8].unsqueeze(2).to_broadcast([128, 8, 64]), ALU.mult), [Bpf, Bq[2]], [Bkn16])

                    def qk_s2(u):
                        which, n = qk_units[u]
                        kn16, Bkn16 = kn16s[u % 3]
                        ph, Bph = PH[u % 2]
                        for hp in range(4):
                            T(lambda h, hp=hp: h.transpose(ph[:, hp * 128:(hp + 1) * 128], kn16[:, hp * 128:(hp + 1) * 128], identb), [Bkn16, Bcstb], [Bph])
                        if which == 0:
                            A(lambda h: h.activation(qT[:, :, n * 128:(n + 1) * 128], ph[:, 0:512].rearrange("p (c t) -> p c t", c=4), AF.Copy, scale=vecs[:, 28:29]), [Bph, Bvecs], [BQ1])
                        else:
                            hi = L - n * 128 - 1
                            lo = L - (n + 1) * 128 - 1
                            A(lambda h: h.activation(kTr[:, :, hi:(lo if lo >= 0 else None):-1], ph[:, 0:512].rearrange("p (c t) -> p c t", c=4), AF.Copy, scale=vecs[:, 29:30]), [Bph, Bvecs], [BQ2])

                    for u in range(len(qk_units)):
                        qk_s1(u)
                        if u >= 2:
                            qk_s2(u - 2)
                    qk_s2(len(qk_units) - 2)
                    qk_s2(len(qk_units) - 1)
                    wgrp, Bwgrp = wgrps[2]
                    for hp in range(4):
                        for bk in range(NBK):
                            pf, Bpf = PF[(hp * NBK + bk) % 2]
                            for dc in range(8):
                                T(lambda h, pf=pf, dc=dc, hp=hp, bk=bk: h.matmul(pf[:, 0:BW], wgrp[:, dc, hp * 128:(hp + 1) * 128], xsT(dc, slice(bk * BW, (bk + 1) * BW)),
                                                                              start=(dc == 0), stop=(dc == 7)), [Bwgrp] + BxsT2, [Bpf])
                            lo = L - (bk + 1) * BW
                            stop = lo - 1 if lo > 0 else None
                            A(lambda h, pf=pf, hp=hp, lo=lo, stop=stop: h.copy(vTr[:, hp, lo + BW - 1:stop:-1], pf[:, 0:BW]), [Bpf], [BS1])
                    for sbk in range(NT):
                        ph, Bph = PH[sbk % 2]
                        for hp in range(4):
                            T(lambda h, ph=ph, hp=hp, sbk=sbk: h.transpose(ph[:, hp * 128:(hp + 1) * 128], vTr[:, hp, sbk * 128:(sbk + 1) * 128], identb), [BS1, Bcstb], [Bph])
                        V(lambda h, ph=ph, sbk=sbk: h.tensor_copy(vrev[:, sbk, :], ph[:, 0:512]), [Bph], [BQ3])
                    wgrp, Bwgrp = wgrps[3]
                    for sp in range(NSP):
                        for j in range(8):
                            pf, Bpf = PF[(sp * 8 + j) % 2]
                            for dc in range(8):
                                T(lambda h, pf=pf, dc=dc, sp=sp, j=j: h.matmul(pf[0:CS, :], xsT(dc, slice(sp * SPT + j, (sp + 1) * SPT, 8)), wgrp[:, dc, :],
                                                                            start=(dc == 0), stop=(dc == 7)), BxsT2 + [Bwgrp], [Bpf])
                            A(lambda h, pf=pf, sp=sp, j=j: h.copy(u_tmw[0:CS, sp, :, j, :], pf[0:CS, :].rearrange("p (g h) -> p g h", h=16)), [Bpf], [BS1])
                    dump(f"qT{b}", qT, BQ1, [128, 4, L]); dump(f"kTr{b}", kTr, BQ2, [128, 4, L]); dump(f"vrev{b}", vrev, BQ3, [128, NT, 512])
                    dump(f"utm{b}", S1[0:CS, 0:NSP * 4096], BS1, [CS, NSP * 4096])
                    fw.barrier()

                with ExitStack() as sj:
                    gams = [sbt(sj, f"gam{k}", [128, L]) for k in range(2)]
                    Pcs = [sbt(sj, f"Pc{k}", [128, L + 1]) for k in range(2)]
                    a16s = [sbt(sj, f"a16_{k}", [128, L], BF16) for k in range(4)]
                    aTs = [sbt(sj, f"aT_{k}", [128, NT, 128], BF16) for k in range(2)]
                    sbs, Bsbs = sbt(sj, "sbs", [128, 512], BF16)
                    wst2s = [sbt(sj, f"wst2_{k}", [128, 8, 128]) for k in range(2)]
                    for q4 in range(8):
                        wst2, Bwst2 = wst2s[q4 % 2]
                        D(lambda h, q4=q4: h.dma_start(out=wst2[:], in_=wout_d[:, q4 * 128:(q4 + 1) * 128].rearrange("(c p) n -> p c n", p=128)), [], [Bwst2])
                        G(lambda h, q4=q4: h.tensor_tensor(wo[:, :, q4 * 128:(q4 + 1) * 128], wst2[:], vecs[:, 16:24].unsqueeze(2).to_broadcast([128, 8, 128]), ALU.mult),
                          [Bwst2, Bvecs], [Bwo])
                    for Pc_, BPc_ in Pcs:
                        V(lambda h, Pc_=Pc_: h.memset(Pc_[:, 0:1], 1.0), [], [BPc_])
                    osb, Bosb = PF[4]
                    units = [(i, hd) for i in range(NT) for hd in range(8)]

                    def stage1(n):
                        i, hd = units[n]
                        a16, Ba16 = a16s[n % 4]
                        gam, Bgam = gams[n % 2]
                        Pc, BPc = Pcs[n % 2]
                        S = (i + 1) * 128
                        k0 = L - S
                        hp = hd // 2; hs = slice((hd % 2) * 64, (hd % 2) * 64 + 64)
                        npc = (S + 511) // 512
                        for pc in range(npc):
                            w = min(512, S - pc * 512)
                            pf, Bpf = PF[pc % 4]
                            if pc == 0:
                                T(lambda h, pf=pf: h.matmul(pf[:, 0:128], identb, maskb, start=True, stop=False), [Bcstb], [Bpf])
                                T(lambda h, pf=pf: h.matmul(pf[:, 0:128], qT[hs, hp, i * 128:(i + 1) * 128], kTr[hs, hp, k0:k0 + 128], start=False, stop=True),
                                  [BQ1, BQ2], [Bpf])
                                if w > 128:
                                    T(lambda h, pf=pf, w=w: h.matmul(pf[:, 128:w], qT[hs, hp, i * 128:(i + 1) * 128], kTr[hs, hp, k0 + 128:k0 + w], start=True, stop=True),
                                      [BQ1, BQ2], [Bpf])
                            else:
                                T(lambda h, pf=pf, w=w, pc=pc: h.matmul(pf[:, 0:w], qT[hs, hp, i * 128:(i + 1) * 128], kTr[hs, hp, k0 + pc * 512:k0 + pc * 512 + w], start=True, stop=True),
                                  [BQ1, BQ2], [Bpf])
                            A(lambda h, pf=pf, pc=pc, w=w: h.activation(gam[:, pc * 512:pc * 512 + w], pf[:, 0:w], AF.Sigmoid, scale=-0.125), [Bpf], [Bgam])
                        V(lambda h: h.tensor_tensor_scan(Pc[:, 1:S + 1], gam[:, 0:S], gam[:, 0:S], 1.0, ALU.mult, ALU.min), [Bgam], [BPc])
                        V(lambda h: h.tensor_tensor(a16[:, 0:S], Pc[:, 0:S], Pc[:, 1:S + 1], ALU.subtract), [BPc], [Ba16])

                    def stage2(n):
                        i, hd = units[n]
                        a16, Ba16 = a16s[n % 4]
                        aT, BaT = aTs[n % 2]
                        for b4 in range(0, i + 1, 4):
                            nb4 = min(4, i + 1 - b4)
                            ph, Bph = PH[(b4 // 4) % 2]
                            for q in range(nb4):
                                T(lambda h, ph=ph, q=q, b4=b4: h.transpose(ph[:, q * 128:(q + 1) * 128], a16[:, (b4 + q) * 128:(b4 + q + 1) * 128], identb), [Ba16, Bcstb], [Bph])
                            A(lambda h, ph=ph, b4=b4, nb4=nb4: h.copy(aT[:, b4:b4 + nb4, :], ph[:, 0:nb4 * 128].rearrange("p (q t) -> p q t", t=128)), [Bph], [BaT])
                        for blk in range(i + 1):
                            T(lambda h, blk=blk: h.matmul(osb[:, hd * 64:(hd + 1) * 64], aT[:, blk, :], vrev[:, NT - 1 - i + blk, hd * 64:(hd + 1) * 64],
                                                          start=(blk == 0), stop=(blk == i)), [BaT, BQ3], [Bosb])
                        if hd == 7:
                            A(lambda h: h.copy(sbs[:], osb[:]), [Bosb], [Bsbs])
                            A(lambda h: h.activation(junk[:, 0:512], osb[:], AF.Square, accum_out=stat[:, 1, i:i + 1]), [Bosb], [Bjunk, Bstat])
                            ph, Bph = PH[2]
                            for hp in range(4):
                                T(lambda h, ph=ph, hp=hp: h.transpose(ph[:, hp * 128:(hp + 1) * 128], sbs[:, hp * 128:(hp + 1) * 128], identb), [Bsbs, Bcstb], [Bph])
                            V(lambda h, ph=ph: h.tensor_copy(sbT[:, :, i * 128:(i + 1) * 128], ph[:, 0:512].rearrange("p (c t) -> p c t", c=4)), [Bph], [BsbT])

                    for n in range(len(units)):
                        stage1(n)
                        if n >= 3:
                            stage2(n - 3)
                    stage2(len(units) - 3)
                    stage2(len(units) - 2)
                    stage2(len(units) - 1)
                    dump(f"sbT{b}", sbT, BsbT, [128, 4, L])
                    fw.barrier()

                with ExitStack() as ss_:
                    _ct = alloc_consts(ss_)
                    for n, shp, dt_ in CSPEC:
                        D(lambda h, n=n, _ct=_ct: h.dma_start(out=_ct[n][0][:], in_=scr[n][0]), [scr[n][1]], [_ct[n][1]])
                    toep = _ct["toep"][0][:].rearrange("p (g f) -> p g f", g=32); Btoep = _ct["toep"][1]
                    wbr = _ct["wbr"][0][:].rearrange("p (g f) -> p g f", g=32); Bwbr = _ct["wbr"][1]
                    wbi = _ct["wbi"][0][:].rearrange("p (g f) -> p g f", g=32); Bwbi = _ct["wbi"][1]
                    ceir = _ct["ceir"][0][:].rearrange("p (g j o) -> p g j o", g=16, j=8); Bceir = _ct["ceir"][1]
                    ceii = _ct["ceii"][0][:].rearrange("p (g j o) -> p g j o", g=16, j=8); Bceii = _ct["ceii"][1]
                    cosT = _ct["cosT"][0][:].rearrange("p (g c) -> p g c", g=16); BcosT = _ct["cosT"][1]
                    sinT = _ct["sinT"][0][:].rearrange("p (g c) -> p g c", g=16); BsinT = _ct["sinT"][1]
                    BU2 = [fw.buf(f"U2_{g}") for g in range(32)]
                    BG2 = [fw.buf(f"G2_{g}") for g in range(32)]
                    BSPr = [fw.buf(f"SPr_{g}") for g in range(16)]; BSPi = [fw.buf(f"SPi_{g}") for g in range(16)]
                    tset = [[sbt(ss_, f"st{k}_{q}", [128, C]) for q in range(8)] for k in range(2)]
                    gset = [[sbt(ss_, f"gt{k}_{q}", [128, C]) for q in range(3)] for k in range(4)]
                    gT, BgT = sbt(ss_, "gT", [128, 512], BF16); sq4 = [sbt(ss_, f"sq4_{k}", [128, 4, 512], BF16) for k in range(1)]
                    for g0 in range(0, 32, 4):
                        ph, Bph = PH[(g0 // 4) % 2]
                        for gg in range(4):
                            g = g0 + gg
                            for sp in range(NSP):
                                T(lambda h, ph=ph, gg=gg, g=g, sp=sp: h.transpose(ph[:, gg * C + sp * CS:gg * C + (sp + 1) * CS], u_tm[0:CS, sp, g, :], cstb[0:CS, 0, 0:CS]),
                                  [BS1, Bcstb], [Bph])
                        A(lambda h, ph=ph, g0=g0: h.copy(U2[:, g0:g0 + 4, :], ph[:, 0:4 * C].rearrange("p (g c) -> p g c", g=4)), [Bph], BU2[g0:g0 + 4] + ([BQ1] if g0 == 0 else []))
                    V(lambda h: h.memset(SPr[:, :, 0:1], 0.0), [], [BQ2] + BSPr); V(lambda h: h.memset(SPi[:, :, 0:1], 0.0), [], [BQ2] + BSPi)
                    for gl in range(16):
                        pr, Bpr = PF[2 * (gl % 2)]; pi_, Bpi = PF[2 * (gl % 2) + 1]
                        (mr, Bmr), (mi, Bmi), (c1, Bc1), (c2, Bc2), (zr, Bzr), (zi, Bzi), (c3, Bc3), (c4, Bc4) = tset[gl % 2]
                        (c5, Bc5), (c6, Bc6) = (c1, Bc1), (c2, Bc2)
                        for (pp, Bpp, wb, Bwb) in ((pr, Bpr, wbr, Bwbr), (pi_, Bpi, wbi, Bwbi)):
                            T(lambda h, pp=pp, wb=wb, gl=gl: h.matmul(pp[:, 0:C], wb[:, gl, :], U2[:, gl, :], start=True, stop=False), [Bwb, BU2[gl]], [Bpp])
                            T(lambda h, pp=pp, wb=wb, gl=gl: h.matmul(pp[:, 0:C], wb[:, gl + 16, :], U2[:, gl + 16, :], start=False, stop=True), [Bwb, BU2[gl + 16]], [Bpp])
                        V(lambda h, gl=gl: h.tensor_tensor(c1[:], pr[:, 0:C], cosT[:, gl, :], ALU.mult), [Bpr, BcosT], [Bc1])
                        V(lambda h, gl=gl: h.tensor_tensor(c2[:], pi_[:, 0:C], sinT[:, gl, :], ALU.mult), [Bpi, BsinT], [Bc2])
                        G(lambda h: h.tensor_tensor(mr[:], c1[:], c2[:], ALU.add), [Bc1, Bc2], [Bmr])
                        V(lambda h, gl=gl, c3=c3, pi_=pi_: h.tensor_tensor(c3[:], pi_[:, 0:C], cosT[:, gl, :], ALU.mult), [Bpi, BcosT], [Bc3])
                        V(lambda h, gl=gl, c4=c4, pr=pr: h.tensor_tensor(c4[:], pr[:, 0:C], sinT[:, gl, :], ALU.mult), [Bpr, BsinT], [Bc4])
                        G(lambda h, mi=mi, c3=c3, c4=c4: h.tensor_tensor(mi[:], c3[:], c4[:], ALU.subtract), [Bc3, Bc4], [Bmi])
                        V(lambda h, gl=gl: h.tensor_tensor_scan(zr[:], rho[:, gl:gl + 1].to_broadcast([128, C]), mr[:], 0.0, ALU.mult, ALU.add), [Brho, Bmr], [Bzr])
                        V(lambda h, gl=gl: h.tensor_tensor_scan(zi[:], rho[:, gl:gl + 1].to_broadcast([128, C]), mi[:], 0.0, ALU.mult, ALU.add), [Brho, Bmi], [Bzi])
                        V(lambda h, gl=gl: h.tensor_tensor(c5[:, 0:C - 1], zr[:, 0:C - 1], cosT[:, gl, 0:C - 1], ALU.mult), [Bzr, BcosT], [Bc5])
                        V(lambda h, gl=gl: h.tensor_tensor(c6[:, 0:C - 1], zi[:, 0:C - 1], sinT[:, gl, 0:C - 1], ALU.mult), [Bzi, BsinT], [Bc6])
                        G(lambda h, gl=gl: h.tensor_tensor(SPr[:, gl, 1:C], c5[:, 0:C - 1], c6[:, 0:C - 1], ALU.subtract), [Bc5, Bc6], [BSPr[gl]])
                        V(lambda h, gl=gl, c3=c3, zr=zr: h.tensor_tensor(c3[:, 0:C - 1], zr[:, 0:C - 1], sinT[:, gl, 0:C - 1], ALU.mult), [Bzr, BsinT], [Bc3])
                        V(lambda h, gl=gl, c4=c4, zi=zi: h.tensor_tensor(c4[:, 0:C - 1], zi[:, 0:C - 1], cosT[:, gl, 0:C - 1], ALU.mult), [Bzi, BcosT], [Bc4])
                        G(lambda h, gl=gl, c3=c3, c4=c4: h.tensor_tensor(SPi[:, gl, 1:C], c3[:, 0:C - 1], c4[:, 0:C - 1], ALU.add), [Bc3, Bc4], [BSPi[gl]])
                    ceir_f = ceir.rearrange("p g j o -> p g (j o)"); ceii_f = ceii.rearrange("p g j o -> p g (j o)")
                    def s3_a(g):
                        hf = g // 16; gl = g % 16; qs = slice(hf * 64, hf * 64 + 64)
                        py, Bpy = PF[g % 4]
                        (gsq, Bgsq), (gw, Bgw), (gs_, Bgs) = gset[g % 4]
                        T(lambda h: h.matmul(py[:, 0:C], toep[:, g, :], U2[:, g, :], start=True, stop=False), [Btoep, BU2[g]], [Bpy])
                        T(lambda h: h.matmul(py[:, 0:C], ceir_f[qs, gl, :], SPr[qs, gl, :], start=False, stop=False), [Bceir, BSPr[gl]], [Bpy])
                        T(lambda h: h.matmul(py[:, 0:C], ceii_f[qs, gl, :], SPi[qs, gl, :], start=False, stop=True), [Bceii, BSPi[gl]], [Bpy])
                        A(lambda h: h.activation(gsq[:], py[:, 0:C], AF.Square), [Bpy], [Bgsq])
                        V(lambda h: h.tensor_scalar(gw[:], gsq[:], 0.044715, 1.0, ALU.mult, ALU.add), [Bgsq], [Bgw])
                        V(lambda h: h.tensor_tensor(gw[:], gw[:], py[:, 0:C], ALU.mult), [Bgw, Bpy], [Bgw])

                    def s3_b(g):
                        py, Bpy = PF[g % 4]
                        (gsq, Bgsq), (gw, Bgw), (gs_, Bgs) = gset[g % 4]
                        A(lambda h: h.activation(gs_[:], gw[:], AF.Sigmoid, scale=1.5957691216), [Bgw], [Bgs])
                        V(lambda h: h.tensor_tensor(G2[:, g, :], gs_[:], py[:, 0:C], ALU.mult), [Bgs, Bpy], [BG2[g]] + ([BS1] if g == 0 else []))

                    for g in range(32):
                        s3_a(g)
                        if g >= 2:
                            s3_b(g - 2)
                    s3_b(30)
                    s3_b(31)
                    for sp in range(NSP):
                        for g0 in range(0, 32, 8):
                            ph, Bph = PH[(g0 // 8) % 2]
                            for gg in range(8):
                                T(lambda h, ph=ph, gg=gg, g0=g0, sp=sp: h.transpose(ph[0:CS, gg * 128:(gg + 1) * 128], G2[:, g0 + gg, sp * CS:(sp + 1) * CS], identb), [BG2[g0 + gg], Bcstb], [Bph])
                            A(lambda h, ph=ph, g0=g0, sp=sp: h.copy(y_tm[0:CS, sp, :, g0 * 16:(g0 + 8) * 16].rearrange("p j (g o) -> p j g o", o=16),
                                                                    ph[0:CS, :].rearrange("p (g j o) -> p j g o", g=8, j=8)), [Bph], [BQ1] + (BU2 if (sp == 0 and g0 == 0) else []))
                    for sp in range(NSP):
                        for tg in range(4):
                            ph, Bph = PH[tg % 2]
                            for j in range(8):
                                T(lambda h, ph=ph, j=j, sp=sp, tg=tg: h.transpose(ph[:, j * CS:(j + 1) * CS], y_tm[0:CS, sp, j, tg * 128:(tg + 1) * 128], cstb[0:CS, 0, 0:CS]), [BQ1, Bcstb], [Bph])
                            V(lambda h, ph=ph, sp=sp, tg=tg: h.tensor_copy(yT[:, tg, sp * SPT:(sp + 1) * SPT].rearrange("p (c j) -> p c j", j=8),
                                                                           ph[:, 0:8 * CS].rearrange("p (j c) -> p c j", j=8)), [Bph], [BQ2] + ((BSPr + BSPi) if (sp == 0 and tg == 0) else []))
                    dump(f"yT{b}", yT, BQ2, [128, 4, L])
                    for bk in range(NBK):
                        nblk = BW // 128
                        pss, Bpss = PF[4]
                        for co in range(4):
                            pf, Bpf = PF[co % 2]
                            for tg in range(4):
                                T(lambda h, pf=pf, tg=tg, co=co, bk=bk: h.matmul(pf[:, 0:BW], wglu[:, tg, co * 128:(co + 1) * 128], yT[:, tg, bk * BW:(bk + 1) * BW], start=(tg == 0), stop=(tg == 3)),
                                  [Bwglu, BQ2], [Bpf])
                            A(lambda h, pf=pf, co=co: h.activation(gT[:, 0:BW], pf[:, 0:BW], AF.Sigmoid, bias=vecs[:, 24 + co:25 + co]), [Bpf, Bvecs], [BgT])
                            V(lambda h, co=co, bk=bk: h.tensor_tensor(ssmT[:, co, bk * BW:(bk + 1) * BW], yT[:, co, bk * BW:(bk + 1) * BW], gT[:, 0:BW], ALU.mult), [BQ2, BgT], [BQ3])
                            sqc, Bsqc = sq4[0]
                            A(lambda h, co=co, bk=bk, sqc=sqc: h.activation(sqc[:, co, 0:BW], ssmT[:, co, bk * BW:(bk + 1) * BW], AF.Square), [BQ3], [Bsqc])
                        for tb in range(nblk):
                            for co in range(4):
                                T(lambda h, tb=tb, co=co, sqc=sqc: h.matmul(pss[:, tb:tb + 1], sqc[:, co, tb * 128:(tb + 1) * 128], onecol[:], start=(co == 0), stop=(co == 3)), [Bsqc, Bonecol], [Bpss])
                        V(lambda h, bk=bk, nblk=nblk: h.tensor_copy(stat[:, 2, bk * nblk:(bk + 1) * nblk], pss[:, 0:nblk]), [Bpss], [Bstat])
                    dump(f"ssmT{b}", ssmT, BQ3, [128, 4, L])
                    fw.barrier()

                with ExitStack() as sk:
                    xb = [sbt(sk, f"xbk{i}", [128, 1024]) for i in range(4)]
                    hb = [sbt(sk, f"hb{i}", [128, 1024]) for i in range(2)]
                    rstd_from_ss(stat[:, 1, :], Bstat, rs[:, 1, :], Brs, 1.0 / 512, stat[:, 3, :], Bstat)
                    rstd_from_ss(stat[:, 2, :], Bstat, rs[:, 2, :], Brs, 1.0 / 512, stat[:, 3, :], Bstat)
                    for n in range(NT):
                        xt, Bxt = xb[n % 4]
                        D(lambda h, xt=xt, n=n: h.dma_start(out=xt[:], in_=x_d[b, n * 128:(n + 1) * 128, :]), [], [Bxt])
                        for hf in range(2):
                            pa, Bpa = PF[hf]; ps_, Bps = PF[2 + hf]
                            for ct in range(4):
                                T(lambda h, pa=pa, ct=ct, n=n, hf=hf: h.matmul(pa[:], sbT[:, ct, n * 128:(n + 1) * 128], wo[:, ct, hf * 512:(hf + 1) * 512], start=(ct == 0), stop=(ct == 3)),
                                  [BsbT, Bwo], [Bpa])
                            for ct in range(4):
                                T(lambda h, ps_=ps_, ct=ct, n=n, hf=hf: h.matmul(ps_[:], ssmT[:, ct, n * 128:(n + 1) * 128], wo[:, 4 + ct, hf * 512:(hf + 1) * 512], start=(ct == 0), stop=(ct == 3)),
                                  [BQ3, Bwo], [Bps])
                        ht, Bht = hb[n % 2]
                        for hf in range(2):
                            pa, Bpa = PF[hf]; ps_, Bps = PF[2 + hf]
                            V(lambda h, pa=pa, xt=xt, ht=ht, n=n, hf=hf: h.scalar_tensor_tensor(ht[:, hf * 512:(hf + 1) * 512], pa[:], rs[:, 1, n:n + 1], xt[:, hf * 512:(hf + 1) * 512], ALU.mult, ALU.add),
                              [Bpa, Brs, Bxt], [Bht])
                            V(lambda h, ps_=ps_, ht=ht, n=n, hf=hf: h.scalar_tensor_tensor(ht[:, hf * 512:(hf + 1) * 512], ps_[:], rs[:, 2, n:n + 1], ht[:, hf * 512:(hf + 1) * 512], ALU.mult, ALU.add),
                              [Bps, Brs, Bht], [Bht])
                        fw.dma("pool", lambda h, ht=ht, n=n: h.dma_start(out=out_d[b, n * 128:(n + 1) * 128, :], in_=ht[:]), [Bht], [BoutBlk[b * NT + n]], track=Bht)
                        dump(f"h{b}_{n}", ht[:], Bht, [128, 1024])
                    dump(f"rs{b}", rs[:], Brs, [128, 4, NT]); dump(f"stat{b}", stat[:], Bstat, [128, 4, NT])
                    fw.barrier()
            fw.barrier()

        with ExitStack() as p2:
            w1, Bw1 = sbt(p2, "w1", [128, 8, 4096], BF16)
            w2, Bw2 = sbt(p2, "w2", [128, 32, 1024], BF16)
            TBG = min(4, NT)
            NG = NB * NT // TBG
            hld = [sbt(p2, f"hld{i}", [128, 1024]) for i in range(2)]
            hrs = [sbt(p2, f"hrs{i}", [128, 1024]) for i in range(2)]
            hs, Bhs = sbt(p2, "hs", [128, 1024], BF16)
            hsTs = [sbt(p2, f"hsT{i}", [128, 8, TBG * 128], BF16) for i in range(2)]
            aT2, BaT2 = sbt(p2, "aT2", [128, 32, TBG * 128], BF16)
            rl, Brl = sbt(p2, "rl", [128, TBG * 128], BF16)
            obs = [sbt(p2, f"ob{i}", [128, 1024]) for i in range(2)]
            st2, Bst2_ = sbt(p2, "st2", [128, 3, NG * TBG])
            Bst2 = [fw.buf(f"st2_{i}") for i in range(NG * TBG)]
            NW = TBG * 128
            G(lambda h: h.memset(st2[:], 0.0), [], Bst2)

            def mlp_prep(grp):
                hsT, BhsT = hsTs[grp % 2]
                for tb in range(TBG):
                    blk = grp * TBG + tb
                    bb, n = blk // NT, blk % NT
                    ht, Bht = hld[blk % 2]
                    Bs = Bst2[blk]
                    D(lambda h: h.dma_start(out=ht[:], in_=out_d[bb, n * 128:(n + 1) * 128, :]), [BoutBlk[blk]], [Bht])
                    A(lambda h: h.activation(hs[:], ht[:], AF.Square, accum_out=st2[:, 0, blk:blk + 1]), [Bht], [Bhs, Bs])
                    rstd_from_ss(st2[:, 0, blk:blk + 1], Bs, st2[:, 1, blk:blk + 1], Bs, 1.0 / 1024, st2[:, 2, blk:blk + 1], Bs)
                    V(lambda h: h.tensor_scalar_mul(hs[:], ht[:], st2[:, 1, blk:blk + 1]), [Bht, Bs], [Bhs])
                    ph, Bph = PH[tb % 2]
                    for dc in range(8):
                        T(lambda h, dc=dc: h.transpose(ph[:, dc * 128:(dc + 1) * 128], hs[:, dc * 128:(dc + 1) * 128], identb), [Bhs, Bcstb], [Bph])
                    A(lambda h: h.copy(hsT[:, :, tb * 128:(tb + 1) * 128], ph[:].rearrange("p (c t) -> p c t", c=8)), [Bph], [BhsT])

            def mlp_main(grp):
                hsT, BhsT = hsTs[grp % 2]
                for ht_ in range(32):
                    pf, Bpf = PF[ht_ % 2]
                    for dc in range(8):
                        T(lambda h, dc=dc: h.matmul(pf[:, 0:NW], w1[:, dc, ht_ * 128:(ht_ + 1) * 128], hsT[:, dc, :], start=(dc == 0), stop=(dc == 7)), [Bw1, BhsT], [Bpf])
                    A(lambda h: h.activation(rl[:], pf[:, 0:NW], AF.Relu), [Bpf], [Brl])
                    if ht_ % 2 == 0:
                        V(lambda h: h.tensor_tensor(aT2[:, ht_, :], rl[:], rl[:], ALU.mult), [Brl], [BaT2])
                    else:
                        G(lambda h: h.tensor_tensor(aT2[:, ht_, :], rl[:], rl[:], ALU.mult), [Brl], [BaT2])
                for tb in range(TBG):
                    blk = grp * TBG + tb
                    bb, n = blk // NT, blk % NT
                    hr, Bhr = hrs[blk % 2]
                    ob, Bob = obs[blk % 2]
                    D(lambda h: h.dma_start(out=hr[:], in_=out_d[bb, n * 128:(n + 1) * 128, :]), [BoutBlk[blk]], [Bhr])
                    for hf in range(2):
                        po, Bpo = PF[2 + hf]
                        for k in range(32):
                            T(lambda h, k=k: h.matmul(po[:], aT2[:, k, tb * 128:(tb + 1) * 128], w2[:, k, hf * 512:(hf + 1) * 512], start=(k == 0), stop=(k == 31)), [BaT2, Bw2], [Bpo])
                        V(lambda h: h.tensor_tensor(ob[:, hf * 512:(hf + 1) * 512], po[:], hr[:, hf * 512:(hf + 1) * 512], ALU.add), [Bpo, Bhr], [Bob])
                    fw.dma("pool", lambda h: h.dma_start(out=out_d[bb, n * 128:(n + 1) * 128, :], in_=ob[:]), [Bob, Bhr], [BoutBlk[blk]], track=Bob)

            mlp_prep(0)
            stg = [obs[0], obs[1], hrs[0], hrs[1]]
            for q in range(32):
                wsa, Bwsa = stg[q % 4]
                D(lambda h, q=q, wsa=wsa: h.dma_start(out=wsa[:].rearrange("p (c n) -> p c n", c=8), in_=w1_d[:, q * 128:(q + 1) * 128].rearrange("(c p) n -> p c n", p=128)), [], [Bwsa])
                (G if q % 2 == 0 else V)(lambda h, q=q, wsa=wsa: h.tensor_tensor(w1[:, :, q * 128:(q + 1) * 128], wsa[:].rearrange("p (c n) -> p c n", c=8),
                                                                               vecs[:, 8:16].unsqueeze(2).to_broadcast([128, 8, 128]), ALU.mult), [Bwsa, Bvecs], [Bw1])
            for q in range(32):
                wsa, Bwsa = stg[q % 4]
                D(lambda h, q=q, wsa=wsa: h.dma_start(out=wsa[:], in_=w2_d[q * 128:(q + 1) * 128, :]), [], [Bwsa])
                (V if q % 2 == 0 else G)(lambda h, q=q, wsa=wsa: h.tensor_copy(w2[:, q, :], wsa[:]), [Bwsa], [Bw2])
            for grp in range(NG):
                if grp + 1 < NG:
                    mlp_prep(grp + 1)
                mlp_main(grp)
            e = fw.E["sp"]
            waits = fw._waits(e, [], BoutBlk + list(dbg_outs.values()))
            fw._do(e, waits, None, None)
            fw.barrier()
    return nc


def _consts():
    ident = np.eye(128, dtype=np.float32)
    t = np.arange(128)[:, None]
    s = np.arange(128)[None, :]
    maskb = np.where(s <= 127 - t, -1000.0, 0.0).astype(np.float32)
    bd = np.zeros((128, 128), np.float32)
    bd[:64, :64] = 1.0
    bd[64:, 64:] = 1.0
    iota = np.tile(np.arange(256, dtype=np.float32)[None, :], (128, 1))
    return np.stack([ident, maskb, bd]), iota


_NC_CACHE = {}


def run(inputs, L, NB, ncores, dbg=False):
    key = (L, NB, dbg)
    if key not in _NC_CACHE:
        _NC_CACHE[key] = build_nc(L, NB, dbg)
    nc = _NC_CACHE[key]
    consts, iota = _consts()
    x = np.ascontiguousarray(inputs["x"], dtype=np.float32)
    in_maps = []
    for c in range(ncores):
        m = {k: np.ascontiguousarray(v, dtype=np.float32) for k, v in inputs.items() if k != "x"}
        m["x"] = np.ascontiguousarray(x[c * NB:(c + 1) * NB])
        m["consts"] = consts
        m["iota"] = iota
        in_maps.append(m)
    res = run_bass_kernel_spmd(nc, in_maps, core_ids=list(range(ncores)))
    return res


def kernel(**inputs):
    res = run(inputs, 2048, 2, 8)
    out = np.concatenate([r["out"] for r in res.results], axis=0)
    return out.astype(np.float32)
```
